# Optimizing a Trainium2 kernel written in Bass

```python
import math
import jax, jax.numpy as jnp
from jax import lax
import numpy as np

D_MODEL = 1024
BATCH = 2
SEQ = 8192
DEPTH = 4
DEC_BATCH = 128
DEC_SEQ = 4
PAST_LEN = 8192
PAGE_SIZE = 128

N_MIXERS = 2
N_SWA = (DEPTH + 1) // 2
N_GDN = DEPTH // 2
SWA_HEADS = 16
SWA_KV_HEADS = 4
SWA_HEAD_DIM = 64
SWA_GROUP = SWA_HEADS // SWA_KV_HEADS
WINDOW = 128
SWA_BLOCK = WINDOW
SWA_Q_W = SWA_HEADS * SWA_HEAD_DIM
SWA_KV_W = SWA_KV_HEADS * SWA_HEAD_DIM
SWA_IN_W = 2 * SWA_Q_W + 2 * SWA_KV_W
SWA_SCALE = SWA_HEAD_DIM ** -0.5
GDN_QK_HEADS = 8
GDN_V_HEADS = 16
GDN_HEAD_DIM = 128
GDN_K_W = GDN_QK_HEADS * GDN_HEAD_DIM
GDN_V_W = GDN_V_HEADS * GDN_HEAD_DIM
GDN_CONV = 4
GDN_CONV_CH = 2 * GDN_K_W + GDN_V_W
GDN_IN_W = GDN_CONV_CH + GDN_V_W + 2 * GDN_V_HEADS
GDN_CHUNK = 64
EPS = 1e-6

kernel_name = 'hybrid_swa_sink_gdn_adaln_step'


def rms_norm(x, g):
    xf = x.astype(jnp.float32)
    y = xf * lax.rsqrt(jnp.mean(xf * xf, axis=-1, keepdims=True) + EPS)
    return (y * g.astype(jnp.float32)).astype(x.dtype)


def l2_norm(x):
    xf = x.astype(jnp.float32)
    return xf * lax.rsqrt(jnp.sum(xf * xf, axis=-1, keepdims=True) + EPS)


def modulate(x, c, norm_g, w_mod, b_mod):
    shift, scale, gate = jnp.split(jax.nn.silu(c) @ w_mod + b_mod, 3, axis=-1)
    h = rms_norm(x, norm_g) * (1 + scale[:, None]) + shift[:, None]
    return h, gate[:, None]


def sink_softmax(logits, mask, sinks):
    logits = jnp.where(mask, logits, -jnp.inf)
    sink = sinks.astype(jnp.float32).reshape(SWA_KV_HEADS, SWA_GROUP, 1, 1)
    m = jnp.maximum(jnp.max(logits, axis=-1, keepdims=True), sink)
    e = jnp.exp(logits - m)
    return e / (jnp.sum(e, axis=-1, keepdims=True) + jnp.exp(sink - m))


def swa_prompt_core(q, k, v, sinks):
    B, T = q.shape[:2]
    nb = T // SWA_BLOCK
    qb = q.reshape(B, nb, SWA_BLOCK, SWA_KV_HEADS, SWA_GROUP, SWA_HEAD_DIM)
    kb = k.reshape(B, nb, SWA_BLOCK, SWA_KV_HEADS, SWA_HEAD_DIM)
    vb = v.reshape(B, nb, SWA_BLOCK, SWA_KV_HEADS, SWA_HEAD_DIM)

    def band(x):
        prev = jnp.pad(x, ((0, 0), (1, 0), (0, 0), (0, 0), (0, 0)))[:, :-1]
        return jnp.concatenate([prev, x], axis=2)

    logits = jnp.einsum('bnqkgd,bnskd->bnkgqs', qb, band(kb),
                        preferred_element_type=jnp.float32) * SWA_SCALE
    qi = jnp.arange(SWA_BLOCK)[:, None]
    si = jnp.arange(2 * SWA_BLOCK)[None, :]
    dist = SWA_BLOCK + qi - si
    in_window = (dist >= 0) & (dist <= WINDOW)
    first = (jnp.arange(nb) == 0)[:, None, None]
    mask = in_window[None] & ~(first & (si < SWA_BLOCK)[None])
    p = sink_softmax(logits, mask[None, :, None, None], sinks)
    o = jnp.einsum('bnkgqs,bnskd->bnqkgd', p.astype(v.dtype), band(vb))
    return o.reshape(B, T, SWA_Q_W)


def swa_sample_core(q, k, v, k_buf, v_buf, sinks):
    B, T = q.shape[:2]
    P = k_buf.shape[1]
    k_all = jnp.concatenate([k_buf.astype(k.dtype), k], axis=1)
    v_all = jnp.concatenate([v_buf.astype(v.dtype), v], axis=1)
    logits = jnp.einsum('bqkgd,bskd->bkgqs', q, k_all,
                        preferred_element_type=jnp.float32) * SWA_SCALE
    dist = (P + jnp.arange(T))[:, None] - jnp.arange(P + T)[None, :]
    mask = (dist >= 0) & (dist <= WINDOW)
    p = sink_softmax(logits, mask, sinks)
    o = jnp.einsum('bkgqs,bskd->bqkgd', p.astype(v.dtype), v_all).reshape(B, T, SWA_Q_W)
    return o, k_all[:, -P:], v_all[:, -P:]


def swa_branch(h, w_in, sinks, w_out, k_buf=None, v_buf=None):
    B, T, _ = h.shape
    q, k, v, z = jnp.split(h @ w_in, [SWA_Q_W, SWA_Q_W + SWA_KV_W, SWA_Q_W + 2 * SWA_KV_W], axis=-1)
    q = q.reshape(B, T, SWA_KV_HEADS, SWA_GROUP, SWA_HEAD_DIM)
    k = k.reshape(B, T, SWA_KV_HEADS, SWA_HEAD_DIM)
    v = v.reshape(B, T, SWA_KV_HEADS, SWA_HEAD_DIM)
    if k_buf is None:
        attn = swa_prompt_core(q, k, v, sinks)
        keep = min(WINDOW, T)
        new_k, new_v = k[:, -keep:], v[:, -keep:]
    else:
        attn, new_k, new_v = swa_sample_core(q, k, v, k_buf, v_buf, sinks)
    out = (attn * jax.nn.silu(z)) @ w_out
    return out, new_k, new_v


def gated_delta_chunked(q, k, v, g, beta, s0, chunk):
    B, T, H, dk = q.shape
    dv = v.shape[-1]
    n = T // chunk

    def blocks(x):
        x = x.reshape((B, n, chunk, H) + x.shape[3:])
        return jnp.moveaxis(x, (1, 3), (0, 2))

    qc = blocks(q) * dk ** -0.5
    kc = blocks(k)
    vc = blocks(v)
    bc = blocks(beta)
    gc = jnp.cumsum(blocks(g), axis=-1)
    idx = jnp.arange(chunk)
    incl = idx[:, None] >= idx[None, :]
    strict = idx[:, None] > idx[None, :]
    decay = jnp.exp(jnp.where(incl, gc[..., :, None] - gc[..., None, :], -jnp.inf))
    kb = kc * bc[..., None]
    lower = jnp.where(strict, jnp.einsum('nbhik,nbhjk->nbhij', kb, kc) * decay, 0.0)
    rhs = jnp.concatenate([vc * bc[..., None], kb * jnp.exp(gc)[..., None]], axis=-1)
    sol = lax.linalg.triangular_solve(lower + jnp.eye(chunk, dtype=lower.dtype), rhs,
                                      left_side=True, lower=True, unit_diagonal=True)
    w_v, w_k = sol[..., :dv], sol[..., dv:]
    qk = jnp.einsum('nbhik,nbhjk->nbhij', qc, kc) * decay
    q_dec = qc * jnp.exp(gc)[..., None]
    k_dec = kc * jnp.exp(gc[..., -1:] - gc)[..., None]
    g_tot = jnp.exp(gc[..., -1])

    def step(S, xs):
        w_v_c, w_k_c, qk_c, q_dec_c, k_dec_c, g_tot_c = xs
        u = w_v_c - jnp.einsum('bhck,bhkv->bhcv', w_k_c, S)
        o = jnp.einsum('bhck,bhkv->bhcv', q_dec_c, S) + jnp.einsum('bhij,bhjv->bhiv', qk_c, u)
        S = S * g_tot_c[..., None, None] + jnp.einsum('bhck,bhcv->bhkv', k_dec_c, u)
        return S, o

    s_final, o = lax.scan(step, s0, (w_v, w_k, qk, q_dec, k_dec, g_tot))
    o = jnp.moveaxis(o, (0, 2), (1, 3)).reshape(B, T, H, dv)
    return o, s_final


def gdn_branch(h, conv_buf, s0, w_in, conv_w, a_log, dt_bias, o_norm_g, w_out):
    B, T, _ = h.shape
    xqkv, z, a, b = jnp.split(h @ w_in, [GDN_CONV_CH, GDN_CONV_CH + GDN_V_W,
                                         GDN_CONV_CH + GDN_V_W + GDN_V_HEADS], axis=-1)
    xpad = jnp.concatenate([conv_buf.astype(xqkv.dtype), xqkv], axis=1)
    conv = lax.conv_general_dilated(xpad, conv_w[:, None, :].astype(xpad.dtype), window_strides=(1,),
                                    padding='VALID', dimension_numbers=('NWC', 'WIO', 'NWC'),
                                    feature_group_count=GDN_CONV_CH)
    conv = jax.nn.silu(conv)
    q, k, v = jnp.split(conv, [GDN_K_W, 2 * GDN_K_W], axis=-1)
    rep = GDN_V_HEADS // GDN_QK_HEADS
    q = jnp.repeat(l2_norm(q.reshape(B, T, GDN_QK_HEADS, GDN_HEAD_DIM)), rep, axis=2)
    k = jnp.repeat(l2_norm(k.reshape(B, T, GDN_QK_HEADS, GDN_HEAD_DIM)), rep, axis=2)
    v = v.reshape(B, T, GDN_V_HEADS, GDN_HEAD_DIM).astype(jnp.float32)
    beta = jax.nn.sigmoid(b.astype(jnp.float32))
    g = -jnp.exp(a_log.astype(jnp.float32)) * jax.nn.softplus(a.astype(jnp.float32) + dt_bias.astype(jnp.float32))
    chunk = math.gcd(T, GDN_CHUNK)
    o, s_new = gated_delta_chunked(q, k, v, g, beta, s0.astype(jnp.float32), chunk)
    o = rms_norm(o, o_norm_g).astype(h.dtype)
    o = (o * jax.nn.silu(z).reshape(B, T, GDN_V_HEADS, GDN_HEAD_DIM)).reshape(B, T, GDN_V_W)
    return o @ w_out, xpad[:, -(GDN_CONV - 1):], s_new.astype(s0.dtype)


def setup_inputs(seed: int = 0) -> dict:
    key = jax.random.key(seed)
    ks = jax.random.split(key, 24)
    f32 = jnp.float32

    def nrm(k, shape, s):
        return s * jax.random.normal(k, shape, f32)

    w_buf = min(WINDOW, PAST_LEN)
    dt = jnp.exp(jax.random.uniform(ks[17], (N_GDN, GDN_V_HEADS), f32, math.log(1e-3), math.log(1e-1)))
    return {
        'x_prompt': nrm(ks[0], (BATCH, SEQ, D_MODEL), 1.0),
        'x_sample': nrm(ks[1], (DEC_BATCH, DEC_SEQ, D_MODEL), 1.0),
        'cache_swa_k': nrm(ks[2], (N_SWA, DEC_BATCH, w_buf, SWA_KV_HEADS, SWA_HEAD_DIM), 1.0),
        'cache_swa_v': nrm(ks[3], (N_SWA, DEC_BATCH, w_buf, SWA_KV_HEADS, SWA_HEAD_DIM), 1.0),
        'state_gdn_conv': nrm(ks[4], (N_GDN, DEC_BATCH, GDN_CONV - 1, GDN_CONV_CH), 1.0),
        'state_gdn_s': nrm(ks[5], (N_GDN, DEC_BATCH, GDN_V_HEADS, GDN_HEAD_DIM, GDN_HEAD_DIM), 0.1),
        'c_prompt': nrm(ks[6], (BATCH, D_MODEL), 1.0),
        'c_sample': nrm(ks[7], (DEC_BATCH, D_MODEL), 1.0),
        'norm_g': 1.0 + nrm(ks[8], (DEPTH, D_MODEL), 0.02),
        'w_mod': nrm(ks[9], (DEPTH, D_MODEL, 3 * D_MODEL), 0.5 * D_MODEL ** -0.5),
        'b_mod': nrm(ks[10], (DEPTH, 3 * D_MODEL), 0.01),
        'swa_w_in': nrm(ks[11], (N_SWA, D_MODEL, SWA_IN_W), D_MODEL ** -0.5),
        'swa_sinks': nrm(ks[12], (N_SWA, SWA_HEADS), 0.5),
        'swa_w_out': nrm(ks[13], (N_SWA, SWA_Q_W, D_MODEL), SWA_Q_W ** -0.5),
        'gdn_w_in': nrm(ks[14], (N_GDN, D_MODEL, GDN_IN_W), D_MODEL ** -0.5),
        'gdn_conv_w': nrm(ks[15], (N_GDN, GDN_CONV, GDN_CONV_CH), GDN_CONV ** -0.5),
        'gdn_a_log': jnp.log(jax.random.uniform(ks[16], (N_GDN, GDN_V_HEADS), f32, 1.0, 16.0)),
        'gdn_dt_bias': dt + jnp.log(-jnp.expm1(-dt)),
        'gdn_o_norm_g': 1.0 + nrm(ks[18], (N_GDN, GDN_HEAD_DIM), 0.02),
        'gdn_w_out': nrm(ks[19], (N_GDN, GDN_V_W, D_MODEL), GDN_V_W ** -0.5),
        'final_norm_g': 1.0 + nrm(ks[20], (D_MODEL,), 0.02),
    }


def reference(x_prompt, x_sample, cache_swa_k, cache_swa_v, state_gdn_conv, state_gdn_s,
              c_prompt, c_sample, norm_g, w_mod, b_mod, swa_w_in, swa_sinks, swa_w_out,
              gdn_w_in, gdn_conv_w, gdn_a_log, gdn_dt_bias, gdn_o_norm_g, gdn_w_out, final_norm_g):
    xp, xs = x_prompt, x_sample
    swa_kp, swa_vp, swa_ks, swa_vs = [], [], [], []
    conv_p, s_p, conv_s, s_s = [], [], [], []
    for layer in range(DEPTH):
        j = layer // N_MIXERS
        hp, gp = modulate(xp, c_prompt, norm_g[layer], w_mod[layer], b_mod[layer])
        hs, gs = modulate(xs, c_sample, norm_g[layer], w_mod[layer], b_mod[layer])
        if layer % N_MIXERS == 0:
            op, kp, vp = swa_branch(hp, swa_w_in[j], swa_sinks[j], swa_w_out[j])
            osm, ksm, vsm = swa_branch(hs, swa_w_in[j], swa_sinks[j], swa_w_out[j],
                                       cache_swa_k[j], cache_swa_v[j])
            swa_kp.append(kp)
            swa_vp.append(vp)
            swa_ks.append(ksm)
            swa_vs.append(vsm)
        else:
            bp = hp.shape[0]
            conv0 = jnp.zeros((bp, GDN_CONV - 1, GDN_CONV_CH), hp.dtype)
            s0 = jnp.zeros((bp, GDN_V_HEADS, GDN_HEAD_DIM, GDN_HEAD_DIM), state_gdn_s.dtype)
            op, cp, sp = gdn_branch(hp, conv0, s0, gdn_w_in[j], gdn_conv_w[j], gdn_a_log[j],
                                    gdn_dt_bias[j], gdn_o_norm_g[j], gdn_w_out[j])
            osm, csm, ssm = gdn_branch(hs, state_gdn_conv[j], state_gdn_s[j], gdn_w_in[j], gdn_conv_w[j],
                                       gdn_a_log[j], gdn_dt_bias[j], gdn_o_norm_g[j], gdn_w_out[j])
            conv_p.append(cp)
            s_p.append(sp)
            conv_s.append(csm)
            s_s.append(ssm)
        xp = xp + gp * op
        xs = xs + gs * osm
    y_prompt = rms_norm(xp, final_norm_g)
    y_sample = rms_norm(xs, final_norm_g)
    return (y_prompt, y_sample,
            jnp.stack(swa_kp), jnp.stack(swa_vp), jnp.stack(swa_ks), jnp.stack(swa_vs),
            jnp.stack(conv_p), jnp.stack(s_p), jnp.stack(conv_s), jnp.stack(s_s))
```

```python
import contextlib
import numpy as np
import ml_dtypes
import concourse.bass as bass
import concourse.mybir as mybir
from concourse.bass_utils import run_bass_kernel_spmd

F32 = mybir.dt.float32
BF16 = mybir.dt.bfloat16
AF = mybir.ActivationFunctionType
ALU = mybir.AluOpType
AX = mybir.AxisListType

D = 1024
EPS = 1e-6
NSB = 16
NS = 64
SWA_W = 2816
GDN_W = 6176


class Sched:
    ENGS = ["sp", "act", "dve", "pool", "pe"]

    def __init__(self, nc, ndma=32):
        self.nc = nc
        self.q = {e: [] for e in self.ENGS}
        self.cnt = {e: 0 for e in self.ENGS}
        self.waited = {e: {} for e in self.ENGS}
        self.lastw = {}
        self.readers = {}
        self.ndma = ndma
        self.dma_val = [0] * ndma
        self.dma_rr = {"sp": 0, "pool": 0}
        self.nops = 0
        import os as _os
        self.limit = int(_os.environ["KSTOP"]) if _os.environ.get("KSTOP") else None
        self.groups = {}

    def _x(self, keys):
        out = []
        for k in keys:
            out.extend(self.groups.get(k, (k,)))
        return out

    def _need(self, eng, tok):
        semkey, val = tok
        if semkey == ("e", "pe") and eng == "pe":
            return
        if self.waited[eng].get(semkey, 0) >= val:
            return
        self.waited[eng][semkey] = val
        self.q[eng].append(("wait", semkey, val))

    def _deps(self, eng, reads, writes):
        for k in reads:
            t = self.lastw.get(k)
            if t is not None:
                self._need(eng, t)
        for k in writes:
            t = self.lastw.get(k)
            if t is not None:
                self._need(eng, t)
            for t in self.readers.get(k, ()):
                self._need(eng, t)

    def _commit(self, tok, reads, writes):
        for k in writes:
            self.lastw[k] = tok
            self.readers[k] = []
        for k in reads:
            if k not in writes:
                self.readers.setdefault(k, []).append(tok)

    def op(self, eng, fn, reads=(), writes=(), inc=True):
        if self.limit is not None and self.nops >= self.limit:
            return
        reads, writes = self._x(reads), self._x(writes)
        self._deps(eng, reads, writes)
        if inc:
            self.cnt[eng] += 1
            tok = (("e", eng), self.cnt[eng])
            self.q[eng].append(("op", fn, ("e", eng), 1))
        else:
            tok = (("e", eng), self.cnt[eng] + 1)
            self.q[eng].append(("op", fn, None, 0))
        self._commit(tok, reads, writes)
        self.nops += 1
        self.flush()

    def dma(self, eng, fn, reads=(), writes=()):
        if self.limit is not None and self.nops >= self.limit:
            return
        reads, writes = self._x(reads), self._x(writes)
        half = self.ndma // 2
        base = 0 if eng == "sp" else half
        idx = base + self.dma_rr[eng]
        self.dma_rr[eng] = (self.dma_rr[eng] + 1) % half
        if self.dma_val[idx] > 0:
            self._need(eng, (("d", idx), self.dma_val[idx]))
        self._deps(eng, reads, writes)
        self.dma_val[idx] += 16
        tok = (("d", idx), self.dma_val[idx])
        self.q[eng].append(("op", fn, ("d", idx), 16))
        self._commit(tok, reads, writes)
        self.nops += 1
        self.flush()

    def barrier(self):
        for e in self.ENGS:
            for f in self.ENGS:
                if f != e and self.cnt[f] > 0:
                    self._need(e, (("e", f), self.cnt[f]))
            for idx in range(self.ndma):
                if self.dma_val[idx] > 0:
                    self._need(e, (("d", idx), self.dma_val[idx]))

    def finish(self):
        for idx in range(self.ndma):
            if self.dma_val[idx] > 0:
                self._need("sp", (("d", idx), self.dma_val[idx]))
        for e in self.ENGS:
            if e != "sp" and self.cnt[e] > 0:
                self._need("sp", (("e", e), self.cnt[e]))

    def attach(self, st):
        nc = self.nc
        self.sems = {}
        for e in self.ENGS:
            self.sems[("e", e)] = st.enter_context(nc.semaphore("s_" + e))
        for i in range(self.ndma):
            self.sems[("d", i)] = st.enter_context(nc.semaphore("d_%d" % i))
        self.eng = {"sp": nc.sync, "act": nc.scalar, "dve": nc.vector, "pool": nc.gpsimd, "pe": nc.tensor}

    def flush(self):
        for eng in self.ENGS:
            for item in self.q[eng]:
                if item[0] == "wait":
                    self.eng[eng].wait_ge(self.sems[item[1]], item[2])
                else:
                    ins = item[1](self.eng[eng])
                    if item[2] is not None:
                        ins.then_inc(self.sems[item[2]], item[3])
            self.q[eng].clear()


def build(T, depth, do_sample=True):
    NT = T // 128
    n_swa = (depth + 1) // 2
    n_gdn = depth // 2
    nc = bass.Bass("TRN2", target_bir_lowering=False)

    def din(name, shape, dt=F32):
        return nc.dram_tensor(name, list(shape), dt, kind="ExternalInput").ap()

    def dout(name, shape):
        return nc.dram_tensor(name, list(shape), F32, kind="ExternalOutput").ap()

    def dscr(name, shape, dt):
        return nc.dram_tensor(name, list(shape), dt).ap()

    xp = din("xp", [T, D])
    xs_in = din("xs", [NS, D])
    ctok = din("ctok", [17, D])
    cache_k = din("cache_k", [2, NSB, 128, 256])
    cache_v = din("cache_v", [2, NSB, 128, 256])
    conv_state = din("conv_state", [2, NSB, 3, 4096])
    s_state = din("s_state", [2, NSB, 16, 128, 128])
    norm_g = din("norm_g", [4, D])
    w_mod = din("w_mod", [4, D, 3 * D])
    b_mod = din("b_mod", [4, 3 * D])
    swa_w_in = din("swa_w_in", [2, D, SWA_W])
    swa_sinks = din("swa_sinks", [2, 16])
    swa_w_out = din("swa_w_out", [2, D, D])
    gdn_w_in = din("gdn_w_in", [2, D, GDN_W])
    gdn_conv_w = din("gdn_conv_w", [2, 4, 4096])
    gdn_conv_wT = din("gdn_conv_wT", [2, 128, 32, 4])
    gdn_a_log = din("gdn_a_log", [2, 16])
    gdn_dt_bias = din("gdn_dt_bias", [2, 16])
    gdn_o_norm_g = din("gdn_o_norm_g", [2, 128])
    gdn_w_out = din("gdn_w_out", [2, 2 * D, D])
    final_norm_g = din("final_norm_g", [D])
    c_ident = din("c_ident", [128, 128], BF16)
    c_mown = din("c_mown", [128, 4, 128], BF16)
    c_mprev = din("c_mprev", [128, 4, 128], BF16)
    c_lmask = din("c_lmask", [128, 128])
    c_uinc = din("c_uinc", [128, 128])
    c_mincl = din("c_mincl", [128, 4, 128], BF16)
    c_mstrict = din("c_mstrict", [128, 4, 128], BF16)
    c_ident4 = din("c_ident4", [128, 4, 128], BF16)
    c_bd16 = din("c_bd16", [128, 4, 128], BF16)
    c_off = din("c_off", [3, 128, 4, 128], BF16)
    c_offT = din("c_offT", [3, 128, 4, 128], BF16)

    y_p = dout("y_p", [T, D])
    y_s = dout("y_s", [NS, D])
    kp_out = dout("kp_out", [2, 128, 256])
    vp_out = dout("vp_out", [2, 128, 256])
    ks_out = dout("ks_out", [2, NSB, 128, 256])
    vs_out = dout("vs_out", [2, NSB, 128, 256])
    convp_out = dout("convp_out", [2, 3, 4096])
    sp_out = dout("sp_out", [2, 16, 128, 128])
    convs_out = dout("convs_out", [2, NSB, 3, 4096])
    ss_out = dout("ss_out", [2, NSB, 16, 128, 128])

    x_scr = dscr("x_scr", [T, D], F32)
    o_scr = dscr("o_scr", [T, 2 * D], BF16)
    qkv_scr = dscr("qkv_scr", [NT + 1, 128, 32 * 128], BF16)
    z_scr = dscr("z_scr", [T + 128, 2 * D], BF16)
    sproj_scr = dscr("sproj_scr", [NS, 4096], F32)
    xpad_scr = dscr("xpad_scr", [NSB, 7, 4096], F32)
    kv_s_scr = dscr("kv_s_scr", [NS, 24 * 128], BF16)
    gb_s_scr = dscr("gb_s_scr", [NS, 32], F32)
    osn_scr = dscr("osn_scr", [NS, 2 * D], F32)
    vs_scr = dscr("vs_scr", [NS, 256], BF16)

    S = Sched(nc)

    with contextlib.ExitStack() as st:
        S.attach(st)
        def sb(name, shape, dt):
            return st.enter_context(nc.sbuf_tensor(name, list(shape), dt))

        def ps(name, shape, dt):
            return st.enter_context(nc.psum_tensor(name, list(shape), dt))

        ident = sb("ident", [128, 128], BF16)
        mown = sb("mown", [128, 4, 128], BF16)
        mprev = sb("mprev", [128, 4, 128], BF16)
        lmask = sb("lmask", [128, 128], F32)
        uinc = sb("uinc", [128, 128], F32)
        mincl = sb("mincl", [128, 4, 128], BF16)
        mstrict = sb("mstrict", [128, 4, 128], BF16)
        ident4 = sb("ident4", [128, 4, 128], BF16)
        ones_f = sb("ones_f", [128, 128], F32)
        ones_b = sb("ones_b", [128, 128], BF16)
        for (t_, d_, k_) in [(ident, c_ident, "ident"), (mown, c_mown, "mown"), (mprev, c_mprev, "mprev"),
                             (lmask, c_lmask, "lmask"), (uinc, c_uinc, "uinc"), (mincl, c_mincl, "mincl"),
                             (mstrict, c_mstrict, "mstrict"), (ident4, c_ident4, "ident4")]:
            S.dma("sp", lambda e, t_=t_, d_=d_: e.dma_start(out=t_[:], in_=d_), [], [k_])
        S.op("pool", lambda e: e.memset(ones_f[:], 1.0), [], ["ones_f"])
        S.op("pool", lambda e: e.memset(ones_b[:], 1.0), [], ["ones_b"])

        pbT = ps("pbT", [128, 1024], BF16)
        pb = [ps("pb%d" % i, [128, 512], F32) for i in range(7)]

        xt = [sb("xt%d" % i, [128, D], F32) for i in range(2)]
        junk = sb("junk", [128, D], F32)
        ss = sb("ss", [128, 1], F32)
        rstd = sb("rstd", [128, 1], F32)
        h32 = sb("h32", [128, D], F32)
        hb = sb("hb", [128, D], BF16)
        hT = sb("hT", [128, 8, 128], BF16)
        modP = sb("modP", [128, 2 * D], F32)
        gateP = [sb("gateP%d" % i, [128, D], F32) for i in range(2)]
        normg = sb("normg", [128, D], F32)
        wmodb = sb("wmodb", [128, 8, 512], BF16)
        bmodb = sb("bmodb", [128, 512], F32)
        cT17 = sb("cT17", [128, 8, 17], BF16)
        cTp = sb("cTp", [128, 8, 128], BF16)
        cTs = sb("cTs", [128, 8, NS], BF16)
        ogT = sb("ogT", [128, 16, 128], BF16)
        tmpo = sb("tmpo", [128, 512], F32)
        xs = sb("xs_res", [NS, D], F32)
        og_s = sb("og_s", [NS, 2 * D], BF16)
        kv32 = sb("kv32", [128, 512], F32)
        zs = sb("zs", [128, 2 * D], BF16)
        expsink = sb("expsink", [128, 16], F32)
        den = sb("den", [128, 4], F32)
        negA = sb("negA", [128, 16], F32)
        dtb = sb("dtb", [128, 16], F32)
        ab_t = sb("ab_t", [128, 32], F32)
        gbt = sb("gbt", [128, 32], F32)
        mods_scr = dscr("mods_scr", [4, NS, 3 * D], F32)
        gb_scr = dscr("gb_scr", [T + 128, 32], F32)
        zs_s_scr = dscr("zs_s_scr", [NS, 2 * D], BF16)

        _cst = contextlib.ExitStack()
        c17 = _cst.enter_context(nc.sbuf_tensor("c17", [17, D], F32))
        c17b = _cst.enter_context(nc.sbuf_tensor("c17b", [17, D], BF16))
        S.dma("sp", lambda e: e.dma_start(out=c17[:], in_=ctok), [], ["c17"])
        S.op("act", lambda e: e.activation(out=c17b[:], in_=c17[:], func=AF.Silu), ["c17"], ["c17b"])
        for c in range(8):
            S.op("pe", lambda e, c=c: e.transpose(out=pbT[:, c * 32:c * 32 + 17], in_=c17b[:, c * 128:(c + 1) * 128],
                                                   identity=ident[0:17, 0:17]), ["c17b", "ident"], ["pbT"])
        S.op("dve", lambda e: e.tensor_copy(out=cT17[:], in_=pbT[:, 0:256].rearrange("p (c m) -> p c m", m=32)[:, :, 0:17]),
             ["pbT"], ["cT17"])
        S.op("dve", lambda e: e.tensor_copy(out=cTp[:], in_=cT17[:, :, 16:17].to_broadcast([128, 8, 128])), ["cT17"], ["cTp"])
        S.op("dve", lambda e: e.tensor_copy(out=cTs[:].rearrange("p c (b t) -> p c b t", t=4),
                                            in_=cT17[:, :, 0:16].unsqueeze(3).to_broadcast([128, 8, 16, 4])), ["cT17"], ["cTs"])

        S.barrier()
        _cst.close()

        def modulation(l):
            S.dma("sp", lambda e: e.dma_start(out=normg[:], in_=norm_g[l].partition_broadcast(128)), [], ["normg"])
            for gi in range(6):
                S.dma("pool", lambda e, gi=gi: e.dma_start(
                    out=wmodb[:], in_=w_mod[l, :, gi * 512:(gi + 1) * 512].rearrange("(c p) n -> p c n", p=128)), [], ["wmodb"])
                S.dma("sp", lambda e, gi=gi: e.dma_start(out=bmodb[:], in_=b_mod[l, gi * 512:(gi + 1) * 512].partition_broadcast(128)),
                      [], ["bmodb"])
                if gi < 4:
                    dst, dk_ = modP[:, gi * 512:(gi + 1) * 512], "modP"
                else:
                    dst, dk_ = gateP[l % 2][:, (gi - 4) * 512:(gi - 3) * 512], "gateP%d" % (l % 2)
                for c in range(8):
                    S.op("pe", lambda e, c=c: e.matmul(pb[0][:, :], lhsT=cTp[:, c, :], rhs=wmodb[:, c, :], start=(c == 0), stop=(c == 7)),
                         ["cTp", "wmodb"], ["pb0"])
                S.op("dve", lambda e, dst=dst: e.tensor_tensor(out=dst, in0=pb[0][:, :], in1=bmodb[:, :], op=ALU.add),
                     ["pb0", "bmodb"], [dk_])
                if gi in (2, 3):
                    S.op("dve", lambda e, dst=dst, gi=gi: e.scalar_tensor_tensor(
                        out=dst, in0=dst, scalar=1.0, in1=normg[:, (gi - 2) * 512:(gi - 1) * 512], op0=ALU.add, op1=ALU.mult),
                        [dk_, "normg"], [dk_])
                if do_sample:
                    for c in range(8):
                        S.op("pe", lambda e, c=c: e.matmul(pb[1][0:NS, :], lhsT=cTs[:, c, :], rhs=wmodb[:, c, :], start=(c == 0), stop=(c == 7)),
                             ["cTs", "wmodb"], ["pb1"])
                    S.op("dve", lambda e: e.tensor_tensor(out=tmpo[0:NS, :], in0=pb[1][0:NS, :], in1=bmodb[0:NS, :], op=ALU.add),
                         ["pb1", "bmodb"], ["tmpo"])
                    if gi in (2, 3):
                        S.op("dve", lambda e, gi=gi: e.scalar_tensor_tensor(
                            out=tmpo[0:NS, :], in0=tmpo[0:NS, :], scalar=1.0, in1=normg[0:NS, (gi - 2) * 512:(gi - 1) * 512],
                            op0=ALU.add, op1=ALU.mult), ["tmpo", "normg"], ["tmpo"])
                    S.dma("sp", lambda e, gi=gi: e.dma_start(out=mods_scr[l, :, gi * 512:(gi + 1) * 512], in_=tmpo[0:NS, :]),
                          ["tmpo"], ["mods_scr%d" % l])

        def load_w(dst, dkey, src, kc, width):
            S.groups[dkey] = ["%s.%d" % (dkey, c) for c in range(kc)]
            for c in range(kc):
                S.dma("pool", lambda e, c=c: e.dma_start(out=dst[:, c, 0:width], in_=src[c * 128:(c + 1) * 128, :]), [],
                      ["%s.%d" % (dkey, c)])

        def norm_to_hT(xin, xkey, M, mod, mkey):
            S.op("act", lambda e: e.activation(out=junk[0:M, 0:D], in_=xin[0:M, :], func=AF.Square, accum_out=ss[0:M, :]),
                 [xkey], ["junk", "ss"])
            S.op("act", lambda e: e.activation(out=rstd[0:M, :], in_=ss[0:M, :], func=AF.Sqrt, scale=1.0 / D, bias=EPS),
                 ["ss"], ["rstd"])
            S.op("dve", lambda e: e.reciprocal(out=rstd[0:M, :], in_=rstd[0:M, :]), ["rstd"], ["rstd"])
            S.op("dve", lambda e: e.scalar_tensor_tensor(out=h32[0:M, :], in0=xin[0:M, :], scalar=rstd[0:M, :],
                                                         in1=mod[0:M, D:2 * D], op0=ALU.mult, op1=ALU.mult),
                 [xkey, "rstd", mkey], ["h32"])
            S.op("dve", lambda e: e.tensor_tensor(out=hb[0:M, :], in0=h32[0:M, :], in1=mod[0:M, 0:D], op=ALU.add),
                 ["h32", mkey], ["hb"])
            for c in range(8):
                S.op("pe", lambda e, c=c: e.transpose(out=pbT[:, c * 128:c * 128 + M], in_=hb[0:M, c * 128:(c + 1) * 128],
                                                       identity=ident[0:M, 0:M]), ["hb", "ident"], ["pbT"], inc=(c == 7))
            S.op("act", lambda e: e.copy(out=hT[:, :, 0:M], in_=pbT[:].rearrange("p (c m) -> p c m", m=128)[:, :, 0:M]),
                 ["pbT"], ["hT"])

        def final_norm(xin, xkey, M, dst_ap, dkey, fng):
            S.op("act", lambda e: e.activation(out=junk[0:M, 0:D], in_=xin[0:M, :], func=AF.Square, accum_out=ss[0:M, :]),
                 [xkey], ["junk", "ss"])
            S.op("act", lambda e: e.activation(out=rstd[0:M, :], in_=ss[0:M, :], func=AF.Sqrt, scale=1.0 / D, bias=EPS),
                 ["ss"], ["rstd"])
            S.op("dve", lambda e: e.reciprocal(out=rstd[0:M, :], in_=rstd[0:M, :]), ["rstd"], ["rstd"])
            S.op("dve", lambda e: e.scalar_tensor_tensor(out=h32[0:M, :], in0=xin[0:M, :], scalar=rstd[0:M, :],
                                                         in1=fng[0:M, :], op0=ALU.mult, op1=ALU.mult),
                 [xkey, "rstd", "normg"], ["h32"])
            S.dma("sp", lambda e: e.dma_start(out=dst_ap, in_=h32[0:M, :]), ["h32"], [dkey])

        def out_proj(og, ogkey, M, kc, xin, xkey, gate, mkey):
            for c in range(kc):
                S.op("pe", lambda e, c=c: e.transpose(out=pbT[:, (c % 8) * 128:(c % 8) * 128 + M],
                                                       in_=og[0:M, c * 128:(c + 1) * 128], identity=ident[0:M, 0:M]),
                     [ogkey, "ident"], ["pbT"], inc=(c % 8 == 7))
                if c % 8 == 7:
                    c0 = c - 7
                    S.op("act", lambda e, c0=c0: e.copy(out=ogT[:, c0:c0 + 8, 0:M],
                                                        in_=pbT[:].rearrange("p (c m) -> p c m", m=128)[:, :, 0:M]),
                         ["pbT"], ["ogT"])
            for gi in range(2):
                bank, bk = pb[gi], "pb%d" % gi
                for c in range(kc):
                    S.op("pe", lambda e, c=c, gi=gi, bank=bank: e.matmul(
                        bank[0:M, :], lhsT=ogT[:, c, 0:M], rhs=w_out_sb[:, c, gi * 512:(gi + 1) * 512],
                        start=(c == 0), stop=(c == kc - 1)), ["ogT", "w_out_sb"], [bk], inc=(c == kc - 1))
                S.op("dve", lambda e, gi=gi, bank=bank: e.tensor_tensor(
                    out=tmpo[0:M, :], in0=bank[0:M, :], in1=gate[0:M, gi * 512:(gi + 1) * 512], op=ALU.mult),
                    [bk, mkey], ["tmpo"])
                S.op("dve", lambda e, gi=gi: e.tensor_tensor(
                    out=xin[0:M, gi * 512:(gi + 1) * 512], in0=xin[0:M, gi * 512:(gi + 1) * 512], in1=tmpo[0:M, :], op=ALU.add),
                    ["tmpo", xkey], [xkey])

        def swa_attend(nq, nprev, nown, q_ap, kprev_ap, kown_ap, vprev_ap, vown_ap, rk, j, out_ap, okey):
            N = 4 * nq
            par = j % 2
            bo, bp, bO = (pb[2], pb[3], pb[4]) if par == 0 else (pb[0], pb[1], pb[5])
            bok, bpk, bOk = ("pb2", "pb3", "pb4") if par == 0 else ("pb0", "pb1", "pb5")
            e_own, e_prev = e_own2[par], e_prev2[par]
            eok, epk = "e_own%d" % par, "e_prev%d" % par
            S.op("pe", lambda e: e.matmul(bo[0:nown, 0:N], lhsT=kown_ap, rhs=q_ap, start=True, stop=True), rk, [bok])
            S.op("act", lambda e: e.activation(out=e_own[0:nown, :, 0:nq], in_=bo[0:nown, 0:N].rearrange("p (g q) -> p g q", g=4),
                                               func=AF.Exp, scale=0.125), [bok], [eok])
            S.op("dve", lambda e: e.tensor_tensor(out=e_own[0:nown, :, 0:nq], in0=e_own[0:nown, :, 0:nq],
                                                   in1=mown[0:nown, :, 0:nq], op=ALU.mult), [eok, "mown"], [eok])
            if nprev:
                S.op("pe", lambda e: e.matmul(bp[0:nprev, 0:N], lhsT=kprev_ap, rhs=q_ap, start=True, stop=True), rk, [bpk])
                S.op("act", lambda e: e.activation(out=e_prev[0:nprev, :, 0:nq],
                                                   in_=bp[0:nprev, 0:N].rearrange("p (g q) -> p g q", g=4),
                                                   func=AF.Exp, scale=0.125), [bpk], [epk])
                S.op("dve", lambda e: e.tensor_tensor(out=e_prev[0:nprev, :, 0:nq], in0=e_prev[0:nprev, :, 0:nq],
                                                       in1=mprev[0:nprev, :, 0:nq], op=ALU.mult), [epk, "mprev"], [epk])
            for g in range(4):
                if nprev:
                    S.op("pe", lambda e, g=g: e.matmul(bO[0:nq, g * 65:(g + 1) * 65], lhsT=e_prev[0:nprev, g, 0:nq], rhs=vprev_ap,
                                                       start=True, stop=False), [epk] + rk, [bOk], inc=False)
                S.op("pe", lambda e, g=g: e.matmul(bO[0:nq, g * 65:(g + 1) * 65], lhsT=e_own[0:nown, g, 0:nq], rhs=vown_ap,
                                                   start=(not nprev), stop=True), [eok] + rk, [bOk], inc=(g == 3))
            O3 = bO[0:nq, 0:260].rearrange("p (g d) -> p g d", d=65)
            S.op("dve", lambda e: e.tensor_tensor(out=den[0:nq, :], in0=O3[:, :, 64], in1=expsink[0:nq, j * 4:(j + 1) * 4], op=ALU.add),
                 [bOk, "expsink"], ["den"])
            S.op("dve", lambda e: e.reciprocal(out=den[0:nq, :], in_=den[0:nq, :]), ["den"], ["den"])
            S.op("dve", lambda e: e.tensor_tensor(out=out_ap, in0=O3[:, :, 0:64],
                                                  in1=den[0:nq, :].unsqueeze(2).to_broadcast([nq, 4, 64]), op=ALU.mult),
                 [bOk, "den"], [okey])

        def swa_layer_setup(l2):
            S.dma("sp", lambda e: e.dma_start(out=expsink[:], in_=swa_sinks[l2].partition_broadcast(128)), [], ["expsink"])
            S.op("act", lambda e: e.activation(out=expsink[:], in_=expsink[:], func=AF.Exp), ["expsink"], ["expsink"])

        def swa_prompt_tile(l2, n, sample=False):
            cur, prv = n % 2, (n + 1) % 2
            if sample:
                cur, prv = 0, 1
            qk, qkk = qkT[cur], "qkT%d" % cur
            for ch in range(10):
                bank = pb[ch // 4 % 2]
                bk = "pb%d" % (ch // 4 % 2)
                sl = slice((ch % 4) * 128, (ch % 4 + 1) * 128)
                for c in range(8):
                    S.op("pe", lambda e, c=c, ch=ch, bank=bank, sl=sl: e.matmul(
                        bank[:, sl], lhsT=w_in_sb[:, c, ch * 128:(ch + 1) * 128], rhs=hT[:, c, :],
                        start=(c == 0), stop=(c == 7)), ["hT", "w_in_sb"], [bk], inc=(c == 7))
                if ch % 4 == 3 or ch == 9:
                    c0 = ch - (ch % 4)
                    nn = ch - c0 + 1
                    S.op("act", lambda e, c0=c0, nn=nn, bank=bank: e.copy(
                        out=qk[:, c0:c0 + nn, :], in_=bank[:, 0:nn * 128].rearrange("p (c m) -> p c m", m=128)),
                        [bk], [qkk])
            for c in range(8):
                S.op("pe", lambda e, c=c: e.matmul(pb[5][:, :], lhsT=hT[:, c, :], rhs=w_in_sb[:, c, 1280:1792],
                                                   start=(c == 0), stop=(c == 7)), ["hT", "w_in_sb"], ["pb5"], inc=(c == 7))
            S.op("dve", lambda e: e.tensor_copy(out=vext[cur][:, :, 0:64], in_=pb[5][:, 256:512].rearrange("p (j d) -> p j d", d=64)),
                 ["pb5"], ["vext%d" % cur])
            if sample:
                S.op("dve", lambda e: e.tensor_copy(out=kv32[:], in_=pb[5][:, :]), ["pb5"], ["kv32"])
                S.dma("sp", lambda e: e.dma_start(out=sproj_scr[:, 0:512], in_=kv32[0:NS, :]), ["kv32"], ["sproj_scr"])
                for (dst_, src_, c0_) in [(ks_out, cache_k, 0), (vs_out, cache_v, 256)]:
                    S.dma("sp", lambda e, dst_=dst_, c0_=c0_: e.dma_start(out=dst_[l2][:, 124:128, :], in_=sproj_scr[:, c0_:c0_ + 256].rearrange("(b t) c -> b t c", t=4)), ["sproj_scr"], ["ksvs_out"])
                    S.dma("sp", lambda e, dst_=dst_, src_=src_: e.dma_start(out=dst_[l2][:, 0:124, :], in_=src_[l2][:, 4:128, :]), [], ["ksvs_out2"])
            elif n == NT - 1:
                S.op("dve", lambda e: e.tensor_copy(out=kv32[:], in_=pb[5][:, :]), ["pb5"], ["kv32"])
                S.dma("sp", lambda e: e.dma_start(out=kp_out[l2], in_=kv32[:, 0:256]), ["kv32"], ["kp_out"])
                S.dma("sp", lambda e: e.dma_start(out=vp_out[l2], in_=kv32[:, 256:512]), ["kv32"], ["vp_out"])
            for gi in range(2):
                bank, bk = pb[gi], "pb%d" % gi
                for c in range(8):
                    S.op("pe", lambda e, c=c, gi=gi, bank=bank: e.matmul(
                        bank[:, :], lhsT=hT[:, c, :], rhs=w_in_sb[:, c, 1792 + gi * 512:1792 + (gi + 1) * 512],
                        start=(c == 0), stop=(c == 7)), ["hT", "w_in_sb"], [bk], inc=(c == 7))
                S.op("act", lambda e, gi=gi, bank=bank: e.activation(out=zs[:, gi * 512:(gi + 1) * 512], in_=bank[:, :], func=AF.Silu),
                     [bk], ["zs"])
            if sample:
                for b in range(NSB):
                    S.dma("pool", lambda e, b=b: e.dma_start(out=hb[:, 0:256], in_=cache_k[l2, b]), [], ["hb"])
                    for jp in range(2):
                        S.op("pe", lambda e, jp=jp: e.transpose(out=pbT[:, jp * 128:(jp + 1) * 128], in_=hb[:, jp * 128:(jp + 1) * 128], identity=ident[:]),
                             ["hb", "ident"], ["pbT"])
                    S.op("act", lambda e: e.copy(out=qkT[1][:, 8:10, :], in_=pbT[:, 0:256].rearrange("p (c m) -> p c m", m=128)), ["pbT"], ["qkT1"])
                    S.dma("pool", lambda e, b=b: e.dma_start(out=vext[1][:, :, 0:64], in_=cache_v[l2, b].rearrange("s (j d) -> s j d", d=64)), [], ["vext1"])
                    for c in range(8):
                        S.op("pe", lambda e, c=c, b=b: e.matmul(pb[5][0:4, 0:256], lhsT=hT[:, c, 4 * b:4 * b + 4], rhs=w_in_sb[:, c, 1536:1792],
                                                              start=(c == 0), stop=(c == 7)), ["hT", "w_in_sb"], ["pb5"], inc=(c == 7))
                    S.op("dve", lambda e: e.tensor_copy(out=vext[0][0:4, :, 0:64], in_=pb[5][0:4, 0:256].rearrange("p (j d) -> p j d", d=64)),
                         ["pb5"], ["vext0"])
                    for j in range(4):
                        jp, half = j // 2, j % 2
                        psl = slice(half * 64, half * 64 + 64)
                        swa_attend(4, 128, 4, qk[psl, jp * 4:jp * 4 + 4, 4 * b:4 * b + 4], qkT[1][psl, 8 + jp, :], qk[psl, 8 + jp, 4 * b:4 * b + 4],
                                   vext[1][:, j, :], vext[0][0:4, j, :], ["qkT0", "qkT1", "vext0", "vext1"], j,
                                   on32[0:4, j * 256:(j + 1) * 256].rearrange("p (g d) -> p g d", d=64), "on32")
                    S.dma("sp", lambda e, b=b: e.dma_start(out=h32[4 * b:4 * b + 4, :], in_=on32[0:4, :]), ["on32"], ["h32"])
                S.op("dve", lambda e: e.tensor_tensor(out=og_s[0:NS, 0:D], in0=h32[0:NS, :], in1=zs[0:NS, 0:D], op=ALU.mult),
                     ["h32", "zs"], ["og_s"])
                return
            for j in range(4):
                jp, half = j // 2, j % 2
                psl = slice(half * 64, half * 64 + 64)
                rk = [qkk, "qkT%d" % prv, "vext%d" % cur, "vext%d" % prv]
                swa_attend(128, 128 if n > 0 else 0, 128,
                           qk[psl, jp * 4:jp * 4 + 4, :], qkT[prv][psl, 8 + jp, :], qk[psl, 8 + jp, :],
                           vext[prv][:, j, :], vext[cur][:, j, :], rk, j,
                           on32[:, j * 256:(j + 1) * 256].rearrange("p (g d) -> p g d", d=64), "on32")
            S.op("dve", lambda e: e.tensor_tensor(out=og[:, 0:D], in0=on32[:, 0:D], in1=zs[:, 0:D], op=ALU.mult),
                 ["on32", "zs"], ["og"])
            S.dma("sp", lambda e: e.dma_start(out=o_scr[n * 128:(n + 1) * 128, 0:D], in_=og[:, 0:D]), ["og"], ["o_scr%d" % n])

        def load_x(n, src, skey):
            S.dma("sp", lambda e: e.dma_start(out=xt[n % 2][:], in_=src[n * 128:(n + 1) * 128, :]), [skey], ["xt%d" % (n % 2)])

        def load_o(n, width):
            S.dma("sp", lambda e: e.dma_start(out=ogl[n % 2][:, 0:width], in_=o_scr[n * 128:(n + 1) * 128, 0:width]),
                  ["o_scr%d" % n], ["ogl%d" % (n % 2)])

        def gdn_layer_setup(l2):
            S.dma("sp", lambda e: e.dma_start(out=negA[:], in_=gdn_a_log[l2].partition_broadcast(128)), [], ["negA"])
            S.op("act", lambda e: e.activation(out=negA[:], in_=negA[:], func=AF.Exp), ["negA"], ["negA"])
            S.op("dve", lambda e: e.tensor_scalar(out=negA[:], in0=negA[:], scalar1=-1.0, scalar2=None, op0=ALU.mult), ["negA"], ["negA"])
            S.dma("sp", lambda e: e.dma_start(out=dtb[:], in_=gdn_dt_bias[l2].partition_broadcast(128)), [], ["dtb"])

        def gdn_core_setup(l2):
            S.dma("sp", lambda e: e.dma_start(out=ong[:], in_=gdn_o_norm_g[l2].partition_broadcast(128)), [], ["ong"])
            S.dma("sp", lambda e: e.dma_start(out=cwT[:], in_=gdn_conv_wT[l2]), [], ["cwT"])

        def gates_from_psum(bank_ap, bk, M, dst_ap, dkey):
            S.op("dve", lambda e: e.tensor_tensor(out=ab_t[0:M, 0:16], in0=bank_ap[:, 0:16], in1=dtb[0:M, :], op=ALU.add),
                 [bk, "dtb"], ["ab_t"])
            S.op("act", lambda e: e.activation(out=ab_t[0:M, 0:16], in_=ab_t[0:M, 0:16], func=AF.Exp), ["ab_t"], ["ab_t"])
            S.op("act", lambda e: e.activation(out=ab_t[0:M, 0:16], in_=ab_t[0:M, 0:16], func=AF.Ln, bias=1.0), ["ab_t"], ["ab_t"])
            S.op("dve", lambda e: e.tensor_tensor(out=dst_ap[:, 0:16], in0=ab_t[0:M, 0:16], in1=negA[0:M, :], op=ALU.mult),
                 ["ab_t", "negA"], [dkey])
            S.op("act", lambda e: e.activation(out=dst_ap[:, 16:32], in_=bank_ap[:, 16:32], func=AF.Sigmoid), [bk], [dkey])

        def gdn_inproj_tile(l2, n):
            xk = "qst"
            for ch in range(32):
                bank = pb[ch // 4 % 2]
                bk = "pb%d" % (ch // 4 % 2)
                sl = slice((ch % 4) * 128, (ch % 4 + 1) * 128)
                for c in range(8):
                    S.op("pe", lambda e, c=c, ch=ch, bank=bank, sl=sl: e.matmul(
                        bank[:, sl], lhsT=w_in_sb[:, c, ch * 128:(ch + 1) * 128], rhs=hT[:, c, :],
                        start=(c == 0), stop=(c == 7)), ["hT", "w_in_sb"], [bk], inc=(c == 7))
                if ch % 4 == 3:
                    c0 = ch - 3
                    eng = "act" if (ch // 4) % 2 == 0 else "dve"
                    if eng == "act":
                        S.op("act", lambda e, c0=c0, bank=bank: e.copy(
                            out=qst[:, c0:c0 + 4, :], in_=bank[:, :].rearrange("p (c m) -> p c m", m=128)), [bk], [xk])
                    else:
                        S.op("dve", lambda e, c0=c0, bank=bank: e.tensor_copy(
                            out=qst[:, c0:c0 + 4, :], in_=bank[:, :].rearrange("p (c m) -> p c m", m=128)), [bk], [xk])
                    if n == NT - 1:
                        S.op("dve", lambda e, bank=bank: e.tensor_copy(
                            out=kv32[:, 0:12].rearrange("p (c m) -> p c m", m=3),
                            in_=bank[:, :].rearrange("p (c m) -> p c m", m=128)[:, :, 125:128]), [bk], ["kv32"])
                        for cc in range(4):
                            S.dma("sp", lambda e, c0=c0, cc=cc: e.dma_start(
                                out=convp_out[l2, :, (c0 + cc) * 128:(c0 + cc + 1) * 128].rearrange("t p -> p t"),
                                in_=kv32[:, cc * 3:cc * 3 + 3], allow_slow_non_contiguous=True),
                                ["kv32"], ["convp_out"])
            if n == NT:
                for gi in range(8):
                    bank, bk = pb[gi % 2], "pb%d" % (gi % 2)
                    for c in range(8):
                        S.op("pe", lambda e, c=c, gi=gi, bank=bank: e.matmul(
                            bank[0:NS, :], lhsT=hT[:, c, 0:NS], rhs=w_in_sb[:, c, gi * 512:(gi + 1) * 512],
                            start=(c == 0), stop=(c == 7)), ["hT", "w_in_sb"], [bk], inc=(c == 7))
                    S.op("dve", lambda e, bank=bank: e.tensor_copy(out=tmpo[0:NS, :], in_=bank[0:NS, :]), [bk], ["tmpo"])
                    S.dma("sp", lambda e, gi=gi: e.dma_start(out=sproj_scr[:, gi * 512:(gi + 1) * 512], in_=tmpo[0:NS, :]), ["tmpo"], ["sproj_scr"])
                S.dma("sp", lambda e: e.dma_start(out=convs_out[l2], in_=sproj_scr.rearrange("(b t) c -> b t c", t=4)[:, 1:4, :]),
                      ["sproj_scr"], ["convs_out"])
            S.dma("sp", lambda e: e.dma_start(out=qkv_scr[n].rearrange("p (c m) -> p c m", m=128), in_=qst[:]),
                  [xk], ["qkv_scr%d" % n])
            for gi in range(4):
                bank, bk = pb[gi % 2], "pb%d" % (gi % 2)
                for c in range(8):
                    S.op("pe", lambda e, c=c, gi=gi, bank=bank: e.matmul(
                        bank[:, :], lhsT=hT[:, c, :], rhs=w_in_sb[:, c, 4096 + gi * 512:4096 + (gi + 1) * 512],
                        start=(c == 0), stop=(c == 7)), ["hT", "w_in_sb"], [bk], inc=(c == 7))
                S.op("act", lambda e, gi=gi, bank=bank: e.activation(out=zs[:, gi * 512:(gi + 1) * 512], in_=bank[:, :], func=AF.Silu),
                     [bk], ["zs"])
            S.dma("sp", lambda e: e.dma_start(out=z_scr[n * 128:(n + 1) * 128, :], in_=zs[:]), ["zs"], ["z_scr%d" % n])
            for c in range(8):
                S.op("pe", lambda e, c=c: e.matmul(pb[5][:, 0:32], lhsT=hT[:, c, :], rhs=w_in_sb[:, c, 6144:6176],
                                                   start=(c == 0), stop=(c == 7)), ["hT", "w_in_sb"], ["pb5"], inc=(c == 7))
            gates_from_psum(pb[5][:, 0:32], "pb5", 128, gbt[:, :], "gbt")
            S.dma("sp", lambda e: e.dma_start(out=gb_scr[n * 128:(n + 1) * 128, :], in_=gbt[:]), ["gbt"], ["gb_scr%d" % n])

        def gdn_chunk(C, nlev, kT, qT, Kt, Vt, g_ap, b_ap, rk, O, okey, extra=None):
            v3 = lambda t_: t_[0:C, :, 0:C]
            b3 = lambda bank: bank[0:C, :].rearrange("p (h m) -> p h m", m=128)[:, :, 0:C]
            bfw = lambda bank: bank[0:C, :].rearrange("p (h m) -> p h m", m=128)
            p22 = lambda ap_: ap_[0:C].rearrange("p (a b) m -> p a b m", b=2)[:, :, :, 0:C]
            bc4 = lambda ap_, W: ap_.unsqueeze(2).to_broadcast([C, 4, W])
            def _gate_part(k_):
                if k_ == 0:
                    S.op("pe", lambda e: e.matmul(pb[5][0:C, 0:16], lhsT=uinc[0:C, 0:C], rhs=g_ap, start=True, stop=True), rk + ["uinc"], ["pb5"])
                    S.op("dve", lambda e: e.tensor_copy(out=gam[0:C, :], in_=pb[5][0:C, 0:16]), ["pb5"], ["gam"])
                    S.op("pe", lambda e: e.matmul(pb[5][:, 16:32], lhsT=ones_f[0:C, :], rhs=g_ap, start=True, stop=True), rk + ["ones_f"], ["pb5"])
                    S.op("dve", lambda e: e.tensor_copy(out=gtl[:], in_=pb[5][:, 16:32]), ["pb5"], ["gtl"])
                    S.op("act", lambda e: e.activation(out=gtot[:], in_=gtl[:], func=AF.Exp), ["gtl"], ["gtot"])
                    S.op("act", lambda e: e.activation(out=eg[0:C, :], in_=gam[0:C, :], func=AF.Exp), ["gam"], ["eg"])
                    S.op("dve", lambda e: e.tensor_scalar(out=negeg[0:C, :], in0=eg[0:C, :], scalar1=-1.0, scalar2=None, op0=ALU.mult), ["eg"], ["negeg"])
                    S.op("dve", lambda e: e.tensor_tensor(out=kdf[0:C, :], in0=gtl[0:C, :], in1=gam[0:C, :], op=ALU.subtract), ["gtl", "gam"], ["kdf"])
                    S.op("act", lambda e: e.activation(out=kdf[0:C, :], in_=kdf[0:C, :], func=AF.Exp), ["kdf"], ["kdf"])

                elif k_ == 1:
                    S.op("pool", lambda e: e.tensor_tensor(
                        out=Kdec[0:C].rearrange("p (q t) d -> p q t d", t=2), in0=Kt.unsqueeze(2).to_broadcast([C, 8, 2, 128]),
                        in1=kdf[0:C, :].rearrange("p (q t) -> p q t", t=2).unsqueeze(3).to_broadcast([C, 8, 2, 128]), op=ALU.mult),
                        rk + ["kdf"], ["Kdec"])

                elif k_ == 2:
                    S.op("pool", lambda e: e.tensor_tensor(out=Sst[:], in0=Sst[:], in1=gtot[:].unsqueeze(2).to_broadcast([128, 16, 128]), op=ALU.mult),
                         ["Sst", "gtot"], ["Sst"])

            def quad_gen(q4, s):
                hs = slice(q4 * 4, q4 * 4 + 4)
                bA, bB, bC = (pb[3], pb[4], pb[6]) if s == 0 else (pb[0], pb[1], pb[5])
                kA, kB, kC = ("pb3", "pb4", "pb6") if s == 0 else ("pb0", "pb1", "pb5")
                Eq, Eb, Nq, NTq, Xq, XTq = Eq2[s], Eb2[s], Nq2[s], NTq2[s], Xq2[s], XTq2[s]
                Pq, PTq = Pq2[s], PTq2[s]
                kn = lambda n_: "%s_s%d" % (n_, s)

                def mm4(bank, bk, L, Lk, Rr, Rk):
                    for hh in range(4):
                        S.op("pe", lambda e, hh=hh: e.matmul(bank[0:C, hh * 128:hh * 128 + C], lhsT=L[0:C, hh, 0:C], rhs=Rr[0:C, hh, 0:C],
                                                             start=True, stop=True), [Lk, Rk], [bk], inc=(hh == 3))
                S.op("dve", lambda e: e.tensor_tensor(out=v3(Gm), in0=lmask[0:C, 0:C].unsqueeze(1).to_broadcast([C, 4, C]),
                                                      in1=bc4(g_ap[:, hs], C), op=ALU.mult), rk + ["lmask"], ["Gm"])
                for hh in range(4):
                    S.op("pe", lambda e, hh=hh: e.matmul(bC[0:C, hh * 128:hh * 128 + C], lhsT=Gm[0:C, hh, 0:C], rhs=uinc[0:C, 0:C],
                                                         start=True, stop=True), ["Gm", "uinc"], [kC], inc=(hh == 3))
                yield
                S.op("act", lambda e: e.activation(out=v3(Eq), in_=b3(bC), func=AF.Exp), [kC], [kn("Eq")])
                for hp in range(2):
                    hq = q4 * 2 + hp
                    S.op("pe", lambda e, hq=hq, hp=hp: e.matmul(bA[0:C, hp * 128:hp * 128 + C], lhsT=kT(hq), rhs=kT(hq),
                                                                start=True, stop=True), rk, [kA], inc=False)
                    S.op("pe", lambda e, hq=hq, hp=hp: e.matmul(bA[0:C, 256 + hp * 128:256 + hp * 128 + C], lhsT=kT(hq), rhs=qT(hq),
                                                                start=True, stop=True), rk, [kA], inc=(hp == 1))
                yield
                S.op("dve", lambda e: e.tensor_tensor(out=v3(Eb), in0=v3(Eq), in1=bc4(b_ap[:, hs], C), op=ALU.mult), [kn("Eq")] + rk, [kn("Eb")])
                S.op("dve", lambda e: e.tensor_tensor(out=v3(Eb), in0=v3(Eb), in1=v3(mstrict), op=ALU.mult), [kn("Eb"), "mstrict"], [kn("Eb")])
                S.op("dve", lambda e: e.tensor_tensor(out=v3(Eq), in0=v3(Eq), in1=v3(mincl), op=ALU.mult), [kn("Eq"), "mincl"], [kn("Eq")])
                kk = bA[0:C, 0:256].rearrange("p (a m) -> p a m", m=128)[:, :, 0:C].unsqueeze(2).to_broadcast([C, 2, 2, C])
                qk_ = bA[0:C, 256:512].rearrange("p (a m) -> p a m", m=128)[:, :, 0:C].unsqueeze(2).to_broadcast([C, 2, 2, C])
                S.op("dve", lambda e: e.tensor_tensor(out=p22(Nq), in0=kk, in1=p22(Eb), op=ALU.mult), [kA, kn("Eb")], [kn("Nq")])
                S.op("dve", lambda e: e.tensor_tensor(out=QKD[0:C, hs, :].rearrange("p (a b) m -> p a b m", b=2)[:, :, :, 0:C],
                                                      in0=qk_, in1=p22(Eq), op=ALU.mult), [kA, kn("Eq")], ["QKD"])
                for hh in range(4):
                    S.op("pe", lambda e, hh=hh: e.transpose(out=pbT[0:C, hh * 128:hh * 128 + C], in_=Nq[0:C, hh, 0:C], identity=ident[0:C, 0:C]),
                         [kn("Nq"), "ident"], ["pbT"], inc=(hh == 3))
                S.op("act", lambda e: e.copy(out=v3(NTq), in_=pbT[0:C, 0:512].rearrange("p (h m) -> p h m", m=128)[:, :, 0:C]),
                     ["pbT"], [kn("NTq")])
                S.op("dve", lambda e: e.tensor_tensor(out=v3(Pq[0]), in0=v3(Nq), in1=v3(bd16), op=ALU.mult), [kn("Nq"), "bd16"], [kn("Pq0")])
                S.op("dve", lambda e: e.tensor_tensor(out=v3(PTq[0]), in0=v3(NTq), in1=v3(bd16), op=ALU.mult), [kn("NTq"), "bd16"], [kn("PTq0")])
                S.op("dve", lambda e: e.tensor_tensor(out=v3(Xq), in0=v3(ident4), in1=v3(Pq[0]), op=ALU.subtract), ["ident4", kn("Pq0")], [kn("Xq")])
                yield
                ci = 0
                for lv in range(nlev):
                    P_, PT_ = Pq[ci], PTq[ci]
                    Pk_, PTk_ = kn("Pq%d" % ci), kn("PTq%d" % ci)
                    ni = 1 - ci
                    lastb = (lv == nlev - 1)
                    if not lastb:
                        mm4(bA, kA, PT_, PTk_, P_, Pk_)
                    mm4(bB, kB, P_, Pk_, PT_, PTk_)
                    yield
                    if not lastb:
                        S.op("act", lambda e, ni=ni: e.copy(out=v3(Pq[ni]), in_=b3(bA)), [kA], [kn("Pq%d" % ni)])
                    S.op("act", lambda e, ni=ni: e.copy(out=v3(PTq[ni]), in_=b3(bB)), [kB], [kn("PTq%d" % ni)])
                    mm4(bC, kC, PTq[ni], kn("PTq%d" % ni), Xq, kn("Xq"))
                    yield
                    S.op("dve", lambda e: e.tensor_tensor(out=v3(Xq), in0=b3(bC), in1=v3(Xq), op=ALU.add), [kC, kn("Xq")], [kn("Xq")])
                    ci = ni
                if C > 16:
                    for si in range(3):
                        S.op("dve", lambda e, si=si: e.tensor_tensor(out=v3(PTq[0]), in0=v3(NTq), in1=v3(offmT[si]), op=ALU.mult),
                             [kn("NTq"), "offmT"], [kn("PTq0")])
                        mm4(bA, kA, PTq[0], kn("PTq0"), Xq, kn("Xq"))
                        for hh in range(4):
                            S.op("pe", lambda e, hh=hh: e.transpose(out=pbT[0:C, hh * 128:hh * 128 + C], in_=Xq[0:C, hh, 0:C],
                                                                    identity=ident[0:C, 0:C]), [kn("Xq"), "ident"], ["pbT"], inc=(hh == 3))
                        S.op("act", lambda e: e.copy(out=v3(XTq), in_=pbT[0:C, 0:512].rearrange("p (h m) -> p h m", m=128)[:, :, 0:C]),
                             ["pbT"], [kn("XTq")])
                        yield
                        S.op("act", lambda e: e.copy(out=v3(Pq[1]), in_=b3(bA)), [kA], [kn("Pq1")])
                        mm4(bC, kC, XTq, kn("XTq"), Pq[1], kn("Pq1"))
                        yield
                        last = (si == 2)
                        Xdst = Xall[0:C, q4 * 4:q4 * 4 + 4, 0:C] if last else v3(Xq)
                        S.op("dve", lambda e, Xdst=Xdst: e.tensor_tensor(out=Xdst, in0=v3(Xq), in1=b3(bC), op=ALU.subtract),
                             [kC, kn("Xq")], ["Xall" if last else kn("Xq")])
                else:
                    S.op("dve", lambda e: e.tensor_copy(out=Xall[0:C, q4 * 4:q4 * 4 + 4, 0:C], in_=v3(Xq)), [kn("Xq")], ["Xall"])

            bg = [extra] if extra is not None else []

            def run_rr(gens, drain=False):
                gens = list(gens)
                while gens or (drain and bg):
                    for g_ in list(gens):
                        try:
                            next(g_)
                        except StopIteration:
                            gens.remove(g_)
                    for g_ in list(bg):
                        try:
                            next(g_)
                        except StopIteration:
                            bg.remove(g_)
            _gate_part(0)
            run_rr([quad_gen(0, 0), quad_gen(1, 1)])
            _gate_part(1)
            run_rr([quad_gen(2, 0), quad_gen(3, 1)], drain=True)
            _gate_part(2)

            def seq_gen(q4, s):
                hs = slice(q4 * 4, q4 * 4 + 4)
                b1, b2 = (pb[0], pb[1]) if s == 0 else (pb[2], pb[3])
                k1, k2 = ("pb0", "pb1") if s == 0 else ("pb2", "pb3")
                tq, Vp, ub = tq2[s], Vp2[s], ub2[s]
                kn = lambda n_: "%s_q%d" % (n_, s)
                for hh in range(4):
                    h = q4 * 4 + hh
                    S.op("pe", lambda e, h=h, hh=hh: e.matmul(b1[0:C, hh * 128:(hh + 1) * 128], lhsT=kT(h // 2), rhs=Sbf[:, h, :],
                                                              start=True, stop=True), rk + ["Sbf"], [k1], inc=(hh == 3))
                for hh in range(4):
                    h = q4 * 4 + hh
                    S.op("pe", lambda e, h=h, hh=hh: e.matmul(b2[0:C, hh * 128:(hh + 1) * 128], lhsT=qT(h // 2), rhs=Sbf[:, h, :],
                                                              start=True, stop=True), rk + ["Sbf"], [k2], inc=(hh == 3))
                yield
                S.op("dve", lambda e: e.tensor_tensor(out=tq[0:C], in0=bfw(b1), in1=bc4(negeg[0:C, hs], 128), op=ALU.mult),
                     [k1, "negeg"], [kn("tq")])
                S.op("dve", lambda e: e.tensor_tensor(out=Vp[0:C], in0=tq[0:C], in1=Vt[:, hs, :], op=ALU.add), [kn("tq")] + rk, [kn("Vp")])
                for hh in range(4):
                    h = q4 * 4 + hh
                    S.op("pe", lambda e, h=h, hh=hh: e.matmul(b1[0:C, hh * 128:(hh + 1) * 128], lhsT=Xall[0:C, h, 0:C], rhs=Vp[0:C, hh, :],
                                                              start=True, stop=True), ["Xall", kn("Vp")], [k1], inc=(hh == 3))
                S.op("dve", lambda e: e.tensor_tensor(out=tq[0:C], in0=bfw(b2), in1=bc4(eg[0:C, hs], 128), op=ALU.mult),
                     [k2, "eg"], [kn("tq")])
                yield
                S.op("dve", lambda e: e.tensor_tensor(out=ub[0:C], in0=bfw(b1), in1=bc4(b_ap[:, hs], 128), op=ALU.mult),
                     [k1] + rk, [kn("ub")])
                for hh in range(4):
                    h = q4 * 4 + hh
                    S.op("pe", lambda e, h=h, hh=hh: e.matmul(b2[0:C, hh * 128:(hh + 1) * 128], lhsT=QKD[0:C, h, 0:C], rhs=ub[0:C, hh, :],
                                                              start=True, stop=True), ["QKD", kn("ub")], [k2], inc=(hh == 3))
                for hh in range(4):
                    h = q4 * 4 + hh
                    S.op("pe", lambda e, h=h, hh=hh: e.matmul(b1[:, hh * 128:(hh + 1) * 128], lhsT=Kdec[0:C, h, :], rhs=ub[0:C, hh, :],
                                                              start=True, stop=True), ["Kdec", kn("ub")], [k1], inc=(hh == 3))
                yield
                S.op("dve", lambda e: e.tensor_tensor(out=O[:, hs, :], in0=bfw(b2), in1=tq[0:C], op=ALU.add), [k2, kn("tq")], [okey])
                S.op("dve", lambda e: e.tensor_tensor(out=Sst[:, hs, :], in0=Sst[:, hs, :],
                                                      in1=b1[:, :].rearrange("p (h m) -> p h m", m=128), op=ALU.add),
                     ["Sst", k1], ["Sst"])
                S.op("act", lambda e: e.copy(out=Sbf[:, hs, :], in_=Sst[:, hs, :]), ["Sst"], ["Sbf"])
            run_rr([seq_gen(0, 0), seq_gen(1, 1)])
            run_rr([seq_gen(2, 0), seq_gen(3, 1)])


        def gdn_onorm_gate(M, o_in, okey, z_in, zkey, dst, dkey):
            S.op("act", lambda e: e.activation(out=junk2[0:M, :], in_=o_in, func=AF.Square), [okey], ["junk2"])
            S.op("dve", lambda e: e.tensor_reduce(out=ssq16[0:M, :], in_=junk2[0:M, :].rearrange("p (h d) -> p h d", d=128), axis=AX.X, op=ALU.add),
                 ["junk2"], ["ssq16"])
            S.op("act", lambda e: e.activation(out=ssq16[0:M, :], in_=ssq16[0:M, :], func=AF.Sqrt, scale=1.0 / 128, bias=EPS), ["ssq16"], ["ssq16"])
            S.op("dve", lambda e: e.reciprocal(out=ssq16[0:M, :], in_=ssq16[0:M, :]), ["ssq16"], ["ssq16"])
            j3 = junk2[0:M, :].rearrange("p (h d) -> p h d", d=128)
            S.op("dve", lambda e: e.tensor_tensor(out=j3, in0=o_in.rearrange("p (h d) -> p h d", d=128),
                                                  in1=ssq16[0:M, :].unsqueeze(2).to_broadcast([M, 16, 128]), op=ALU.mult),
                 [okey, "ssq16"], ["junk2"])
            S.op("dve", lambda e: e.tensor_tensor(out=j3, in0=j3, in1=ong[0:M, :].unsqueeze(1).to_broadcast([M, 16, 128]), op=ALU.mult),
                 ["junk2", "ong"], ["junk2"])
            S.op("dve", lambda e: e.tensor_tensor(out=dst, in0=junk2[0:M, :], in1=z_in, op=ALU.mult), ["junk2", zkey], [dkey])

        def gdn_core_tile(l2, n):
            xi, xk = xin[n % 2], "xin%d" % (n % 2)
            xo, xok = xin[(n + 1) % 2], "xin%d" % ((n + 1) % 2)
            if n == 0:
                S.dma("sp", lambda e: e.dma_start(out=xi[:, :, 3:131], in_=qkv_scr[n].rearrange("p (c m) -> p c m", m=128)),
                      ["qkv_scr%d" % n], [xk])
            S.dma("sp", lambda e: e.dma_start(out=zs[:], in_=z_scr[n * 128:(n + 1) * 128, :]), ["z_scr%d" % n], ["zs"])
            S.dma("sp", lambda e: e.dma_start(out=gbt[:], in_=gb_scr[n * 128:(n + 1) * 128, :]), ["gb_scr%d" % n], ["gbt"])
            if n == 0:
                S.op("pool", lambda e: e.memset(xi[:, :, 0:3], 0.0), [xk], [xk])
            else:
                S.op("pool", lambda e: e.tensor_copy(out=xi[:, :, 0:3], in_=xo[:, :, 128:131]), [xk, xok], [xk])
            if n + 1 < NT:
                S.dma("sp", lambda e: e.dma_start(out=xo[:, :, 3:131], in_=qkv_scr[n + 1].rearrange("p (c m) -> p c m", m=128)),
                      ["qkv_scr%d" % (n + 1)], [xok])
            def conv_group(gi, ylds):
                csl = slice(gi * 8, gi * 8 + 8)
                for j, eng in [(0, "pool"), (1, "pool"), (2, "dve"), (3, "dve")]:
                    S.op(eng, lambda e, j=j, csl=csl: e.tensor_tensor(
                        out=ct[j][:], in0=xi[:, csl, j:j + 128], in1=cwT[:, csl, j:j + 1].to_broadcast([128, 8, 128]), op=ALU.mult),
                        [xk, "cwT"], ["ct%d" % j])
                    if ylds and j % 2 == 1:
                        yield
                S.op("dve", lambda e: e.tensor_tensor(out=ct[2][:], in0=ct[2][:], in1=ct[3][:], op=ALU.add), ["ct2", "ct3"], ["ct2"])
                if ylds:
                    yield
                S.op("dve", lambda e: e.tensor_tensor(out=ct[0][:], in0=ct[0][:], in1=ct[1][:], op=ALU.add), ["ct0", "ct1"], ["ct0"])
                S.op("dve", lambda e: e.tensor_tensor(out=ct[0][:], in0=ct[0][:], in1=ct[2][:], op=ALU.add), ["ct0", "ct2"], ["ct0"])
                if ylds:
                    yield
                S.op("act", lambda e, csl=csl: e.activation(out=cs[:, csl, :], in_=ct[0][:], func=AF.Silu), ["ct0"], ["cs"])

            for gi in range(2):
                for _ in conv_group(gi, False):
                    pass
            gdn_qk_norm()
            for hq in range(8):
                S.op("pe", lambda e, hq=hq: e.transpose(out=pbT[:, hq * 128:(hq + 1) * 128], in_=qkn[:, 8 + hq, :], identity=ident[:]),
                     ["qkn", "ident"], ["pbT"], inc=(hq == 7))
            S.op("act", lambda e: e.copy(out=Ktm[:], in_=pbT[:].rearrange("p (c m) -> p c m", m=128)), ["pbT"], ["Ktm"])

            def v_gen():
                for gi in (2, 3):
                    yield from conv_group(gi, True)
                    yield
                    half = gi - 2
                    for hh in range(8):
                        S.op("pe", lambda e, hh=hh, half=half: e.transpose(out=pbT[:, hh * 128:(hh + 1) * 128], in_=cs[:, 16 + half * 8 + hh, :],
                                                                           identity=ident[:]), ["cs", "ident"], ["pbT"], inc=(hh == 7))
                    S.op("act", lambda e, half=half: e.copy(out=Vtm[:, half * 8:half * 8 + 8, :], in_=pbT[:].rearrange("p (c m) -> p c m", m=128)),
                         ["pbT"], ["Vtm"])
                    yield
            gdn_chunk(128, 3,
                      lambda hq: qkn[:, 8 + hq, :], lambda hq: qkn[:, hq, :], Ktm[:], Vtm[:],
                      gbt[:, 0:16], gbt[:, 16:32], ["qkn", "Ktm", "Vtm", "gbt"],
                      o32[:], "o32", extra=v_gen())
            gdn_onorm_gate(128, o32[:].rearrange("p h d -> p (h d)"), "o32", zs[:], "zs", og[:], "og")
            S.dma("sp", lambda e: e.dma_start(out=o_scr[n * 128:(n + 1) * 128, :], in_=og[:]), ["og"], ["o_scr%d" % n])

        def sample_main(l):
            is_swa_ = (l % 2 == 0)
            l2_ = l // 2
            lp_ = l - 1
            kcp = 8 if (lp_ % 2 == 0) else 16
            if l > 0:
                S.dma("sp", lambda e: e.dma_start(out=h32[0:NS, :], in_=mods_scr[lp_, :, 2 * D:3 * D]), ["mods_scr%d" % lp_], ["h32"])
                out_proj(og_s, "og_s", NS, kcp, xs, "xs", h32, "h32")
            if l < depth:
                S.dma("sp", lambda e: e.dma_start(out=modP[0:NS, :], in_=mods_scr[l, :, 0:2 * D]), ["mods_scr%d" % l], ["modP"])
                norm_to_hT(xs, "xs", NS, modP, "modP")
                if is_swa_:
                    swa_prompt_tile(l2_, NT, sample=True)
                else:
                    gdn_inproj_tile(l2_, NT)
            else:
                final_norm(xs, "xs", NS, y_s, "y_s", normg)

        def gdn_qk_norm():
            for qd in range(4):
                hs = slice(qd * 4, qd * 4 + 4)
                S.op("act", lambda e, hs=hs: e.activation(out=sq[:], in_=cs[:, hs, :], func=AF.Square), ["cs"], ["sq"])
                S.op("pe", lambda e: e.matmul(pb[0][:, :], lhsT=ones_b[:], rhs=sq[:].rearrange("p h m -> p (h m)"), start=True, stop=True),
                     ["sq", "ones_b"], ["pb0"])
                S.op("dve", lambda e: e.tensor_copy(out=rn[:].rearrange("p h m -> p (h m)"), in_=pb[0][:, :]), ["pb0"], ["rn"])
                S.op("act", lambda e: e.activation(out=rn[:], in_=rn[:], func=AF.Sqrt, bias=EPS), ["rn"], ["rn"])
                S.op("dve", lambda e: e.reciprocal(out=rn[:], in_=rn[:]), ["rn"], ["rn"])
                sc = (128.0 ** -0.5) if qd < 2 else 1.0
                S.op("dve", lambda e, hs=hs, sc=sc: e.scalar_tensor_tensor(out=qkn[:, hs, :], in0=cs[:, hs, :], scalar=sc, in1=rn[:],
                                                                          op0=ALU.mult, op1=ALU.mult), ["cs", "rn"], ["qkn"])

        def sample_gdn_core(l2):
            xi, xk = xin[0], "xin0"
            xv = xi[:, :, 0:112].rearrange("p c (b t) -> p c b t", t=7)
            S.dma("sp", lambda e: e.dma_start(out=xin[1][:, :, 0:128], in_=qkv_scr[NT].rearrange("p (c m) -> p c m", m=128)),
                  ["qkv_scr%d" % NT], ["xin1"])
            for gi in range(4):
                csl = slice(gi * 8, gi * 8 + 8)
                S.op("pool", lambda e, csl=csl: e.tensor_copy(out=xv[:, csl, :, 3:7],
                                                              in_=xin[1][:, csl, 0:64].rearrange("p c (b t) -> p c b t", t=4)),
                     ["xin1", xk], [xk])
                S.dma("pool", lambda e, gi=gi: e.dma_start(out=hb[0:48, :], in_=conv_state[l2][:, :, gi * 1024:(gi + 1) * 1024].rearrange("b t c -> (b t) c")),
                      [], ["hb"])
                for c in range(8):
                    S.op("pe", lambda e, c=c: e.transpose(out=pbT[:, c * 128:c * 128 + 48], in_=hb[0:48, c * 128:(c + 1) * 128], identity=ident[0:48, 0:48]),
                         ["hb", "ident"], ["pbT"])
                S.op("act", lambda e, csl=csl: e.copy(out=xv[:, csl, :, 0:3],
                                                      in_=pbT[:].rearrange("p (c m) -> p c m", m=128)[:, :, 0:48].rearrange("p c (b t) -> p c b t", t=3)),
                     ["pbT", xk], [xk])
                for j, eng in [(0, "pool"), (1, "pool"), (2, "dve"), (3, "dve")]:
                    S.op(eng, lambda e, j=j, csl=csl: e.tensor_tensor(
                        out=ct[j][:, :, 0:64].rearrange("p c (b t) -> p c b t", t=4), in0=xv[:, csl, :, j:j + 4],
                        in1=cwT[:, csl, j:j + 1].unsqueeze(3).to_broadcast([128, 8, 16, 4]), op=ALU.mult), [xk, "cwT"], ["ct%d" % j])
                S.op("dve", lambda e: e.tensor_tensor(out=ct[2][:], in0=ct[2][:], in1=ct[3][:], op=ALU.add), ["ct2", "ct3"], ["ct2"])
                S.op("dve", lambda e: e.tensor_tensor(out=ct[0][:], in0=ct[0][:], in1=ct[1][:], op=ALU.add), ["ct0", "ct1"], ["ct0"])
                S.op("dve", lambda e: e.tensor_tensor(out=ct[0][:], in0=ct[0][:], in1=ct[2][:], op=ALU.add), ["ct0", "ct2"], ["ct0"])
                S.op("act", lambda e, csl=csl: e.activation(out=cs[:, csl, :], in_=ct[0][:], func=AF.Silu), ["ct0"], ["cs"])
            gdn_qk_norm()
            for b in range(NSB):
                S.dma("sp", lambda e, b=b: e.dma_start(out=Sst[:], in_=s_state[l2, b].rearrange("h k v -> k h v")), [], ["Sst"])
                S.op("pool", lambda e: e.tensor_copy(out=Sbf[:], in_=Sst[:]), ["Sst"], ["Sbf"])
                S.dma("sp", lambda e, b=b: e.dma_start(out=gbt[0:4, :], in_=gb_scr[NT * 128 + 4 * b:NT * 128 + 4 * b + 4, :]),
                      ["gb_scr%d" % NT], ["gbt"])
                for hq in range(8):
                    S.op("pe", lambda e, hq=hq, b=b: e.transpose(out=pbT[0:4, hq * 128:(hq + 1) * 128], in_=qkn[:, 8 + hq, 4 * b:4 * b + 4], identity=ident[:]),
                         ["qkn", "ident"], ["pbT"])
                S.op("act", lambda e: e.copy(out=Ktm[0:4, :, :], in_=pbT[0:4, :].rearrange("p (c m) -> p c m", m=128)), ["pbT"], ["Ktm"])
                for half in range(2):
                    for hh in range(8):
                        S.op("pe", lambda e, hh=hh, half=half, b=b: e.transpose(out=pbT[0:4, hh * 128:(hh + 1) * 128],
                                                                               in_=cs[:, 16 + half * 8 + hh, 4 * b:4 * b + 4], identity=ident[:]),
                             ["cs", "ident"], ["pbT"])
                    S.op("act", lambda e, half=half: e.copy(out=Vtm[0:4, half * 8:half * 8 + 8, :], in_=pbT[0:4, :].rearrange("p (c m) -> p c m", m=128)),
                         ["pbT"], ["Vtm"])
                gdn_chunk(4, 1,
                          lambda hq, b=b: qkn[:, 8 + hq, 4 * b:4 * b + 4], lambda hq, b=b: qkn[:, hq, 4 * b:4 * b + 4],
                          Ktm[0:4], Vtm[0:4],
                          gbt[0:4, 0:16], gbt[0:4, 16:32], ["qkn", "Ktm", "Vtm", "gbt"],
                          o32[0:4], "o32")
                S.dma("sp", lambda e, b=b: e.dma_start(out=ss_out[l2, b].rearrange("h k v -> k h v"), in_=Sst[:]), ["Sst"], ["ss_out"])
                S.dma("sp", lambda e, b=b: e.dma_start(out=osall[4 * b:4 * b + 4, :], in_=o32[0:4, :, :].rearrange("p h d -> p (h d)")),
                      ["o32"], ["osall"])
            S.dma("sp", lambda e: e.dma_start(out=zs[0:NS, :], in_=z_scr[NT * 128:NT * 128 + NS, :]), ["z_scr%d" % NT], ["zs"])
            gdn_onorm_gate(NS, osall[0:NS, :], "osall", zs[0:NS, :], "zs", og_s[0:NS, :], "og_s")

        def sbs(stk, name, shape, dt):
            return stk.enter_context(nc.sbuf_tensor(name, list(shape), dt))

        S.dma("sp", lambda e: e.dma_start(out=xs[:], in_=xs_in), [], ["xs"])
        for l in range(depth + 1):
            is_swa = (l % 2 == 0)
            l2 = l // 2
            lp = l - 1
            kc_prev = 8 if (lp % 2 == 0) else 16
            tg = "_p%d" % l
            with contextlib.ExitStack() as stk:
                if l > 0:
                    w_out_sb = sbs(stk, "w_out_sb" + tg, [128, kc_prev, D], BF16)
                    ogl = [sbs(stk, "ogl%d" % i + tg, [128, kc_prev * 128], BF16) for i in range(2)]
                if l < depth:
                    w_in_sb = sbs(stk, "w_in_sb" + tg, [128, 8, SWA_W if is_swa else GDN_W], BF16)
                    if is_swa:
                        qkT = [sbs(stk, "qkT%d" % i + tg, [128, 10, 128], BF16) for i in range(2)]
                        vext = [sbs(stk, "vext%d" % i + tg, [128, 4, 65], BF16) for i in range(2)]
                        e_own2 = [sbs(stk, "e_own%d" % i + tg, [128, 4, 128], BF16) for i in range(2)]
                        e_prev2 = [sbs(stk, "e_prev%d" % i + tg, [128, 4, 128], BF16) for i in range(2)]
                        on32 = sbs(stk, "on32" + tg, [128, D], F32)
                        og = sbs(stk, "og" + tg, [128, D], BF16)
                        for i in range(2):
                            S.op("pool", lambda e, i=i: e.memset(vext[i][:], 1.0), [], ["vext%d" % i])
                    else:
                        qst = sbs(stk, "qst" + tg, [128, 32, 128], BF16)
                if l > 0:
                    if lp % 2 == 0:
                        load_w(w_out_sb, "w_out_sb", swa_w_out[lp // 2], 8, D)
                    else:
                        load_w(w_out_sb, "w_out_sb", gdn_w_out[lp // 2], 16, D)
                if l < depth:
                    if is_swa:
                        load_w(w_in_sb, "w_in_sb", swa_w_in[l2], 8, SWA_W)
                        swa_layer_setup(l2)
                    else:
                        load_w(w_in_sb, "w_in_sb", gdn_w_in[l2], 8, GDN_W)
                        gdn_layer_setup(l2)
                    modulation(l)
                else:
                    S.dma("sp", lambda e: e.dma_start(out=normg[:], in_=final_norm_g.partition_broadcast(128)), [], ["normg"])
                gprev, gpk = gateP[lp % 2], "gateP%d" % (lp % 2)
                src_x, skey_fn = (xp, lambda n: "xp") if l <= 1 else (x_scr, lambda n: "x_scr%d" % n)
                load_x(0, src_x, skey_fn(0))
                if l > 0:
                    load_o(0, kc_prev * 128)
                for n in range(NT):
                    if n + 1 < NT:
                        load_x(n + 1, src_x, skey_fn(n + 1))
                        if l > 0:
                            load_o(n + 1, kc_prev * 128)
                    xc, xck = xt[n % 2], "xt%d" % (n % 2)
                    if l > 0:
                        out_proj(ogl[n % 2], "ogl%d" % (n % 2), 128, kc_prev, xc, xck, gprev, gpk)
                        if l < depth:
                            S.dma("sp", lambda e, n=n, xc=xc: e.dma_start(out=x_scr[n * 128:(n + 1) * 128, :], in_=xc[:]),
                                  [xck], ["x_scr%d" % n])
                    if l < depth:
                        norm_to_hT(xc, xck, 128, modP, "modP")
                        if is_swa:
                            swa_prompt_tile(l2, n)
                        else:
                            gdn_inproj_tile(l2, n)
                    else:
                        final_norm(xc, xck, 128, y_p[n * 128:(n + 1) * 128, :], "y_p", normg)
                if do_sample:
                    sample_main(l)
                S.barrier()
                S.flush()
            if l < depth and not is_swa:
                tg = "_c%d" % l
                with contextlib.ExitStack() as stk:
                    xin = [sbs(stk, "xin%d" % i + tg, [128, 32, 131], BF16) for i in range(2)]
                    cwT = sbs(stk, "cwT" + tg, [128, 32, 4], F32)
                    ctall = sbs(stk, "ctall" + tg, [128, 4, 8, 128], F32)
                    ct = [ctall[:, i] for i in range(4)]
                    osall = ctall[:, 0:2].rearrange("p a c m -> p (a c m)")
                    junk2 = ctall[:, 2:4].rearrange("p a c m -> p (a c m)")
                    rn = ctall[:, 0, 0:4, :]
                    S.groups["osall"] = ["ct0", "ct1"]
                    S.groups["junk2"] = ["ct2", "ct3"]
                    S.groups["rn"] = ["ct0"]
                    cs = sbs(stk, "cs" + tg, [128, 32, 128], BF16)
                    sq = sbs(stk, "sq" + tg, [128, 4, 128], BF16)
                    qkn = sbs(stk, "qkn" + tg, [128, 16, 128], BF16)
                    Ktm = sbs(stk, "Ktm" + tg, [128, 8, 128], BF16)
                    Vtm = sbs(stk, "Vtm" + tg, [128, 16, 128], BF16)
                    ong = sbs(stk, "ong" + tg, [128, 128], F32)
                    gam = sbs(stk, "gam" + tg, [128, 16], F32)
                    gtl = sbs(stk, "gtl" + tg, [128, 16], F32)
                    gtot = sbs(stk, "gtot" + tg, [128, 16], F32)
                    eg = sbs(stk, "eg" + tg, [128, 16], F32)
                    negeg = sbs(stk, "negeg" + tg, [128, 16], F32)
                    kdf = sbs(stk, "kdf" + tg, [128, 16], F32)
                    Gm = sbs(stk, "Gm" + tg, [128, 4, 128], F32)
                    QKD = sbs(stk, "QKD" + tg, [128, 16, 128], BF16)
                    Eq2 = [sbs(stk, "Eq%d" % i + tg, [128, 4, 128], BF16) for i in range(2)]
                    Eb2 = [sbs(stk, "Eb%d" % i + tg, [128, 4, 128], BF16) for i in range(2)]
                    Nq2 = [sbs(stk, "Nq%d" % i + tg, [128, 4, 128], BF16) for i in range(2)]
                    NTq2 = [sbs(stk, "NTq%d" % i + tg, [128, 4, 128], BF16) for i in range(2)]
                    Xq2 = [sbs(stk, "Xq%d" % i + tg, [128, 4, 128], BF16) for i in range(2)]
                    XTq2 = [sbs(stk, "XTq%d" % i + tg, [128, 4, 128], BF16) for i in range(2)]
                    Pq2 = [[sbs(stk, "Pq%d_%d" % (i, j) + tg, [128, 4, 128], BF16) for j in range(2)] for i in range(2)]
                    PTq2 = [[sbs(stk, "PTq%d_%d" % (i, j) + tg, [128, 4, 128], BF16) for j in range(2)] for i in range(2)]
                    tq2 = [sbs(stk, "tq%d" % i + tg, [128, 4, 128], F32) for i in range(2)]
                    Vp2 = [sbs(stk, "Vp%d" % i + tg, [128, 4, 128], BF16) for i in range(2)]
                    ub2 = [sbs(stk, "ub%d" % i + tg, [128, 4, 128], BF16) for i in range(2)]
                    bd16 = sbs(stk, "bd16" + tg, [128, 4, 128], BF16)
                    offmT = [sbs(stk, "offmT%d" % i + tg, [128, 4, 128], BF16) for i in range(3)]
                    S.dma("sp", lambda e: e.dma_start(out=bd16[:], in_=c_bd16), [], ["bd16"])
                    for i in range(3):
                        S.dma("sp", lambda e, i=i: e.dma_start(out=offmT[i][:], in_=c_offT[i]), [], ["offmT"])
                    Xall = sbs(stk, "Xall" + tg, [128, 16, 128], BF16)
                    Kdec = sbs(stk, "Kdec" + tg, [128, 16, 128], BF16)
                    Sst = sbs(stk, "Sst" + tg, [128, 16, 128], F32)
                    Sbf = sbs(stk, "Sbf" + tg, [128, 16, 128], BF16)
                    o32 = sbs(stk, "o32" + tg, [128, 16, 128], F32)
                    ssq16 = sbs(stk, "ssq16" + tg, [128, 16], F32)
                    og = sbs(stk, "og" + tg, [128, 2 * D], BF16)
                    gdn_core_setup(l2)
                    S.op("pool", lambda e: e.memset(Sst[:], 0.0), [], ["Sst"])
                    S.op("pool", lambda e: e.memset(Sbf[:], 0.0), [], ["Sbf"])
                    for n in range(NT):
                        gdn_core_tile(l2, n)
                    S.dma("sp", lambda e: e.dma_start(out=sp_out[l2].rearrange("h k v -> k h v"), in_=Sst[:]), ["Sst"], ["sp_out"])
                    if do_sample:
                        sample_gdn_core(l2)
                    S.barrier()
                    S.flush()

        S.finish()
        S.flush()
    return nc


def _consts():
    bf = ml_dtypes.bfloat16
    i = np.arange(128)
    r = {}
    r["c_ident"] = np.eye(128).astype(bf)
    s, q = i[:, None], i[None, :]
    r["c_mown"] = np.repeat((q >= s)[:, None, :], 4, axis=1).astype(bf)
    r["c_mprev"] = np.repeat((s >= q)[:, None, :], 4, axis=1).astype(bf)
    r["c_lmask"] = (s > q).astype(np.float32)
    r["c_uinc"] = (s <= q).astype(np.float32)
    r["c_mincl"] = np.repeat((q >= s)[:, None, :], 4, axis=1).astype(bf)
    r["c_mstrict"] = np.repeat((q > s)[:, None, :], 4, axis=1).astype(bf)
    r["c_ident4"] = np.repeat((q == s)[:, None, :], 4, axis=1).astype(bf)
    r["c_bd16"] = np.repeat(((q // 16) == (s // 16))[:, None, :], 4, axis=1).astype(bf)
    offs = []
    for sz in (16, 32, 64):
        m = ((s // (2 * sz)) == (q // (2 * sz))) & ((s // sz) % 2 == 0) & ((q // sz) % 2 == 1)
        offs.append(m)
    r["c_off"] = np.stack([np.repeat(m[:, None, :], 4, axis=1) for m in offs]).astype(bf)
    r["c_offT"] = np.stack([np.repeat(m.T[:, None, :], 4, axis=1) for m in offs]).astype(bf)
    return {k: np.ascontiguousarray(v) for k, v in r.items()}


def _swa_cols():
    cols = []
    for jp in range(2):
        for g in range(4):
            a = 4 * (2 * jp) + g
            b = 4 * (2 * jp + 1) + g
            cols += list(range(a * 64, (a + 1) * 64)) + list(range(b * 64, (b + 1) * 64))
    cols += list(range(1024, 1280))
    cols += list(range(1024, 1536))
    cols += list(range(1536, 2560))
    return np.array(cols)


def prep_in_maps(inp, T, cores):
    f = np.float32
    shared = dict(_consts())
    shared["norm_g"] = np.ascontiguousarray(inp["norm_g"], f)
    shared["w_mod"] = np.ascontiguousarray(inp["w_mod"], f)
    shared["b_mod"] = np.ascontiguousarray(inp["b_mod"], f)
    shared["swa_w_in"] = np.ascontiguousarray(np.asarray(inp["swa_w_in"], f)[:, :, _swa_cols()])
    shared["swa_sinks"] = np.ascontiguousarray(inp["swa_sinks"], f)
    shared["swa_w_out"] = np.ascontiguousarray(inp["swa_w_out"], f)
    shared["gdn_w_in"] = np.ascontiguousarray(inp["gdn_w_in"], f)
    cw = np.asarray(inp["gdn_conv_w"], f)
    shared["gdn_conv_w"] = np.ascontiguousarray(cw)
    shared["gdn_conv_wT"] = np.ascontiguousarray(cw.reshape(2, 4, 32, 128).transpose(0, 3, 2, 1))
    shared["gdn_a_log"] = np.ascontiguousarray(inp["gdn_a_log"], f)
    shared["gdn_dt_bias"] = np.ascontiguousarray(inp["gdn_dt_bias"], f)
    shared["gdn_o_norm_g"] = np.ascontiguousarray(inp["gdn_o_norm_g"], f)
    shared["gdn_w_out"] = np.ascontiguousarray(inp["gdn_w_out"], f)
    shared["final_norm_g"] = np.ascontiguousarray(inp["final_norm_g"], f)
    maps = []
    for c in cores:
        b = c // 4
        sl = slice(NSB * c, NSB * (c + 1))
        m = dict(shared)
        m["xp"] = np.ascontiguousarray(np.asarray(inp["x_prompt"])[b, :T], f)
        m["xs"] = np.ascontiguousarray(np.asarray(inp["x_sample"])[sl].reshape(NS, D), f)
        m["ctok"] = np.ascontiguousarray(np.concatenate([np.asarray(inp["c_sample"])[sl], np.asarray(inp["c_prompt"])[b:b + 1]], 0), f)
        m["cache_k"] = np.ascontiguousarray(np.asarray(inp["cache_swa_k"])[:, sl].reshape(2, NSB, 128, 256), f)
        m["cache_v"] = np.ascontiguousarray(np.asarray(inp["cache_swa_v"])[:, sl].reshape(2, NSB, 128, 256), f)
        m["conv_state"] = np.ascontiguousarray(np.asarray(inp["state_gdn_conv"])[:, sl], f)
        m["s_state"] = np.ascontiguousarray(np.asarray(inp["state_gdn_s"])[:, sl], f)
        maps.append(m)
    return maps


_NC_CACHE = {}


def kernel(**inputs):
    T = 8192
    key = (T, 4)
    if key not in _NC_CACHE:
        _NC_CACHE[key] = build(T, 4, True)
    nc = _NC_CACHE[key]
    cores = list(range(8))
    maps = prep_in_maps(inputs, T, cores)
    res = run_bass_kernel_spmd(nc, maps, core_ids=cores)
    R = res.results
    f = np.float32
    y_prompt = np.stack([R[0]["y_p"], R[4]["y_p"]]).astype(f)
    y_sample = np.concatenate([R[c]["y_s"].reshape(NSB, 4, D) for c in range(8)], 0).astype(f)
    kp = np.stack([R[0]["kp_out"], R[4]["kp_out"]], 1).reshape(2, 2, 128, 4, 64).astype(f)
    vp = np.stack([R[0]["vp_out"], R[4]["vp_out"]], 1).reshape(2, 2, 128, 4, 64).astype(f)
    ks = np.concatenate([R[c]["ks_out"] for c in range(8)], 1).reshape(2, 128, 128, 4, 64).astype(f)
    vs = np.concatenate([R[c]["vs_out"] for c in range(8)], 1).reshape(2, 128, 128, 4, 64).astype(f)
    convp = np.stack([R[0]["convp_out"], R[4]["convp_out"]], 1).astype(f)
    sp = np.stack([R[0]["sp_out"], R[4]["sp_out"]], 1).astype(f)
    convs = np.concatenate([R[c]["convs_out"] for c in range(8)], 1).astype(f)
    ssn = np.concatenate([R[c]["ss_out"] for c in range(8)], 1).astype(f)
    return (y_prompt, y_sample, kp, vp, ks, vs, convp, sp, convs, ssn)
```

```python
import contextlib
import numpy as np
import ml_dtypes
import concourse.bass as bass
import concourse.mybir as mybir
from concourse.bass_utils import run_bass_kernel_spmd

F32 = mybir.dt.float32
BF16 = mybir.dt.bfloat16
AF = mybir.ActivationFunctionType
ALU = mybir.AluOpType
AX = mybir.AxisListType

D = 1024
EPS = 1e-6
NSB = 16
NS = 64
SWA_W = 2816
GDN_W = 6176


class Sched:
    ENGS = ["sp", "act", "dve", "pool", "pe"]

    def __init__(self, nc, ndma=32):
        self.nc = nc
        self.q = {e: [] for e in self.ENGS}
        self.cnt = {e: 0 for e in self.ENGS}
        self.waited = {e: {} for e in self.ENGS}
        self.lastw = {}
        self.readers = {}
        self.ndma = ndma
        self.dma_val = [0] * ndma
        self.dma_rr = {"sp": 0, "pool": 0}
        self.nops = 0
        import os as _os
        self.limit = int(_os.environ["KSTOP"]) if _os.environ.get("KSTOP") else None
        self.groups = {}

    def _x(self, keys):
        out = []
        for k in keys:
            out.extend(self.groups.get(k, (k,)))
        return out

    def _need(self, eng, tok):
        semkey, val = tok
        if semkey == ("e", "pe") and eng == "pe":
            return
        if self.waited[eng].get(semkey, 0) >= val:
            return
        self.waited[eng][semkey] = val
        self.q[eng].append(("wait", semkey, val))

    def _deps(self, eng, reads, writes):
        for k in reads:
            t = self.lastw.get(k)
            if t is not None:
                self._need(eng, t)
        for k in writes:
            t = self.lastw.get(k)
            if t is not None:
                self._need(eng, t)
            for t in self.readers.get(k, ()):
                self._need(eng, t)

    def _commit(self, tok, reads, writes):
        for k in writes:
            self.lastw[k] = tok
            self.readers[k] = []
        for k in reads:
            if k not in writes:
                self.readers.setdefault(k, []).append(tok)

    def op(self, eng, fn, reads=(), writes=(), inc=True):
        if self.limit is not None and self.nops >= self.limit:
            return
        reads, writes = self._x(reads), self._x(writes)
        self._deps(eng, reads, writes)
        if inc:
            self.cnt[eng] += 1
            tok = (("e", eng), self.cnt[eng])
            self.q[eng].append(("op", fn, ("e", eng), 1))
        else:
            tok = (("e", eng), self.cnt[eng] + 1)
            self.q[eng].append(("op", fn, None, 0))
        self._commit(tok, reads, writes)
        self.nops += 1
        self.flush()

    def dma(self, eng, fn, reads=(), writes=()):
        if self.limit is not None and self.nops >= self.limit:
            return
        reads, writes = self._x(reads), self._x(writes)
        half = self.ndma // 2
        base = 0 if eng == "sp" else half
        idx = base + self.dma_rr[eng]
        self.dma_rr[eng] = (self.dma_rr[eng] + 1) % half
        if self.dma_val[idx] > 0:
            self._need(eng, (("d", idx), self.dma_val[idx]))
        self._deps(eng, reads, writes)
        self.dma_val[idx] += 16
        tok = (("d", idx), self.dma_val[idx])
        self.q[eng].append(("op", fn, ("d", idx), 16))
        self._commit(tok, reads, writes)
        self.nops += 1
        self.flush()

    def barrier(self):
        for e in self.ENGS:
            for f in self.ENGS:
                if f != e and self.cnt[f] > 0:
                    self._need(e, (("e", f), self.cnt[f]))
            for idx in range(self.ndma):
                if self.dma_val[idx] > 0:
                    self._need(e, (("d", idx), self.dma_val[idx]))

    def finish(self):
        for idx in range(self.ndma):
            if self.dma_val[idx] > 0:
                self._need("sp", (("d", idx), self.dma_val[idx]))
        for e in self.ENGS:
            if e != "sp" and self.cnt[e] > 0:
                self._need("sp", (("e", e), self.cnt[e]))

    def attach(self, st):
        nc = self.nc
        self.sems = {}
        for e in self.ENGS:
            self.sems[("e", e)] = st.enter_context(nc.semaphore("s_" + e))
        for i in range(self.ndma):
            self.sems[("d", i)] = st.enter_context(nc.semaphore("d_%d" % i))
        self.eng = {"sp": nc.sync, "act": nc.scalar, "dve": nc.vector, "pool": nc.gpsimd, "pe": nc.tensor}

    def flush(self):
        for eng in self.ENGS:
            for item in self.q[eng]:
                if item[0] == "wait":
                    self.eng[eng].wait_ge(self.sems[item[1]], item[2])
                else:
                    ins = item[1](self.eng[eng])
                    if item[2] is not None:
                        ins.then_inc(self.sems[item[2]], item[3])
            self.q[eng].clear()


def build(T, depth, do_sample=True):
    NT = T // 128
    n_swa = (depth + 1) // 2
    n_gdn = depth // 2
    nc = bass.Bass("TRN2", target_bir_lowering=False)

    def din(name, shape, dt=F32):
        return nc.dram_tensor(name, list(shape), dt, kind="ExternalInput").ap()

    def dout(name, shape):
        return nc.dram_tensor(name, list(shape), F32, kind="ExternalOutput").ap()

    def dscr(name, shape, dt):
        return nc.dram_tensor(name, list(shape), dt).ap()

    xp = din("xp", [T, D])
    xs_in = din("xs", [NS, D])
    ctok = din("ctok", [17, D])
    cache_k = din("cache_k", [2, NSB, 128, 256])
    cache_v = din("cache_v", [2, NSB, 128, 256])
    conv_state = din("conv_state", [2, NSB, 3, 4096])
    s_state = din("s_state", [2, NSB, 16, 128, 128])
    norm_g = din("norm_g", [4, D])
    w_mod = din("w_mod", [4, D, 3 * D])
    b_mod = din("b_mod", [4, 3 * D])
    swa_w_in = din("swa_w_in", [2, D, SWA_W])
    swa_sinks = din("swa_sinks", [2, 16])
    swa_w_out = din("swa_w_out", [2, D, D])
    gdn_w_in = din("gdn_w_in", [2, D, GDN_W])
    gdn_conv_w = din("gdn_conv_w", [2, 4, 4096])
    gdn_conv_wT = din("gdn_conv_wT", [2, 128, 32, 4])
    gdn_a_log = din("gdn_a_log", [2, 16])
    gdn_dt_bias = din("gdn_dt_bias", [2, 16])
    gdn_o_norm_g = din("gdn_o_norm_g", [2, 128])
    gdn_w_out = din("gdn_w_out", [2, 2 * D, D])
    final_norm_g = din("final_norm_g", [D])
    c_ident = din("c_ident", [128, 128], BF16)
    c_mown = din("c_mown", [128, 4, 128], BF16)
    c_mprev = din("c_mprev", [128, 4, 128], BF16)
    c_lmask = din("c_lmask", [128, 128])
    c_uinc = din("c_uinc", [128, 128])
    c_mincl = din("c_mincl", [128, 4, 128], BF16)
    c_mstrict = din("c_mstrict", [128, 4, 128], BF16)
    c_ident4 = din("c_ident4", [128, 4, 128], BF16)
    c_bd16 = din("c_bd16", [128, 4, 128], BF16)
    c_off = din("c_off", [3, 128, 4, 128], BF16)
    c_offT = din("c_offT", [3, 128, 4, 128], BF16)

    y_p = dout("y_p", [T, D])
    y_s = dout("y_s", [NS, D])
    kp_out = dout("kp_out", [2, 128, 256])
    vp_out = dout("vp_out", [2, 128, 256])
    ks_out = dout("ks_out", [2, NSB, 128, 256])
    vs_out = dout("vs_out", [2, NSB, 128, 256])
    convp_out = dout("convp_out", [2, 3, 4096])
    sp_out = dout("sp_out", [2, 16, 128, 128])
    convs_out = dout("convs_out", [2, NSB, 3, 4096])
    ss_out = dout("ss_out", [2, NSB, 16, 128, 128])

    x_scr = dscr("x_scr", [T, D], F32)
    o_scr = dscr("o_scr", [T, 2 * D], BF16)
    qkv_scr = dscr("qkv_scr", [NT + 1, 128, 32 * 128], BF16)
    z_scr = dscr("z_scr", [T + 128, 2 * D], BF16)
    sproj_scr = dscr("sproj_scr", [NS, 4096], F32)
    xpad_scr = dscr("xpad_scr", [NSB, 7, 4096], F32)
    kv_s_scr = dscr("kv_s_scr", [NS, 24 * 128], BF16)
    gb_s_scr = dscr("gb_s_scr", [NS, 32], F32)
    osn_scr = dscr("osn_scr", [NS, 2 * D], F32)
    vs_scr = dscr("vs_scr", [NS, 256], BF16)

    S = Sched(nc)

    with contextlib.ExitStack() as st:
        S.attach(st)
        def sb(name, shape, dt):
            return st.enter_context(nc.sbuf_tensor(name, list(shape), dt))

        def ps(name, shape, dt):
            return st.enter_context(nc.psum_tensor(name, list(shape), dt))

        ident = sb("ident", [128, 128], BF16)
        mown = sb("mown", [128, 4, 128], BF16)
        mprev = sb("mprev", [128, 4, 128], BF16)
        lmask = sb("lmask", [128, 128], F32)
        uinc = sb("uinc", [128, 128], F32)
        mincl = sb("mincl", [128, 4, 128], BF16)
        mstrict = sb("mstrict", [128, 4, 128], BF16)
        ident4 = sb("ident4", [128, 4, 128], BF16)
        ones_f = sb("ones_f", [128, 128], F32)
        ones_b = sb("ones_b", [128, 128], BF16)
        for (t_, d_, k_) in [(ident, c_ident, "ident"), (mown, c_mown, "mown"), (mprev, c_mprev, "mprev"),
                             (lmask, c_lmask, "lmask"), (uinc, c_uinc, "uinc"), (mincl, c_mincl, "mincl"),
                             (mstrict, c_mstrict, "mstrict"), (ident4, c_ident4, "ident4")]:
            S.dma("sp", lambda e, t_=t_, d_=d_: e.dma_start(out=t_[:], in_=d_), [], [k_])
        S.op("pool", lambda e: e.memset(ones_f[:], 1.0), [], ["ones_f"])
        S.op("pool", lambda e: e.memset(ones_b[:], 1.0), [], ["ones_b"])

        pbT = ps("pbT", [128, 1024], BF16)
        pb = [ps("pb%d" % i, [128, 512], F32) for i in range(7)]

        xt = [sb("xt%d" % i, [128, D], F32) for i in range(2)]
        junk = sb("junk", [128, D], F32)
        ss = sb("ss", [128, 1], F32)
        rstd = sb("rstd", [128, 1], F32)
        h32 = sb("h32", [128, D], F32)
        hb = sb("hb", [128, D], BF16)
        hT2 = [sb("hT%d" % i, [128, 8, 128], BF16) for i in range(2)]
        modP = sb("modP", [128, 2 * D], F32)
        gateP = [sb("gateP%d" % i, [128, D], F32) for i in range(2)]
        normg = sb("normg", [128, D], F32)
        wmodb = sb("wmodb", [128, 8, 512], BF16)
        bmodb = sb("bmodb", [128, 512], F32)
        cT17 = sb("cT17", [128, 8, 17], BF16)
        cTp = sb("cTp", [128, 8, 128], BF16)
        cTs = sb("cTs", [128, 8, NS], BF16)
        ogT = sb("ogT", [128, 16, 128], BF16)
        tmpo = sb("tmpo", [128, 512], F32)
        xs = sb("xs_res", [NS, D], F32)
        og_s = sb("og_s", [NS, 2 * D], BF16)
        kv32 = sb("kv32", [128, 512], F32)
        zs = sb("zs", [128, 2 * D], BF16)
        expsink = sb("expsink", [128, 16], F32)
        den = sb("den", [128, 4], F32)
        negA = sb("negA", [128, 16], F32)
        dtb = sb("dtb", [128, 16], F32)
        ab_t = sb("ab_t", [128, 32], F32)
        gbt = sb("gbt", [128, 32], F32)
        mods_scr = dscr("mods_scr", [4, NS, 3 * D], F32)
        gb_scr = dscr("gb_scr", [T + 128, 32], F32)
        zs_s_scr = dscr("zs_s_scr", [NS, 2 * D], BF16)

        _cst = contextlib.ExitStack()
        c17 = _cst.enter_context(nc.sbuf_tensor("c17", [17, D], F32))
        c17b = _cst.enter_context(nc.sbuf_tensor("c17b", [17, D], BF16))
        S.dma("sp", lambda e: e.dma_start(out=c17[:], in_=ctok), [], ["c17"])
        S.op("act", lambda e: e.activation(out=c17b[:], in_=c17[:], func=AF.Silu), ["c17"], ["c17b"])
        for c in range(8):
            S.op("pe", lambda e, c=c: e.transpose(out=pbT[:, c * 32:c * 32 + 17], in_=c17b[:, c * 128:(c + 1) * 128],
                                                   identity=ident[0:17, 0:17]), ["c17b", "ident"], ["pbT"])
        S.op("dve", lambda e: e.tensor_copy(out=cT17[:], in_=pbT[:, 0:256].rearrange("p (c m) -> p c m", m=32)[:, :, 0:17]),
             ["pbT"], ["cT17"])
        S.op("dve", lambda e: e.tensor_copy(out=cTp[:], in_=cT17[:, :, 16:17].to_broadcast([128, 8, 128])), ["cT17"], ["cTp"])
        S.op("dve", lambda e: e.tensor_copy(out=cTs[:].rearrange("p c (b t) -> p c b t", t=4),
                                            in_=cT17[:, :, 0:16].unsqueeze(3).to_broadcast([128, 8, 16, 4])), ["cT17"], ["cTs"])

        S.barrier()
        _cst.close()

        def modulation(l):
            S.dma("sp", lambda e: e.dma_start(out=normg[:], in_=norm_g[l].partition_broadcast(128)), [], ["normg"])
            for gi in range(6):
                S.dma("pool", lambda e, gi=gi: e.dma_start(
                    out=wmodb[:], in_=w_mod[l, :, gi * 512:(gi + 1) * 512].rearrange("(c p) n -> p c n", p=128)), [], ["wmodb"])
                S.dma("sp", lambda e, gi=gi: e.dma_start(out=bmodb[:], in_=b_mod[l, gi * 512:(gi + 1) * 512].partition_broadcast(128)),
                      [], ["bmodb"])
                if gi < 4:
                    dst, dk_ = modP[:, gi * 512:(gi + 1) * 512], "modP"
                else:
                    dst, dk_ = gateP[l % 2][:, (gi - 4) * 512:(gi - 3) * 512], "gateP%d" % (l % 2)
                for c in range(8):
                    S.op("pe", lambda e, c=c: e.matmul(pb[0][:, :], lhsT=cTp[:, c, :], rhs=wmodb[:, c, :], start=(c == 0), stop=(c == 7)),
                         ["cTp", "wmodb"], ["pb0"])
                S.op("dve", lambda e, dst=dst: e.tensor_tensor(out=dst, in0=pb[0][:, :], in1=bmodb[:, :], op=ALU.add),
                     ["pb0", "bmodb"], [dk_])
                if gi in (2, 3):
                    S.op("dve", lambda e, dst=dst, gi=gi: e.scalar_tensor_tensor(
                        out=dst, in0=dst, scalar=1.0, in1=normg[:, (gi - 2) * 512:(gi - 1) * 512], op0=ALU.add, op1=ALU.mult),
                        [dk_, "normg"], [dk_])
                if do_sample:
                    for c in range(8):
                        S.op("pe", lambda e, c=c: e.matmul(pb[1][0:NS, :], lhsT=cTs[:, c, :], rhs=wmodb[:, c, :], start=(c == 0), stop=(c == 7)),
                             ["cTs", "wmodb"], ["pb1"])
                    S.op("dve", lambda e: e.tensor_tensor(out=tmpo[0:NS, :], in0=pb[1][0:NS, :], in1=bmodb[0:NS, :], op=ALU.add),
                         ["pb1", "bmodb"], ["tmpo"])
                    if gi in (2, 3):
                        S.op("dve", lambda e, gi=gi: e.scalar_tensor_tensor(
                            out=tmpo[0:NS, :], in0=tmpo[0:NS, :], scalar=1.0, in1=normg[0:NS, (gi - 2) * 512:(gi - 1) * 512],
                            op0=ALU.add, op1=ALU.mult), ["tmpo", "normg"], ["tmpo"])
                    S.dma("sp", lambda e, gi=gi: e.dma_start(out=mods_scr[l, :, gi * 512:(gi + 1) * 512], in_=tmpo[0:NS, :]),
                          ["tmpo"], ["mods_scr%d" % l])

        def load_w(dst, dkey, src, kc, width):
            S.groups[dkey] = ["%s.%d" % (dkey, c) for c in range(kc)]
            for c in range(kc):
                S.dma("pool", lambda e, c=c: e.dma_start(out=dst[:, c, 0:width], in_=src[c * 128:(c + 1) * 128, :]), [],
                      ["%s.%d" % (dkey, c)])

        def norm_to_hT(xin, xkey, M, mod, mkey, hT, hTk):
            S.op("act", lambda e: e.activation(out=junk[0:M, 0:D], in_=xin[0:M, :], func=AF.Square, accum_out=ss[0:M, :]),
                 [xkey], ["junk", "ss"])
            S.op("act", lambda e: e.activation(out=rstd[0:M, :], in_=ss[0:M, :], func=AF.Sqrt, scale=1.0 / D, bias=EPS),
                 ["ss"], ["rstd"])
            S.op("dve", lambda e: e.reciprocal(out=rstd[0:M, :], in_=rstd[0:M, :]), ["rstd"], ["rstd"])
            S.op("dve", lambda e: e.scalar_tensor_tensor(out=h32[0:M, :], in0=xin[0:M, :], scalar=rstd[0:M, :],
                                                         in1=mod[0:M, D:2 * D], op0=ALU.mult, op1=ALU.mult),
                 [xkey, "rstd", mkey], ["h32"])
            S.op("dve", lambda e: e.tensor_tensor(out=hb[0:M, :], in0=h32[0:M, :], in1=mod[0:M, 0:D], op=ALU.add),
                 ["h32", mkey], ["hb"])
            for c in range(8):
                S.op("pe", lambda e, c=c: e.transpose(out=pbT[:, c * 128:c * 128 + M], in_=hb[0:M, c * 128:(c + 1) * 128],
                                                       identity=ident[0:M, 0:M]), ["hb", "ident"], ["pbT"], inc=(c == 7))
            S.op("act", lambda e: e.copy(out=hT[:, :, 0:M], in_=pbT[:].rearrange("p (c m) -> p c m", m=128)[:, :, 0:M]),
                 ["pbT"], [hTk])

        def final_norm(xin, xkey, M, dst_ap, dkey, fng):
            S.op("act", lambda e: e.activation(out=junk[0:M, 0:D], in_=xin[0:M, :], func=AF.Square, accum_out=ss[0:M, :]),
                 [xkey], ["junk", "ss"])
            S.op("act", lambda e: e.activation(out=rstd[0:M, :], in_=ss[0:M, :], func=AF.Sqrt, scale=1.0 / D, bias=EPS),
                 ["ss"], ["rstd"])
            S.op("dve", lambda e: e.reciprocal(out=rstd[0:M, :], in_=rstd[0:M, :]), ["rstd"], ["rstd"])
            S.op("dve", lambda e: e.scalar_tensor_tensor(out=h32[0:M, :], in0=xin[0:M, :], scalar=rstd[0:M, :],
                                                         in1=fng[0:M, :], op0=ALU.mult, op1=ALU.mult),
                 [xkey, "rstd", "normg"], ["h32"])
            S.dma("sp", lambda e: e.dma_start(out=dst_ap, in_=h32[0:M, :]), ["h32"], [dkey])

        def out_proj(og, ogkey, M, kc, xin, xkey, gate, mkey):
            for c in range(kc):
                S.op("pe", lambda e, c=c: e.transpose(out=pbT[:, (c % 8) * 128:(c % 8) * 128 + M],
                                                       in_=og[0:M, c * 128:(c + 1) * 128], identity=ident[0:M, 0:M]),
                     [ogkey, "ident"], ["pbT"], inc=(c % 8 == 7))
                if c % 8 == 7:
                    c0 = c - 7
                    S.op("act", lambda e, c0=c0: e.copy(out=ogT[:, c0:c0 + 8, 0:M],
                                                        in_=pbT[:].rearrange("p (c m) -> p c m", m=128)[:, :, 0:M]),
                         ["pbT"], ["ogT"])
            for gi in range(2):
                bank, bk = pb[gi], "pb%d" % gi
                for c in range(kc):
                    S.op("pe", lambda e, c=c, gi=gi, bank=bank: e.matmul(
                        bank[0:M, :], lhsT=ogT[:, c, 0:M], rhs=w_out_sb[:, c, gi * 512:(gi + 1) * 512],
                        start=(c == 0), stop=(c == kc - 1)), ["ogT", "w_out_sb"], [bk], inc=(c == kc - 1))
                S.op("dve", lambda e, gi=gi, bank=bank: e.tensor_tensor(
                    out=tmpo[0:M, :], in0=bank[0:M, :], in1=gate[0:M, gi * 512:(gi + 1) * 512], op=ALU.mult),
                    [bk, mkey], ["tmpo"])
                S.op("dve", lambda e, gi=gi: e.tensor_tensor(
                    out=xin[0:M, gi * 512:(gi + 1) * 512], in0=xin[0:M, gi * 512:(gi + 1) * 512], in1=tmpo[0:M, :], op=ALU.add),
                    ["tmpo", xkey], [xkey])

        def swa_attend(nq, nprev, nown, q_ap, kprev_ap, kown_ap, vprev_ap, vown_ap, rk, j, out_ap, okey):
            N = 4 * nq
            par = j % 2
            bo, bp, bO = (pb[2], pb[3], pb[4]) if par == 0 else (pb[0], pb[1], pb[5])
            bok, bpk, bOk = ("pb2", "pb3", "pb4") if par == 0 else ("pb0", "pb1", "pb5")
            e_own, e_prev = e_own2[par], e_prev2[par]
            eok, epk = "e_own%d" % par, "e_prev%d" % par
            S.op("pe", lambda e: e.matmul(bo[0:nown, 0:N], lhsT=kown_ap, rhs=q_ap, start=True, stop=True), rk, [bok])
            S.op("act", lambda e: e.activation(out=e_own[0:nown, :, 0:nq], in_=bo[0:nown, 0:N].rearrange("p (g q) -> p g q", g=4),
                                               func=AF.Exp, scale=0.125), [bok], [eok])
            S.op("dve", lambda e: e.tensor_tensor(out=e_own[0:nown, :, 0:nq], in0=e_own[0:nown, :, 0:nq],
                                                   in1=mown[0:nown, :, 0:nq], op=ALU.mult), [eok, "mown"], [eok])
            if nprev:
                S.op("pe", lambda e: e.matmul(bp[0:nprev, 0:N], lhsT=kprev_ap, rhs=q_ap, start=True, stop=True), rk, [bpk])
                S.op("act", lambda e: e.activation(out=e_prev[0:nprev, :, 0:nq],
                                                   in_=bp[0:nprev, 0:N].rearrange("p (g q) -> p g q", g=4),
                                                   func=AF.Exp, scale=0.125), [bpk], [epk])
                S.op("dve", lambda e: e.tensor_tensor(out=e_prev[0:nprev, :, 0:nq], in0=e_prev[0:nprev, :, 0:nq],
                                                       in1=mprev[0:nprev, :, 0:nq], op=ALU.mult), [epk, "mprev"], [epk])
            for g in range(4):
                if nprev:
                    S.op("pe", lambda e, g=g: e.matmul(bO[0:nq, g * 65:(g + 1) * 65], lhsT=e_prev[0:nprev, g, 0:nq], rhs=vprev_ap,
                                                       start=True, stop=False), [epk] + rk, [bOk], inc=False)
                S.op("pe", lambda e, g=g: e.matmul(bO[0:nq, g * 65:(g + 1) * 65], lhsT=e_own[0:nown, g, 0:nq], rhs=vown_ap,
                                                   start=(not nprev), stop=True), [eok] + rk, [bOk], inc=(g == 3))
            O3 = bO[0:nq, 0:260].rearrange("p (g d) -> p g d", d=65)
            S.op("dve", lambda e: e.tensor_tensor(out=den[0:nq, :], in0=O3[:, :, 64], in1=expsink[0:nq, j * 4:(j + 1) * 4], op=ALU.add),
                 [bOk, "expsink"], ["den"])
            S.op("dve", lambda e: e.reciprocal(out=den[0:nq, :], in_=den[0:nq, :]), ["den"], ["den"])
            S.op("dve", lambda e: e.tensor_tensor(out=out_ap, in0=O3[:, :, 0:64],
                                                  in1=den[0:nq, :].unsqueeze(2).to_broadcast([nq, 4, 64]), op=ALU.mult),
                 [bOk, "den"], [okey])

        def swa_layer_setup(l2):
            S.dma("sp", lambda e: e.dma_start(out=expsink[:], in_=swa_sinks[l2].partition_broadcast(128)), [], ["expsink"])
            S.op("act", lambda e: e.activation(out=expsink[:], in_=expsink[:], func=AF.Exp), ["expsink"], ["expsink"])

        def swa_prompt_tile(l2, n, hT, hTk, sample=False):
            cur, prv = n % 2, (n + 1) % 2
            if sample:
                cur, prv = 0, 1
            qk, qkk = qkT[cur], "qkT%d" % cur
            for ch in range(10):
                bank = pb[ch // 4 % 2]
                bk = "pb%d" % (ch // 4 % 2)
                sl = slice((ch % 4) * 128, (ch % 4 + 1) * 128)
                for c in range(8):
                    S.op("pe", lambda e, c=c, ch=ch, bank=bank, sl=sl: e.matmul(
                        bank[:, sl], lhsT=w_in_sb[:, c, ch * 128:(ch + 1) * 128], rhs=hT[:, c, :],
                        start=(c == 0), stop=(c == 7)), [hTk, "w_in_sb"], [bk], inc=(c == 7))
                if ch % 4 == 3 or ch == 9:
                    c0 = ch - (ch % 4)
                    nn = ch - c0 + 1
                    S.op("act", lambda e, c0=c0, nn=nn, bank=bank: e.copy(
                        out=qk[:, c0:c0 + nn, :], in_=bank[:, 0:nn * 128].rearrange("p (c m) -> p c m", m=128)),
                        [bk], [qkk])
                    yield
            for c in range(8):
                S.op("pe", lambda e, c=c: e.matmul(pb[5][:, :], lhsT=hT[:, c, :], rhs=w_in_sb[:, c, 1280:1792],
                                                   start=(c == 0), stop=(c == 7)), [hTk, "w_in_sb"], ["pb5"], inc=(c == 7))
            S.op("dve", lambda e: e.tensor_copy(out=vext[cur][:, :, 0:64], in_=pb[5][:, 256:512].rearrange("p (j d) -> p j d", d=64)),
                 ["pb5"], ["vext%d" % cur])
            if sample:
                S.op("dve", lambda e: e.tensor_copy(out=kv32[:], in_=pb[5][:, :]), ["pb5"], ["kv32"])
                S.dma("sp", lambda e: e.dma_start(out=sproj_scr[:, 0:512], in_=kv32[0:NS, :]), ["kv32"], ["sproj_scr"])
                for (dst_, src_, c0_) in [(ks_out, cache_k, 0), (vs_out, cache_v, 256)]:
                    S.dma("sp", lambda e, dst_=dst_, c0_=c0_: e.dma_start(out=dst_[l2][:, 124:128, :], in_=sproj_scr[:, c0_:c0_ + 256].rearrange("(b t) c -> b t c", t=4)), ["sproj_scr"], ["ksvs_out"])
                    S.dma("sp", lambda e, dst_=dst_, src_=src_: e.dma_start(out=dst_[l2][:, 0:124, :], in_=src_[l2][:, 4:128, :]), [], ["ksvs_out2"])
            elif n == NT - 1:
                S.op("dve", lambda e: e.tensor_copy(out=kv32[:], in_=pb[5][:, :]), ["pb5"], ["kv32"])
                S.dma("sp", lambda e: e.dma_start(out=kp_out[l2], in_=kv32[:, 0:256]), ["kv32"], ["kp_out"])
                S.dma("sp", lambda e: e.dma_start(out=vp_out[l2], in_=kv32[:, 256:512]), ["kv32"], ["vp_out"])
            for gi in range(2):
                bank, bk = pb[gi], "pb%d" % gi
                for c in range(8):
                    S.op("pe", lambda e, c=c, gi=gi, bank=bank: e.matmul(
                        bank[:, :], lhsT=hT[:, c, :], rhs=w_in_sb[:, c, 1792 + gi * 512:1792 + (gi + 1) * 512],
                        start=(c == 0), stop=(c == 7)), [hTk, "w_in_sb"], [bk], inc=(c == 7))
                S.op("act", lambda e, gi=gi, bank=bank: e.activation(out=zs[:, gi * 512:(gi + 1) * 512], in_=bank[:, :], func=AF.Silu),
                     [bk], ["zs"])
                yield
            if sample:
                for b in range(NSB):
                    S.dma("pool", lambda e, b=b: e.dma_start(out=hb[:, 0:256], in_=cache_k[l2, b]), [], ["hb"])
                    for jp in range(2):
                        S.op("pe", lambda e, jp=jp: e.transpose(out=pbT[:, jp * 128:(jp + 1) * 128], in_=hb[:, jp * 128:(jp + 1) * 128], identity=ident[:]),
                             ["hb", "ident"], ["pbT"])
                    S.op("act", lambda e: e.copy(out=qkT[1][:, 8:10, :], in_=pbT[:, 0:256].rearrange("p (c m) -> p c m", m=128)), ["pbT"], ["qkT1"])
                    S.dma("pool", lambda e, b=b: e.dma_start(out=vext[1][:, :, 0:64], in_=cache_v[l2, b].rearrange("s (j d) -> s j d", d=64)), [], ["vext1"])
                    for c in range(8):
                        S.op("pe", lambda e, c=c, b=b: e.matmul(pb[5][0:4, 0:256], lhsT=hT[:, c, 4 * b:4 * b + 4], rhs=w_in_sb[:, c, 1536:1792],
                                                              start=(c == 0), stop=(c == 7)), [hTk, "w_in_sb"], ["pb5"], inc=(c == 7))
                    S.op("dve", lambda e: e.tensor_copy(out=vext[0][0:4, :, 0:64], in_=pb[5][0:4, 0:256].rearrange("p (j d) -> p j d", d=64)),
                         ["pb5"], ["vext0"])
                    for j in range(4):
                        jp, half = j // 2, j % 2
                        psl = slice(half * 64, half * 64 + 64)
                        swa_attend(4, 128, 4, qk[psl, jp * 4:jp * 4 + 4, 4 * b:4 * b + 4], qkT[1][psl, 8 + jp, :], qk[psl, 8 + jp, 4 * b:4 * b + 4],
                                   vext[1][:, j, :], vext[0][0:4, j, :], ["qkT0", "qkT1", "vext0", "vext1"], j,
                                   on32[0:4, j * 256:(j + 1) * 256].rearrange("p (g d) -> p g d", d=64), "on32")
                    S.dma("sp", lambda e, b=b: e.dma_start(out=h32[4 * b:4 * b + 4, :], in_=on32[0:4, :]), ["on32"], ["h32"])
                S.op("dve", lambda e: e.tensor_tensor(out=og_s[0:NS, 0:D], in0=h32[0:NS, :], in1=zs[0:NS, 0:D], op=ALU.mult),
                     ["h32", "zs"], ["og_s"])
                return
            for j in range(4):
                jp, half = j // 2, j % 2
                psl = slice(half * 64, half * 64 + 64)
                rk = [qkk, "qkT%d" % prv, "vext%d" % cur, "vext%d" % prv]
                swa_attend(128, 128 if n > 0 else 0, 128,
                           qk[psl, jp * 4:jp * 4 + 4, :], qkT[prv][psl, 8 + jp, :], qk[psl, 8 + jp, :],
                           vext[prv][:, j, :], vext[cur][:, j, :], rk, j,
                           on32[:, j * 256:(j + 1) * 256].rearrange("p (g d) -> p g d", d=64), "on32")
                yield
            S.op("dve", lambda e: e.tensor_tensor(out=og[:, 0:D], in0=on32[:, 0:D], in1=zs[:, 0:D], op=ALU.mult),
                 ["on32", "zs"], ["og"])
            S.dma("sp", lambda e: e.dma_start(out=o_scr[n * 128:(n + 1) * 128, 0:D], in_=og[:, 0:D]), ["og"], ["o_scr%d" % n])

        def load_x(n, src, skey):
            S.dma("sp", lambda e: e.dma_start(out=xt[n % 2][:], in_=src[n * 128:(n + 1) * 128, :]), [skey], ["xt%d" % (n % 2)])

        def load_o(n, width):
            S.dma("sp", lambda e: e.dma_start(out=ogl[n % 2][:, 0:width], in_=o_scr[n * 128:(n + 1) * 128, 0:width]),
                  ["o_scr%d" % n], ["ogl%d" % (n % 2)])

        def gdn_layer_setup(l2):
            S.dma("sp", lambda e: e.dma_start(out=negA[:], in_=gdn_a_log[l2].partition_broadcast(128)), [], ["negA"])
            S.op("act", lambda e: e.activation(out=negA[:], in_=negA[:], func=AF.Exp), ["negA"], ["negA"])
            S.op("dve", lambda e: e.tensor_scalar(out=negA[:], in0=negA[:], scalar1=-1.0, scalar2=None, op0=ALU.mult), ["negA"], ["negA"])
            S.dma("sp", lambda e: e.dma_start(out=dtb[:], in_=gdn_dt_bias[l2].partition_broadcast(128)), [], ["dtb"])

        def gdn_core_setup(l2):
            S.dma("sp", lambda e: e.dma_start(out=ong[:], in_=gdn_o_norm_g[l2].partition_broadcast(128)), [], ["ong"])
            S.dma("sp", lambda e: e.dma_start(out=cwT[:], in_=gdn_conv_wT[l2]), [], ["cwT"])

        def gates_from_psum(bank_ap, bk, M, dst_ap, dkey):
            S.op("dve", lambda e: e.tensor_tensor(out=ab_t[0:M, 0:16], in0=bank_ap[:, 0:16], in1=dtb[0:M, :], op=ALU.add),
                 [bk, "dtb"], ["ab_t"])
            S.op("act", lambda e: e.activation(out=ab_t[0:M, 0:16], in_=ab_t[0:M, 0:16], func=AF.Exp), ["ab_t"], ["ab_t"])
            S.op("act", lambda e: e.activation(out=ab_t[0:M, 0:16], in_=ab_t[0:M, 0:16], func=AF.Ln, bias=1.0), ["ab_t"], ["ab_t"])
            S.op("dve", lambda e: e.tensor_tensor(out=dst_ap[:, 0:16], in0=ab_t[0:M, 0:16], in1=negA[0:M, :], op=ALU.mult),
                 ["ab_t", "negA"], [dkey])
            S.op("act", lambda e: e.activation(out=dst_ap[:, 16:32], in_=bank_ap[:, 16:32], func=AF.Sigmoid), [bk], [dkey])

        def gdn_inproj_tile(l2, n, hT, hTk):
            xk = "qst"
            for ch in range(32):
                bank = pb[ch // 4 % 2]
                bk = "pb%d" % (ch // 4 % 2)
                sl = slice((ch % 4) * 128, (ch % 4 + 1) * 128)
                for c in range(8):
                    S.op("pe", lambda e, c=c, ch=ch, bank=bank, sl=sl: e.matmul(
                        bank[:, sl], lhsT=w_in_sb[:, c, ch * 128:(ch + 1) * 128], rhs=hT[:, c, :],
                        start=(c == 0), stop=(c == 7)), [hTk, "w_in_sb"], [bk], inc=(c == 7))
                if ch % 4 == 3:
                    c0 = ch - 3
                    eng = "act" if (ch // 4) % 2 == 0 else "dve"
                    if eng == "act":
                        S.op("act", lambda e, c0=c0, bank=bank: e.copy(
                            out=qst[:, c0:c0 + 4, :], in_=bank[:, :].rearrange("p (c m) -> p c m", m=128)), [bk], [xk])
                    else:
                        S.op("dve", lambda e, c0=c0, bank=bank: e.tensor_copy(
                            out=qst[:, c0:c0 + 4, :], in_=bank[:, :].rearrange("p (c m) -> p c m", m=128)), [bk], [xk])
                    if n == NT - 1:
                        S.op("dve", lambda e, bank=bank: e.tensor_copy(
                            out=kv32[:, 0:12].rearrange("p (c m) -> p c m", m=3),
                            in_=bank[:, :].rearrange("p (c m) -> p c m", m=128)[:, :, 125:128]), [bk], ["kv32"])
                        for cc in range(4):
                            S.dma("sp", lambda e, c0=c0, cc=cc: e.dma_start(
                                out=convp_out[l2, :, (c0 + cc) * 128:(c0 + cc + 1) * 128].rearrange("t p -> p t"),
                                in_=kv32[:, cc * 3:cc * 3 + 3], allow_slow_non_contiguous=True),
                                ["kv32"], ["convp_out"])
                    yield
            if n == NT:
                for gi in range(8):
                    bank, bk = pb[gi % 2], "pb%d" % (gi % 2)
                    for c in range(8):
                        S.op("pe", lambda e, c=c, gi=gi, bank=bank: e.matmul(
                            bank[0:NS, :], lhsT=hT[:, c, 0:NS], rhs=w_in_sb[:, c, gi * 512:(gi + 1) * 512],
                            start=(c == 0), stop=(c == 7)), [hTk, "w_in_sb"], [bk], inc=(c == 7))
                    S.op("dve", lambda e, bank=bank: e.tensor_copy(out=tmpo[0:NS, :], in_=bank[0:NS, :]), [bk], ["tmpo"])
                    S.dma("sp", lambda e, gi=gi: e.dma_start(out=sproj_scr[:, gi * 512:(gi + 1) * 512], in_=tmpo[0:NS, :]), ["tmpo"], ["sproj_scr"])
                S.dma("sp", lambda e: e.dma_start(out=convs_out[l2], in_=sproj_scr.rearrange("(b t) c -> b t c", t=4)[:, 1:4, :]),
                      ["sproj_scr"], ["convs_out"])
            S.dma("sp", lambda e: e.dma_start(out=qkv_scr[n].rearrange("p (c m) -> p c m", m=128), in_=qst[:]),
                  [xk], ["qkv_scr%d" % n])
            for gi in range(4):
                bank, bk = pb[gi % 2], "pb%d" % (gi % 2)
                for c in range(8):
                    S.op("pe", lambda e, c=c, gi=gi, bank=bank: e.matmul(
                        bank[:, :], lhsT=hT[:, c, :], rhs=w_in_sb[:, c, 4096 + gi * 512:4096 + (gi + 1) * 512],
                        start=(c == 0), stop=(c == 7)), [hTk, "w_in_sb"], [bk], inc=(c == 7))
                S.op("act", lambda e, gi=gi, bank=bank: e.activation(out=zs[:, gi * 512:(gi + 1) * 512], in_=bank[:, :], func=AF.Silu),
                     [bk], ["zs"])
                yield
            S.dma("sp", lambda e: e.dma_start(out=z_scr[n * 128:(n + 1) * 128, :], in_=zs[:]), ["zs"], ["z_scr%d" % n])
            for c in range(8):
                S.op("pe", lambda e, c=c: e.matmul(pb[5][:, 0:32], lhsT=hT[:, c, :], rhs=w_in_sb[:, c, 6144:6176],
                                                   start=(c == 0), stop=(c == 7)), [hTk, "w_in_sb"], ["pb5"], inc=(c == 7))
            gates_from_psum(pb[5][:, 0:32], "pb5", 128, gbt[:, :], "gbt")
            S.dma("sp", lambda e: e.dma_start(out=gb_scr[n * 128:(n + 1) * 128, :], in_=gbt[:]), ["gbt"], ["gb_scr%d" % n])

        def gdn_chunk(C, nlev, kT, qT, Kt, Vt, g_ap, b_ap, rk, O, okey, extra=None):
            v3 = lambda t_: t_[0:C, :, 0:C]
            b3 = lambda bank: bank[0:C, :].rearrange("p (h m) -> p h m", m=128)[:, :, 0:C]
            bfw = lambda bank: bank[0:C, :].rearrange("p (h m) -> p h m", m=128)
            p22 = lambda ap_: ap_[0:C].rearrange("p (a b) m -> p a b m", b=2)[:, :, :, 0:C]
            bc4 = lambda ap_, W: ap_.unsqueeze(2).to_broadcast([C, 4, W])
            def _gate_part(k_):
                if k_ == 0:
                    S.op("pe", lambda e: e.matmul(pb[5][0:C, 0:16], lhsT=uinc[0:C, 0:C], rhs=g_ap, start=True, stop=True), rk + ["uinc"], ["pb5"])
                    S.op("dve", lambda e: e.tensor_copy(out=gam[0:C, :], in_=pb[5][0:C, 0:16]), ["pb5"], ["gam"])
                    S.op("pe", lambda e: e.matmul(pb[5][:, 16:32], lhsT=ones_f[0:C, :], rhs=g_ap, start=True, stop=True), rk + ["ones_f"], ["pb5"])
                    S.op("dve", lambda e: e.tensor_copy(out=gtl[:], in_=pb[5][:, 16:32]), ["pb5"], ["gtl"])
                    S.op("act", lambda e: e.activation(out=gtot[:], in_=gtl[:], func=AF.Exp), ["gtl"], ["gtot"])
                    S.op("act", lambda e: e.activation(out=eg[0:C, :], in_=gam[0:C, :], func=AF.Exp), ["gam"], ["eg"])
                    S.op("dve", lambda e: e.tensor_scalar(out=negeg[0:C, :], in0=eg[0:C, :], scalar1=-1.0, scalar2=None, op0=ALU.mult), ["eg"], ["negeg"])
                    S.op("dve", lambda e: e.tensor_tensor(out=kdf[0:C, :], in0=gtl[0:C, :], in1=gam[0:C, :], op=ALU.subtract), ["gtl", "gam"], ["kdf"])
                    S.op("act", lambda e: e.activation(out=kdf[0:C, :], in_=kdf[0:C, :], func=AF.Exp), ["kdf"], ["kdf"])

                elif k_ == 1:
                    S.op("pool", lambda e: e.tensor_tensor(
                        out=Kdec[0:C].rearrange("p (q t) d -> p q t d", t=2), in0=Kt.unsqueeze(2).to_broadcast([C, 8, 2, 128]),
                        in1=kdf[0:C, :].rearrange("p (q t) -> p q t", t=2).unsqueeze(3).to_broadcast([C, 8, 2, 128]), op=ALU.mult),
                        rk + ["kdf"], ["Kdec"])

                elif k_ == 2:
                    S.op("pool", lambda e: e.tensor_tensor(out=Sst[:], in0=Sst[:], in1=gtot[:].unsqueeze(2).to_broadcast([128, 16, 128]), op=ALU.mult),
                         ["Sst", "gtot"], ["Sst"])

            def quad_gen(q4, s):
                hs = slice(q4 * 4, q4 * 4 + 4)
                bA, bB, bC = (pb[3], pb[4], pb[6]) if s == 0 else (pb[0], pb[1], pb[5])
                kA, kB, kC = ("pb3", "pb4", "pb6") if s == 0 else ("pb0", "pb1", "pb5")
                Eq, Eb, Nq, NTq, Xq, XTq = Eq2[s], Eb2[s], Nq2[s], NTq2[s], Xq2[s], XTq2[s]
                Pq, PTq = Pq2[s], PTq2[s]
                kn = lambda n_: "%s_s%d" % (n_, s)

                def mm4(bank, bk, L, Lk, Rr, Rk):
                    for hh in range(4):
                        S.op("pe", lambda e, hh=hh: e.matmul(bank[0:C, hh * 128:hh * 128 + C], lhsT=L[0:C, hh, 0:C], rhs=Rr[0:C, hh, 0:C],
                                                             start=True, stop=True), [Lk, Rk], [bk], inc=(hh == 3))
                S.op("dve", lambda e: e.tensor_tensor(out=v3(Gm), in0=lmask[0:C, 0:C].unsqueeze(1).to_broadcast([C, 4, C]),
                                                      in1=bc4(g_ap[:, hs], C), op=ALU.mult), rk + ["lmask"], ["Gm"])
                for hh in range(4):
                    S.op("pe", lambda e, hh=hh: e.matmul(bC[0:C, hh * 128:hh * 128 + C], lhsT=Gm[0:C, hh, 0:C], rhs=uinc[0:C, 0:C],
                                                         start=True, stop=True), ["Gm", "uinc"], [kC], inc=(hh == 3))
                yield
                S.op("act", lambda e: e.activation(out=v3(Eq), in_=b3(bC), func=AF.Exp), [kC], [kn("Eq")])
                for hp in range(2):
                    hq = q4 * 2 + hp
                    S.op("pe", lambda e, hq=hq, hp=hp: e.matmul(bA[0:C, hp * 128:hp * 128 + C], lhsT=kT(hq), rhs=kT(hq),
                                                                start=True, stop=True), rk, [kA], inc=False)
                    S.op("pe", lambda e, hq=hq, hp=hp: e.matmul(bA[0:C, 256 + hp * 128:256 + hp * 128 + C], lhsT=kT(hq), rhs=qT(hq),
                                                                start=True, stop=True), rk, [kA], inc=(hp == 1))
                yield
                S.op("dve", lambda e: e.tensor_tensor(out=v3(Eb), in0=v3(Eq), in1=bc4(b_ap[:, hs], C), op=ALU.mult), [kn("Eq")] + rk, [kn("Eb")])
                S.op("dve", lambda e: e.tensor_tensor(out=v3(Eb), in0=v3(Eb), in1=v3(mstrict), op=ALU.mult), [kn("Eb"), "mstrict"], [kn("Eb")])
                S.op("dve", lambda e: e.tensor_tensor(out=v3(Eq), in0=v3(Eq), in1=v3(mincl), op=ALU.mult), [kn("Eq"), "mincl"], [kn("Eq")])
                kk = bA[0:C, 0:256].rearrange("p (a m) -> p a m", m=128)[:, :, 0:C].unsqueeze(2).to_broadcast([C, 2, 2, C])
                qk_ = bA[0:C, 256:512].rearrange("p (a m) -> p a m", m=128)[:, :, 0:C].unsqueeze(2).to_broadcast([C, 2, 2, C])
                S.op("dve", lambda e: e.tensor_tensor(out=p22(Nq), in0=kk, in1=p22(Eb), op=ALU.mult), [kA, kn("Eb")], [kn("Nq")])
                S.op("dve", lambda e: e.tensor_tensor(out=QKD[0:C, hs, :].rearrange("p (a b) m -> p a b m", b=2)[:, :, :, 0:C],
                                                      in0=qk_, in1=p22(Eq), op=ALU.mult), [kA, kn("Eq")], ["QKD"])
                for hh in range(4):
                    S.op("pe", lambda e, hh=hh: e.transpose(out=pbT[0:C, hh * 128:hh * 128 + C], in_=Nq[0:C, hh, 0:C], identity=ident[0:C, 0:C]),
                         [kn("Nq"), "ident"], ["pbT"], inc=(hh == 3))
                S.op("act", lambda e: e.copy(out=v3(NTq), in_=pbT[0:C, 0:512].rearrange("p (h m) -> p h m", m=128)[:, :, 0:C]),
                     ["pbT"], [kn("NTq")])
                S.op("dve", lambda e: e.tensor_tensor(out=v3(Pq[0]), in0=v3(Nq), in1=v3(bd16), op=ALU.mult), [kn("Nq"), "bd16"], [kn("Pq0")])
                S.op("dve", lambda e: e.tensor_tensor(out=v3(PTq[0]), in0=v3(NTq), in1=v3(bd16), op=ALU.mult), [kn("NTq"), "bd16"], [kn("PTq0")])
                S.op("dve", lambda e: e.tensor_tensor(out=v3(Xq), in0=v3(ident4), in1=v3(Pq[0]), op=ALU.subtract), ["ident4", kn("Pq0")], [kn("Xq")])
                yield
                ci = 0
                for lv in range(nlev):
                    P_, PT_ = Pq[ci], PTq[ci]
                    Pk_, PTk_ = kn("Pq%d" % ci), kn("PTq%d" % ci)
                    ni = 1 - ci
                    lastb = (lv == nlev - 1)
                    if not lastb:
                        mm4(bA, kA, PT_, PTk_, P_, Pk_)
                    mm4(bB, kB, P_, Pk_, PT_, PTk_)
                    yield
                    if not lastb:
                        S.op("act", lambda e, ni=ni: e.copy(out=v3(Pq[ni]), in_=b3(bA)), [kA], [kn("Pq%d" % ni)])
                    S.op("act", lambda e, ni=ni: e.copy(out=v3(PTq[ni]), in_=b3(bB)), [kB], [kn("PTq%d" % ni)])
                    mm4(bC, kC, PTq[ni], kn("PTq%d" % ni), Xq, kn("Xq"))
                    yield
                    S.op("dve", lambda e: e.tensor_tensor(out=v3(Xq), in0=b3(bC), in1=v3(Xq), op=ALU.add), [kC, kn("Xq")], [kn("Xq")])
                    ci = ni
                if C > 16:
                    for si in range(3):
                        S.op("dve", lambda e, si=si: e.tensor_tensor(out=v3(PTq[0]), in0=v3(NTq), in1=v3(offmT[si]), op=ALU.mult),
                             [kn("NTq"), "offmT"], [kn("PTq0")])
                        mm4(bA, kA, PTq[0], kn("PTq0"), Xq, kn("Xq"))
                        for hh in range(4):
                            S.op("pe", lambda e, hh=hh: e.transpose(out=pbT[0:C, hh * 128:hh * 128 + C], in_=Xq[0:C, hh, 0:C],
                                                                    identity=ident[0:C, 0:C]), [kn("Xq"), "ident"], ["pbT"], inc=(hh == 3))
                        S.op("act", lambda e: e.copy(out=v3(XTq), in_=pbT[0:C, 0:512].rearrange("p (h m) -> p h m", m=128)[:, :, 0:C]),
                             ["pbT"], [kn("XTq")])
                        yield
                        S.op("act", lambda e: e.copy(out=v3(Pq[1]), in_=b3(bA)), [kA], [kn("Pq1")])
                        mm4(bC, kC, XTq, kn("XTq"), Pq[1], kn("Pq1"))
                        yield
                        last = (si == 2)
                        Xdst = Xall[0:C, q4 * 4:q4 * 4 + 4, 0:C] if last else v3(Xq)
                        S.op("dve", lambda e, Xdst=Xdst: e.tensor_tensor(out=Xdst, in0=v3(Xq), in1=b3(bC), op=ALU.subtract),
                             [kC, kn("Xq")], ["Xall" if last else kn("Xq")])
                else:
                    S.op("dve", lambda e: e.tensor_copy(out=Xall[0:C, q4 * 4:q4 * 4 + 4, 0:C], in_=v3(Xq)), [kn("Xq")], ["Xall"])

            bg = [extra] if extra is not None else []

            def run_rr(gens, drain=False):
                gens = list(gens)
                while gens or (drain and bg):
                    for g_ in list(gens):
                        try:
                            next(g_)
                        except StopIteration:
                            gens.remove(g_)
                    for g_ in list(bg):
                        try:
                            next(g_)
                        except StopIteration:
                            bg.remove(g_)
            _gate_part(0)
            run_rr([quad_gen(0, 0), quad_gen(1, 1)])
            _gate_part(1)
            run_rr([quad_gen(2, 0), quad_gen(3, 1)], drain=True)
            _gate_part(2)

            def seq_gen(q4, s):
                hs = slice(q4 * 4, q4 * 4 + 4)
                b1, b2 = (pb[0], pb[1]) if s == 0 else (pb[2], pb[3])
                k1, k2 = ("pb0", "pb1") if s == 0 else ("pb2", "pb3")
                tq, Vp, ub = tq2[s], Vp2[s], ub2[s]
                kn = lambda n_: "%s_q%d" % (n_, s)
                for hh in range(4):
                    h = q4 * 4 + hh
                    S.op("pe", lambda e, h=h, hh=hh: e.matmul(b1[0:C, hh * 128:(hh + 1) * 128], lhsT=kT(h // 2), rhs=Sbf[:, h, :],
                                                              start=True, stop=True), rk + ["Sbf"], [k1], inc=(hh == 3))
                for hh in range(4):
                    h = q4 * 4 + hh
                    S.op("pe", lambda e, h=h, hh=hh: e.matmul(b2[0:C, hh * 128:(hh + 1) * 128], lhsT=qT(h // 2), rhs=Sbf[:, h, :],
                                                              start=True, stop=True), rk + ["Sbf"], [k2], inc=(hh == 3))
                yield
                S.op("dve", lambda e: e.tensor_tensor(out=tq[0:C], in0=bfw(b1), in1=bc4(negeg[0:C, hs], 128), op=ALU.mult),
                     [k1, "negeg"], [kn("tq")])
                S.op("dve", lambda e: e.tensor_tensor(out=Vp[0:C], in0=tq[0:C], in1=Vt[:, hs, :], op=ALU.add), [kn("tq")] + rk, [kn("Vp")])
                for hh in range(4):
                    h = q4 * 4 + hh
                    S.op("pe", lambda e, h=h, hh=hh: e.matmul(b1[0:C, hh * 128:(hh + 1) * 128], lhsT=Xall[0:C, h, 0:C], rhs=Vp[0:C, hh, :],
                                                              start=True, stop=True), ["Xall", kn("Vp")], [k1], inc=(hh == 3))
                S.op("dve", lambda e: e.tensor_tensor(out=tq[0:C], in0=bfw(b2), in1=bc4(eg[0:C, hs], 128), op=ALU.mult),
                     [k2, "eg"], [kn("tq")])
                yield
                S.op("dve", lambda e: e.tensor_tensor(out=ub[0:C], in0=bfw(b1), in1=bc4(b_ap[:, hs], 128), op=ALU.mult),
                     [k1] + rk, [kn("ub")])
                for hh in range(4):
                    h = q4 * 4 + hh
                    S.op("pe", lambda e, h=h, hh=hh: e.matmul(b2[0:C, hh * 128:(hh + 1) * 128], lhsT=QKD[0:C, h, 0:C], rhs=ub[0:C, hh, :],
                                                              start=True, stop=True), ["QKD", kn("ub")], [k2], inc=(hh == 3))
                for hh in range(4):
                    h = q4 * 4 + hh
                    S.op("pe", lambda e, h=h, hh=hh: e.matmul(b1[:, hh * 128:(hh + 1) * 128], lhsT=Kdec[0:C, h, :], rhs=ub[0:C, hh, :],
                                                              start=True, stop=True), ["Kdec", kn("ub")], [k1], inc=(hh == 3))
                yield
                S.op("dve", lambda e: e.tensor_tensor(out=O[:, hs, :], in0=bfw(b2), in1=tq[0:C], op=ALU.add), [k2, kn("tq")], [okey])
                S.op("dve", lambda e: e.tensor_tensor(out=Sst[:, hs, :], in0=Sst[:, hs, :],
                                                      in1=b1[:, :].rearrange("p (h m) -> p h m", m=128), op=ALU.add),
                     ["Sst", k1], ["Sst"])
                S.op("act", lambda e: e.copy(out=Sbf[:, hs, :], in_=Sst[:, hs, :]), ["Sst"], ["Sbf"])
            run_rr([seq_gen(0, 0), seq_gen(1, 1)])
            run_rr([seq_gen(2, 0), seq_gen(3, 1)])


        def gdn_onorm_gate(M, o_in, okey, z_in, zkey, dst, dkey):
            S.op("act", lambda e: e.activation(out=junk2[0:M, :], in_=o_in, func=AF.Square), [okey], ["junk2"])
            S.op("dve", lambda e: e.tensor_reduce(out=ssq16[0:M, :], in_=junk2[0:M, :].rearrange("p (h d) -> p h d", d=128), axis=AX.X, op=ALU.add),
                 ["junk2"], ["ssq16"])
            S.op("act", lambda e: e.activation(out=ssq16[0:M, :], in_=ssq16[0:M, :], func=AF.Sqrt, scale=1.0 / 128, bias=EPS), ["ssq16"], ["ssq16"])
            S.op("dve", lambda e: e.reciprocal(out=ssq16[0:M, :], in_=ssq16[0:M, :]), ["ssq16"], ["ssq16"])
            j3 = junk2[0:M, :].rearrange("p (h d) -> p h d", d=128)
            S.op("dve", lambda e: e.tensor_tensor(out=j3, in0=o_in.rearrange("p (h d) -> p h d", d=128),
                                                  in1=ssq16[0:M, :].unsqueeze(2).to_broadcast([M, 16, 128]), op=ALU.mult),
                 [okey, "ssq16"], ["junk2"])
            S.op("dve", lambda e: e.tensor_tensor(out=j3, in0=j3, in1=ong[0:M, :].unsqueeze(1).to_broadcast([M, 16, 128]), op=ALU.mult),
                 ["junk2", "ong"], ["junk2"])
            S.op("dve", lambda e: e.tensor_tensor(out=dst, in0=junk2[0:M, :], in1=z_in, op=ALU.mult), ["junk2", zkey], [dkey])

        def gdn_core_tile(l2, n):
            xi, xk = xin[n % 2], "xin%d" % (n % 2)
            xo, xok = xin[(n + 1) % 2], "xin%d" % ((n + 1) % 2)
            if n == 0:
                S.dma("sp", lambda e: e.dma_start(out=xi[:, :, 3:131], in_=qkv_scr[n].rearrange("p (c m) -> p c m", m=128)),
                      ["qkv_scr%d" % n], [xk])
            S.dma("sp", lambda e: e.dma_start(out=zs[:], in_=z_scr[n * 128:(n + 1) * 128, :]), ["z_scr%d" % n], ["zs"])
            S.dma("sp", lambda e: e.dma_start(out=gbt[:], in_=gb_scr[n * 128:(n + 1) * 128, :]), ["gb_scr%d" % n], ["gbt"])
            if n == 0:
                S.op("pool", lambda e: e.memset(xi[:, :, 0:3], 0.0), [xk], [xk])
            else:
                S.op("pool", lambda e: e.tensor_copy(out=xi[:, :, 0:3], in_=xo[:, :, 128:131]), [xk, xok], [xk])
            if n + 1 < NT:
                S.dma("sp", lambda e: e.dma_start(out=xo[:, :, 3:131], in_=qkv_scr[n + 1].rearrange("p (c m) -> p c m", m=128)),
                      ["qkv_scr%d" % (n + 1)], [xok])
            def conv_group(gi, ylds):
                csl = slice(gi * 8, gi * 8 + 8)
                for j, eng in [(0, "pool"), (1, "pool"), (2, "dve"), (3, "dve")]:
                    S.op(eng, lambda e, j=j, csl=csl: e.tensor_tensor(
                        out=ct[j][:], in0=xi[:, csl, j:j + 128], in1=cwT[:, csl, j:j + 1].to_broadcast([128, 8, 128]), op=ALU.mult),
                        [xk, "cwT"], ["ct%d" % j])
                    if ylds and j % 2 == 1:
                        yield
                S.op("dve", lambda e: e.tensor_tensor(out=ct[2][:], in0=ct[2][:], in1=ct[3][:], op=ALU.add), ["ct2", "ct3"], ["ct2"])
                if ylds:
                    yield
                S.op("dve", lambda e: e.tensor_tensor(out=ct[0][:], in0=ct[0][:], in1=ct[1][:], op=ALU.add), ["ct0", "ct1"], ["ct0"])
                S.op("dve", lambda e: e.tensor_tensor(out=ct[0][:], in0=ct[0][:], in1=ct[2][:], op=ALU.add), ["ct0", "ct2"], ["ct0"])
                if ylds:
                    yield
                S.op("act", lambda e, csl=csl: e.activation(out=cs[:, csl, :], in_=ct[0][:], func=AF.Silu), ["ct0"], ["cs"])

            for gi in range(2):
                for _ in conv_group(gi, False):
                    pass
            gdn_qk_norm()
            for hq in range(8):
                S.op("pe", lambda e, hq=hq: e.transpose(out=pbT[:, hq * 128:(hq + 1) * 128], in_=qkn[:, 8 + hq, :], identity=ident[:]),
                     ["qkn", "ident"], ["pbT"], inc=(hq == 7))
            S.op("act", lambda e: e.copy(out=Ktm[:], in_=pbT[:].rearrange("p (c m) -> p c m", m=128)), ["pbT"], ["Ktm"])

            def v_gen():
                for gi in (2, 3):
                    yield from conv_group(gi, True)
                    yield
                    half = gi - 2
                    for hh in range(8):
                        S.op("pe", lambda e, hh=hh, half=half: e.transpose(out=pbT[:, hh * 128:(hh + 1) * 128], in_=cs[:, 16 + half * 8 + hh, :],
                                                                           identity=ident[:]), ["cs", "ident"], ["pbT"], inc=(hh == 7))
                    S.op("act", lambda e, half=half: e.copy(out=Vtm[:, half * 8:half * 8 + 8, :], in_=pbT[:].rearrange("p (c m) -> p c m", m=128)),
                         ["pbT"], ["Vtm"])
                    yield
            gdn_chunk(128, 3,
                      lambda hq: qkn[:, 8 + hq, :], lambda hq: qkn[:, hq, :], Ktm[:], Vtm[:],
                      gbt[:, 0:16], gbt[:, 16:32], ["qkn", "Ktm", "Vtm", "gbt"],
                      o32[:], "o32", extra=v_gen())
            gdn_onorm_gate(128, o32[:].rearrange("p h d -> p (h d)"), "o32", zs[:], "zs", og[:], "og")
            S.dma("sp", lambda e: e.dma_start(out=o_scr[n * 128:(n + 1) * 128, :], in_=og[:]), ["og"], ["o_scr%d" % n])

        def sample_main(l):
            is_swa_ = (l % 2 == 0)
            l2_ = l // 2
            lp_ = l - 1
            kcp = 8 if (lp_ % 2 == 0) else 16
            if l > 0:
                S.dma("sp", lambda e: e.dma_start(out=h32[0:NS, :], in_=mods_scr[lp_, :, 2 * D:3 * D]), ["mods_scr%d" % lp_], ["h32"])
                out_proj(og_s, "og_s", NS, kcp, xs, "xs", h32, "h32")
            if l < depth:
                S.dma("sp", lambda e: e.dma_start(out=modP[0:NS, :], in_=mods_scr[l, :, 0:2 * D]), ["mods_scr%d" % l], ["modP"])
                norm_to_hT(xs, "xs", NS, modP, "modP", hT2[0], "hT0")
                if is_swa_:
                    for _ in swa_prompt_tile(l2_, NT, hT2[0], "hT0", sample=True):
                        pass
                else:
                    for _ in gdn_inproj_tile(l2_, NT, hT2[0], "hT0"):
                        pass
            else:
                final_norm(xs, "xs", NS, y_s, "y_s", normg)

        def gdn_qk_norm():
            for qd in range(4):
                hs = slice(qd * 4, qd * 4 + 4)
                S.op("act", lambda e, hs=hs: e.activation(out=sq[:], in_=cs[:, hs, :], func=AF.Square), ["cs"], ["sq"])
                S.op("pe", lambda e: e.matmul(pb[0][:, :], lhsT=ones_b[:], rhs=sq[:].rearrange("p h m -> p (h m)"), start=True, stop=True),
                     ["sq", "ones_b"], ["pb0"])
                S.op("dve", lambda e: e.tensor_copy(out=rn[:].rearrange("p h m -> p (h m)"), in_=pb[0][:, :]), ["pb0"], ["rn"])
                S.op("act", lambda e: e.activation(out=rn[:], in_=rn[:], func=AF.Sqrt, bias=EPS), ["rn"], ["rn"])
                S.op("dve", lambda e: e.reciprocal(out=rn[:], in_=rn[:]), ["rn"], ["rn"])
                sc = (128.0 ** -0.5) if qd < 2 else 1.0
                S.op("dve", lambda e, hs=hs, sc=sc: e.scalar_tensor_tensor(out=qkn[:, hs, :], in0=cs[:, hs, :], scalar=sc, in1=rn[:],
                                                                          op0=ALU.mult, op1=ALU.mult), ["cs", "rn"], ["qkn"])

        def sample_gdn_core(l2):
            xi, xk = xin[0], "xin0"
            xv = xi[:, :, 0:112].rearrange("p c (b t) -> p c b t", t=7)
            S.dma("sp", lambda e: e.dma_start(out=xin[1][:, :, 0:128], in_=qkv_scr[NT].rearrange("p (c m) -> p c m", m=128)),
                  ["qkv_scr%d" % NT], ["xin1"])
            for gi in range(4):
                csl = slice(gi * 8, gi * 8 + 8)
                S.op("pool", lambda e, csl=csl: e.tensor_copy(out=xv[:, csl, :, 3:7],
                                                              in_=xin[1][:, csl, 0:64].rearrange("p c (b t) -> p c b t", t=4)),
                     ["xin1", xk], [xk])
                S.dma("pool", lambda e, gi=gi: e.dma_start(out=hb[0:48, :], in_=conv_state[l2][:, :, gi * 1024:(gi + 1) * 1024].rearrange("b t c -> (b t) c")),
                      [], ["hb"])
                for c in range(8):
                    S.op("pe", lambda e, c=c: e.transpose(out=pbT[:, c * 128:c * 128 + 48], in_=hb[0:48, c * 128:(c + 1) * 128], identity=ident[0:48, 0:48]),
                         ["hb", "ident"], ["pbT"])
                S.op("act", lambda e, csl=csl: e.copy(out=xv[:, csl, :, 0:3],
                                                      in_=pbT[:].rearrange("p (c m) -> p c m", m=128)[:, :, 0:48].rearrange("p c (b t) -> p c b t", t=3)),
                     ["pbT", xk], [xk])
                for j, eng in [(0, "pool"), (1, "pool"), (2, "dve"), (3, "dve")]:
                    S.op(eng, lambda e, j=j, csl=csl: e.tensor_tensor(
                        out=ct[j][:, :, 0:64].rearrange("p c (b t) -> p c b t", t=4), in0=xv[:, csl, :, j:j + 4],
                        in1=cwT[:, csl, j:j + 1].unsqueeze(3).to_broadcast([128, 8, 16, 4]), op=ALU.mult), [xk, "cwT"], ["ct%d" % j])
                S.op("dve", lambda e: e.tensor_tensor(out=ct[2][:], in0=ct[2][:], in1=ct[3][:], op=ALU.add), ["ct2", "ct3"], ["ct2"])
                S.op("dve", lambda e: e.tensor_tensor(out=ct[0][:], in0=ct[0][:], in1=ct[1][:], op=ALU.add), ["ct0", "ct1"], ["ct0"])
                S.op("dve", lambda e: e.tensor_tensor(out=ct[0][:], in0=ct[0][:], in1=ct[2][:], op=ALU.add), ["ct0", "ct2"], ["ct0"])
                S.op("act", lambda e, csl=csl: e.activation(out=cs[:, csl, :], in_=ct[0][:], func=AF.Silu), ["ct0"], ["cs"])
            gdn_qk_norm()
            for b in range(NSB):
                S.dma("sp", lambda e, b=b: e.dma_start(out=Sst[:], in_=s_state[l2, b].rearrange("h k v -> k h v")), [], ["Sst"])
                S.op("pool", lambda e: e.tensor_copy(out=Sbf[:], in_=Sst[:]), ["Sst"], ["Sbf"])
                S.dma("sp", lambda e, b=b: e.dma_start(out=gbt[0:4, :], in_=gb_scr[NT * 128 + 4 * b:NT * 128 + 4 * b + 4, :]),
                      ["gb_scr%d" % NT], ["gbt"])
                for hq in range(8):
                    S.op("pe", lambda e, hq=hq, b=b: e.transpose(out=pbT[0:4, hq * 128:(hq + 1) * 128], in_=qkn[:, 8 + hq, 4 * b:4 * b + 4], identity=ident[:]),
                         ["qkn", "ident"], ["pbT"])
                S.op("act", lambda e: e.copy(out=Ktm[0:4, :, :], in_=pbT[0:4, :].rearrange("p (c m) -> p c m", m=128)), ["pbT"], ["Ktm"])
                for half in range(2):
                    for hh in range(8):
                        S.op("pe", lambda e, hh=hh, half=half, b=b: e.transpose(out=pbT[0:4, hh * 128:(hh + 1) * 128],
                                                                               in_=cs[:, 16 + half * 8 + hh, 4 * b:4 * b + 4], identity=ident[:]),
                             ["cs", "ident"], ["pbT"])
                    S.op("act", lambda e, half=half: e.copy(out=Vtm[0:4, half * 8:half * 8 + 8, :], in_=pbT[0:4, :].rearrange("p (c m) -> p c m", m=128)),
                         ["pbT"], ["Vtm"])
                gdn_chunk(4, 1,
                          lambda hq, b=b: qkn[:, 8 + hq, 4 * b:4 * b + 4], lambda hq, b=b: qkn[:, hq, 4 * b:4 * b + 4],
                          Ktm[0:4], Vtm[0:4],
                          gbt[0:4, 0:16], gbt[0:4, 16:32], ["qkn", "Ktm", "Vtm", "gbt"],
                          o32[0:4], "o32")
                S.dma("sp", lambda e, b=b: e.dma_start(out=ss_out[l2, b].rearrange("h k v -> k h v"), in_=Sst[:]), ["Sst"], ["ss_out"])
                S.dma("sp", lambda e, b=b: e.dma_start(out=osall[4 * b:4 * b + 4, :], in_=o32[0:4, :, :].rearrange("p h d -> p (h d)")),
                      ["o32"], ["osall"])
            S.dma("sp", lambda e: e.dma_start(out=zs[0:NS, :], in_=z_scr[NT * 128:NT * 128 + NS, :]), ["z_scr%d" % NT], ["zs"])
            gdn_onorm_gate(NS, osall[0:NS, :], "osall", zs[0:NS, :], "zs", og_s[0:NS, :], "og_s")

        def sbs(stk, name, shape, dt):
            return stk.enter_context(nc.sbuf_tensor(name, list(shape), dt))

        S.dma("sp", lambda e: e.dma_start(out=xs[:], in_=xs_in), [], ["xs"])
        for l in range(depth + 1):
            is_swa = (l % 2 == 0)
            l2 = l // 2
            lp = l - 1
            kc_prev = 8 if (lp % 2 == 0) else 16
            tg = "_p%d" % l
            with contextlib.ExitStack() as stk:
                if l > 0:
                    w_out_sb = sbs(stk, "w_out_sb" + tg, [128, kc_prev, D], BF16)
                    ogl = [sbs(stk, "ogl%d" % i + tg, [128, kc_prev * 128], BF16) for i in range(2)]
                if l < depth:
                    w_in_sb = sbs(stk, "w_in_sb" + tg, [128, 8, SWA_W if is_swa else GDN_W], BF16)
                    if is_swa:
                        qkT = [sbs(stk, "qkT%d" % i + tg, [128, 10, 128], BF16) for i in range(2)]
                        vext = [sbs(stk, "vext%d" % i + tg, [128, 4, 65], BF16) for i in range(2)]
                        e_own2 = [sbs(stk, "e_own%d" % i + tg, [128, 4, 128], BF16) for i in range(2)]
                        e_prev2 = [sbs(stk, "e_prev%d" % i + tg, [128, 4, 128], BF16) for i in range(2)]
                        on32 = sbs(stk, "on32" + tg, [128, D], F32)
                        og = sbs(stk, "og" + tg, [128, D], BF16)
                        for i in range(2):
                            S.op("pool", lambda e, i=i: e.memset(vext[i][:], 1.0), [], ["vext%d" % i])
                    else:
                        qst = sbs(stk, "qst" + tg, [128, 32, 128], BF16)
                if l > 0:
                    if lp % 2 == 0:
                        load_w(w_out_sb, "w_out_sb", swa_w_out[lp // 2], 8, D)
                    else:
                        load_w(w_out_sb, "w_out_sb", gdn_w_out[lp // 2], 16, D)
                if l < depth:
                    if is_swa:
                        load_w(w_in_sb, "w_in_sb", swa_w_in[l2], 8, SWA_W)
                        swa_layer_setup(l2)
                    else:
                        load_w(w_in_sb, "w_in_sb", gdn_w_in[l2], 8, GDN_W)
                        gdn_layer_setup(l2)
                    modulation(l)
                else:
                    S.dma("sp", lambda e: e.dma_start(out=normg[:], in_=final_norm_g.partition_broadcast(128)), [], ["normg"])
                gprev, gpk = gateP[lp % 2], "gateP%d" % (lp % 2)
                src_x, skey_fn = (xp, lambda n: "xp") if l <= 1 else (x_scr, lambda n: "x_scr%d" % n)
                def pre_gen(n):
                    if n + 1 < NT:
                        load_x(n + 1, src_x, skey_fn(n + 1))
                        if l > 0:
                            load_o(n + 1, kc_prev * 128)
                    xc, xck = xt[n % 2], "xt%d" % (n % 2)
                    if l > 0:
                        out_proj(ogl[n % 2], "ogl%d" % (n % 2), 128, kc_prev, xc, xck, gprev, gpk)
                        yield
                        if l < depth:
                            S.dma("sp", lambda e, n=n, xc=xc: e.dma_start(out=x_scr[n * 128:(n + 1) * 128, :], in_=xc[:]),
                                  [xck], ["x_scr%d" % n])
                    if l < depth:
                        norm_to_hT(xc, xck, 128, modP, "modP", hT2[n % 2], "hT%d" % (n % 2))
                    else:
                        final_norm(xc, xck, 128, y_p[n * 128:(n + 1) * 128, :], "y_p", normg)
                    yield

                def run_bg(main, bgs):
                    bgs = list(bgs)
                    for _ in main:
                        for g_ in list(bgs):
                            try:
                                next(g_)
                            except StopIteration:
                                bgs.remove(g_)
                    for g_ in bgs:
                        for _ in g_:
                            pass

                load_x(0, src_x, skey_fn(0))
                if l > 0:
                    load_o(0, kc_prev * 128)
                for _ in pre_gen(0):
                    pass
                for n in range(NT):
                    bgs = [pre_gen(n + 1)] if n + 1 < NT else []
                    if l < depth:
                        if is_swa:
                            main = swa_prompt_tile(l2, n, hT2[n % 2], "hT%d" % (n % 2))
                        else:
                            main = gdn_inproj_tile(l2, n, hT2[n % 2], "hT%d" % (n % 2))
                    else:
                        main = iter(())
                    run_bg(main, bgs)
                if do_sample:
                    sample_main(l)
                S.barrier()
                S.flush()
            if l < depth and not is_swa:
                tg = "_c%d" % l
                with contextlib.ExitStack() as stk:
                    xin = [sbs(stk, "xin%d" % i + tg, [128, 32, 131], BF16) for i in range(2)]
                    cwT = sbs(stk, "cwT" + tg, [128, 32, 4], F32)
                    ctall = sbs(stk, "ctall" + tg, [128, 4, 8, 128], F32)
                    ct = [ctall[:, i] for i in range(4)]
                    osall = ctall[:, 0:2].rearrange("p a c m -> p (a c m)")
                    junk2 = ctall[:, 2:4].rearrange("p a c m -> p (a c m)")
                    rn = ctall[:, 0, 0:4, :]
                    S.groups["osall"] = ["ct0", "ct1"]
                    S.groups["junk2"] = ["ct2", "ct3"]
                    S.groups["rn"] = ["ct0"]
                    cs = sbs(stk, "cs" + tg, [128, 32, 128], BF16)
                    sq = sbs(stk, "sq" + tg, [128, 4, 128], BF16)
                    qkn = sbs(stk, "qkn" + tg, [128, 16, 128], BF16)
                    Ktm = sbs(stk, "Ktm" + tg, [128, 8, 128], BF16)
                    Vtm = sbs(stk, "Vtm" + tg, [128, 16, 128], BF16)
                    ong = sbs(stk, "ong" + tg, [128, 128], F32)
                    gam = sbs(stk, "gam" + tg, [128, 16], F32)
                    gtl = sbs(stk, "gtl" + tg, [128, 16], F32)
                    gtot = sbs(stk, "gtot" + tg, [128, 16], F32)
                    eg = sbs(stk, "eg" + tg, [128, 16], F32)
                    negeg = sbs(stk, "negeg" + tg, [128, 16], F32)
                    kdf = sbs(stk, "kdf" + tg, [128, 16], F32)
                    Gm = sbs(stk, "Gm" + tg, [128, 4, 128], F32)
                    QKD = sbs(stk, "QKD" + tg, [128, 16, 128], BF16)
                    Eq2 = [sbs(stk, "Eq%d" % i + tg, [128, 4, 128], BF16) for i in range(2)]
                    Eb2 = [sbs(stk, "Eb%d" % i + tg, [128, 4, 128], BF16) for i in range(2)]
                    Nq2 = [sbs(stk, "Nq%d" % i + tg, [128, 4, 128], BF16) for i in range(2)]
                    NTq2 = [sbs(stk, "NTq%d" % i + tg, [128, 4, 128], BF16) for i in range(2)]
                    Xq2 = [sbs(stk, "Xq%d" % i + tg, [128, 4, 128], BF16) for i in range(2)]
                    XTq2 = [sbs(stk, "XTq%d" % i + tg, [128, 4, 128], BF16) for i in range(2)]
                    Pq2 = [[sbs(stk, "Pq%d_%d" % (i, j) + tg, [128, 4, 128], BF16) for j in range(2)] for i in range(2)]
                    PTq2 = [[sbs(stk, "PTq%d_%d" % (i, j) + tg, [128, 4, 128], BF16) for j in range(2)] for i in range(2)]
                    tq2 = [sbs(stk, "tq%d" % i + tg, [128, 4, 128], F32) for i in range(2)]
                    Vp2 = [sbs(stk, "Vp%d" % i + tg, [128, 4, 128], BF16) for i in range(2)]
                    ub2 = [sbs(stk, "ub%d" % i + tg, [128, 4, 128], BF16) for i in range(2)]
                    bd16 = sbs(stk, "bd16" + tg, [128, 4, 128], BF16)
                    offmT = [sbs(stk, "offmT%d" % i + tg, [128, 4, 128], BF16) for i in range(3)]
                    S.dma("sp", lambda e: e.dma_start(out=bd16[:], in_=c_bd16), [], ["bd16"])
                    for i in range(3):
                        S.dma("sp", lambda e, i=i: e.dma_start(out=offmT[i][:], in_=c_offT[i]), [], ["offmT"])
                    Xall = sbs(stk, "Xall" + tg, [128, 16, 128], BF16)
                    Kdec = sbs(stk, "Kdec" + tg, [128, 16, 128], BF16)
                    Sst = sbs(stk, "Sst" + tg, [128, 16, 128], F32)
                    Sbf = sbs(stk, "Sbf" + tg, [128, 16, 128], BF16)
                    o32 = sbs(stk, "o32" + tg, [128, 16, 128], F32)
                    ssq16 = sbs(stk, "ssq16" + tg, [128, 16], F32)
                    og = sbs(stk, "og" + tg, [128, 2 * D], BF16)
                    gdn_core_setup(l2)
                    S.op("pool", lambda e: e.memset(Sst[:], 0.0), [], ["Sst"])
                    S.op("pool", lambda e: e.memset(Sbf[:], 0.0), [], ["Sbf"])
                    for n in range(NT):
                        gdn_core_tile(l2, n)
                    S.dma("sp", lambda e: e.dma_start(out=sp_out[l2].rearrange("h k v -> k h v"), in_=Sst[:]), ["Sst"], ["sp_out"])
                    if do_sample:
                        sample_gdn_core(l2)
                    S.barrier()
                    S.flush()

        S.finish()
        S.flush()
    return nc


def _consts():
    bf = ml_dtypes.bfloat16
    i = np.arange(128)
    r = {}
    r["c_ident"] = np.eye(128).astype(bf)
    s, q = i[:, None], i[None, :]
    r["c_mown"] = np.repeat((q >= s)[:, None, :], 4, axis=1).astype(bf)
    r["c_mprev"] = np.repeat((s >= q)[:, None, :], 4, axis=1).astype(bf)
    r["c_lmask"] = (s > q).astype(np.float32)
    r["c_uinc"] = (s <= q).astype(np.float32)
    r["c_mincl"] = np.repeat((q >= s)[:, None, :], 4, axis=1).astype(bf)
    r["c_mstrict"] = np.repeat((q > s)[:, None, :], 4, axis=1).astype(bf)
    r["c_ident4"] = np.repeat((q == s)[:, None, :], 4, axis=1).astype(bf)
    r["c_bd16"] = np.repeat(((q // 16) == (s // 16))[:, None, :], 4, axis=1).astype(bf)
    offs = []
    for sz in (16, 32, 64):
        m = ((s // (2 * sz)) == (q // (2 * sz))) & ((s // sz) % 2 == 0) & ((q // sz) % 2 == 1)
        offs.append(m)
    r["c_off"] = np.stack([np.repeat(m[:, None, :], 4, axis=1) for m in offs]).astype(bf)
    r["c_offT"] = np.stack([np.repeat(m.T[:, None, :], 4, axis=1) for m in offs]).astype(bf)
    return {k: np.ascontiguousarray(v) for k, v in r.items()}


def _swa_cols():
    cols = []
    for jp in range(2):
        for g in range(4):
            a = 4 * (2 * jp) + g
            b = 4 * (2 * jp + 1) + g
            cols += list(range(a * 64, (a + 1) * 64)) + list(range(b * 64, (b + 1) * 64))
    cols += list(range(1024, 1280))
    cols += list(range(1024, 1536))
    cols += list(range(1536, 2560))
    return np.array(cols)


def prep_in_maps(inp, T, cores):
    f = np.float32
    shared = dict(_consts())
    shared["norm_g"] = np.ascontiguousarray(inp["norm_g"], f)
    shared["w_mod"] = np.ascontiguousarray(inp["w_mod"], f)
    shared["b_mod"] = np.ascontiguousarray(inp["b_mod"], f)
    shared["swa_w_in"] = np.ascontiguousarray(np.asarray(inp["swa_w_in"], f)[:, :, _swa_cols()])
    shared["swa_sinks"] = np.ascontiguousarray(inp["swa_sinks"], f)
    shared["swa_w_out"] = np.ascontiguousarray(inp["swa_w_out"], f)
    shared["gdn_w_in"] = np.ascontiguousarray(inp["gdn_w_in"], f)
    cw = np.asarray(inp["gdn_conv_w"], f)
    shared["gdn_conv_w"] = np.ascontiguousarray(cw)
    shared["gdn_conv_wT"] = np.ascontiguousarray(cw.reshape(2, 4, 32, 128).transpose(0, 3, 2, 1))
    shared["gdn_a_log"] = np.ascontiguousarray(inp["gdn_a_log"], f)
    shared["gdn_dt_bias"] = np.ascontiguousarray(inp["gdn_dt_bias"], f)
    shared["gdn_o_norm_g"] = np.ascontiguousarray(inp["gdn_o_norm_g"], f)
    shared["gdn_w_out"] = np.ascontiguousarray(inp["gdn_w_out"], f)
    shared["final_norm_g"] = np.ascontiguousarray(inp["final_norm_g"], f)
    maps = []
    for c in cores:
        b = c // 4
        sl = slice(NSB * c, NSB * (c + 1))
        m = dict(shared)
        m["xp"] = np.ascontiguousarray(np.asarray(inp["x_prompt"])[b, :T], f)
        m["xs"] = np.ascontiguousarray(np.asarray(inp["x_sample"])[sl].reshape(NS, D), f)
        m["ctok"] = np.ascontiguousarray(np.concatenate([np.asarray(inp["c_sample"])[sl], np.asarray(inp["c_prompt"])[b:b + 1]], 0), f)
        m["cache_k"] = np.ascontiguousarray(np.asarray(inp["cache_swa_k"])[:, sl].reshape(2, NSB, 128, 256), f)
        m["cache_v"] = np.ascontiguousarray(np.asarray(inp["cache_swa_v"])[:, sl].reshape(2, NSB, 128, 256), f)
        m["conv_state"] = np.ascontiguousarray(np.asarray(inp["state_gdn_conv"])[:, sl], f)
        m["s_state"] = np.ascontiguousarray(np.asarray(inp["state_gdn_s"])[:, sl], f)
        maps.append(m)
    return maps


_NC_CACHE = {}


def kernel(**inputs):
    T = 8192
    key = (T, 4)
    if key not in _NC_CACHE:
        _NC_CACHE[key] = build(T, 4, True)
    nc = _NC_CACHE[key]
    cores = list(range(8))
    maps = prep_in_maps(inputs, T, cores)
    res = run_bass_kernel_spmd(nc, maps, core_ids=cores)
    R = res.results
    f = np.float32
    y_prompt = np.stack([R[0]["y_p"], R[4]["y_p"]]).astype(f)
    y_sample = np.concatenate([R[c]["y_s"].reshape(NSB, 4, D) for c in range(8)], 0).astype(f)
    kp = np.stack([R[0]["kp_out"], R[4]["kp_out"]], 1).reshape(2, 2, 128, 4, 64).astype(f)
    vp = np.stack([R[0]["vp_out"], R[4]["vp_out"]], 1).reshape(2, 2, 128, 4, 64).astype(f)
    ks = np.concatenate([R[c]["ks_out"] for c in range(8)], 1).reshape(2, 128, 128, 4, 64).astype(f)
    vs = np.concatenate([R[c]["vs_out"] for c in range(8)], 1).reshape(2, 128, 128, 4, 64).astype(f)
    convp = np.stack([R[0]["convp_out"], R[4]["convp_out"]], 1).astype(f)
    sp = np.stack([R[0]["sp_out"], R[4]["sp_out"]], 1).astype(f)
    convs = np.concatenate([R[c]["convs_out"] for c in range(8)], 1).astype(f)
    ssn = np.concatenate([R[c]["ss_out"] for c in range(8)], 1).astype(f)
    return (y_prompt, y_sample, kp, vp, ks, vs, convp, sp, convs, ssn)
```

```python
import contextlib
import numpy as np
import ml_dtypes
import concourse.bass as bass
import concourse.mybir as mybir
from concourse.bass_utils import run_bass_kernel_spmd

F32 = mybir.dt.float32
BF16 = mybir.dt.bfloat16
AF = mybir.ActivationFunctionType
ALU = mybir.AluOpType
AX = mybir.AxisListType

D = 1024
EPS = 1e-6
NSB = 16
NS = 64
SWA_W = 2816
GDN_W = 6176


class Sched:
    ENGS = ["sp", "act", "dve", "pool", "pe"]

    def __init__(self, nc, ndma=32):
        self.nc = nc
        self.q = {e: [] for e in self.ENGS}
        self.cnt = {e: 0 for e in self.ENGS}
        self.waited = {e: {} for e in self.ENGS}
        self.lastw = {}
        self.readers = {}
        self.ndma = ndma
        self.dma_val = [0] * ndma
        self.dma_rr = {"sp": 0, "pool": 0}
        self.nops = 0
        import os as _os
        self.limit = int(_os.environ["KSTOP"]) if _os.environ.get("KSTOP") else None
        self.groups = {}

    def _x(self, keys):
        out = []
        for k in keys:
            out.extend(self.groups.get(k, (k,)))
        return out

    def _need(self, eng, tok):
        semkey, val = tok
        if semkey == ("e", "pe") and eng == "pe":
            return
        if self.waited[eng].get(semkey, 0) >= val:
            return
        self.waited[eng][semkey] = val
        self.q[eng].append(("wait", semkey, val))

    def _deps(self, eng, reads, writes):
        for k in reads:
            t = self.lastw.get(k)
            if t is not None:
                self._need(eng, t)
        for k in writes:
            t = self.lastw.get(k)
            if t is not None:
                self._need(eng, t)
            for t in self.readers.get(k, ()):
                self._need(eng, t)

    def _commit(self, tok, reads, writes):
        for k in writes:
            self.lastw[k] = tok
            self.readers[k] = []
        for k in reads:
            if k not in writes:
                self.readers.setdefault(k, []).append(tok)

    def op(self, eng, fn, reads=(), writes=(), inc=True):
        if self.limit is not None and self.nops >= self.limit:
            return
        reads, writes = self._x(reads), self._x(writes)
        self._deps(eng, reads, writes)
        if inc:
            self.cnt[eng] += 1
            tok = (("e", eng), self.cnt[eng])
            self.q[eng].append(("op", fn, ("e", eng), 1))
        else:
            tok = (("e", eng), self.cnt[eng] + 1)
            self.q[eng].append(("op", fn, None, 0))
        self._commit(tok, reads, writes)
        self.nops += 1
        self.flush()

    def dma(self, eng, fn, reads=(), writes=()):
        if self.limit is not None and self.nops >= self.limit:
            return
        reads, writes = self._x(reads), self._x(writes)
        half = self.ndma // 2
        base = 0 if eng == "sp" else half
        idx = base + self.dma_rr[eng]
        self.dma_rr[eng] = (self.dma_rr[eng] + 1) % half
        if self.dma_val[idx] > 0:
            self._need(eng, (("d", idx), self.dma_val[idx]))
        self._deps(eng, reads, writes)
        self.dma_val[idx] += 16
        tok = (("d", idx), self.dma_val[idx])
        self.q[eng].append(("op", fn, ("d", idx), 16))
        self._commit(tok, reads, writes)
        self.nops += 1
        self.flush()

    def barrier(self):
        for e in self.ENGS:
            for f in self.ENGS:
                if f != e and self.cnt[f] > 0:
                    self._need(e, (("e", f), self.cnt[f]))
            for idx in range(self.ndma):
                if self.dma_val[idx] > 0:
                    self._need(e, (("d", idx), self.dma_val[idx]))

    def finish(self):
        for idx in range(self.ndma):
            if self.dma_val[idx] > 0:
                self._need("sp", (("d", idx), self.dma_val[idx]))
        for e in self.ENGS:
            if e != "sp" and self.cnt[e] > 0:
                self._need("sp", (("e", e), self.cnt[e]))

    def attach(self, st):
        nc = self.nc
        self.sems = {}
        for e in self.ENGS:
            self.sems[("e", e)] = st.enter_context(nc.semaphore("s_" + e))
        for i in range(self.ndma):
            self.sems[("d", i)] = st.enter_context(nc.semaphore("d_%d" % i))
        self.eng = {"sp": nc.sync, "act": nc.scalar, "dve": nc.vector, "pool": nc.gpsimd, "pe": nc.tensor}

    def flush(self):
        for eng in self.ENGS:
            for item in self.q[eng]:
                if item[0] == "wait":
                    self.eng[eng].wait_ge(self.sems[item[1]], item[2])
                else:
                    ins = item[1](self.eng[eng])
                    if item[2] is not None:
                        ins.then_inc(self.sems[item[2]], item[3])
            self.q[eng].clear()


def build(T, depth, do_sample=True):
    NT = T // 128
    n_swa = (depth + 1) // 2
    n_gdn = depth // 2
    nc = bass.Bass("TRN2", target_bir_lowering=False)

    def din(name, shape, dt=F32):
        return nc.dram_tensor(name, list(shape), dt, kind="ExternalInput").ap()

    def dout(name, shape):
        return nc.dram_tensor(name, list(shape), F32, kind="ExternalOutput").ap()

    def dscr(name, shape, dt):
        return nc.dram_tensor(name, list(shape), dt).ap()

    xp = din("xp", [T, D])
    xs_in = din("xs", [NS, D])
    ctok = din("ctok", [17, D])
    cache_k = din("cache_k", [2, NSB, 128, 256])
    cache_v = din("cache_v", [2, NSB, 128, 256])
    conv_state = din("conv_state", [2, NSB, 3, 4096])
    s_state = din("s_state", [2, NSB, 16, 128, 128])
    norm_g = din("norm_g", [4, D])
    w_mod = din("w_mod", [4, D, 3 * D])
    b_mod = din("b_mod", [4, 3 * D])
    swa_w_in = din("swa_w_in", [2, D, SWA_W])
    swa_sinks = din("swa_sinks", [2, 16])
    swa_w_out = din("swa_w_out", [2, D, D])
    gdn_w_in = din("gdn_w_in", [2, D, GDN_W])
    gdn_conv_w = din("gdn_conv_w", [2, 4, 4096])
    gdn_conv_wT = din("gdn_conv_wT", [2, 128, 32, 4])
    gdn_a_log = din("gdn_a_log", [2, 16])
    gdn_dt_bias = din("gdn_dt_bias", [2, 16])
    gdn_o_norm_g = din("gdn_o_norm_g", [2, 128])
    gdn_w_out = din("gdn_w_out", [2, 2 * D, D])
    final_norm_g = din("final_norm_g", [D])
    c_ident = din("c_ident", [128, 128], BF16)
    c_mown = din("c_mown", [128, 4, 128], BF16)
    c_mprev = din("c_mprev", [128, 4, 128], BF16)
    c_lmask = din("c_lmask", [128, 128])
    c_uinc = din("c_uinc", [128, 128])
    c_mincl = din("c_mincl", [128, 4, 128], BF16)
    c_mstrict = din("c_mstrict", [128, 4, 128], BF16)
    c_ident4 = din("c_ident4", [128, 4, 128], BF16)
    c_bd16 = din("c_bd16", [128, 4, 128], BF16)
    c_off = din("c_off", [3, 128, 4, 128], BF16)
    c_offT = din("c_offT", [3, 128, 4, 128], BF16)

    y_p = dout("y_p", [T, D])
    y_s = dout("y_s", [NS, D])
    kp_out = dout("kp_out", [2, 128, 256])
    vp_out = dout("vp_out", [2, 128, 256])
    ks_out = dout("ks_out", [2, NSB, 128, 256])
    vs_out = dout("vs_out", [2, NSB, 128, 256])
    convp_out = dout("convp_out", [2, 3, 4096])
    sp_out = dout("sp_out", [2, 16, 128, 128])
    convs_out = dout("convs_out", [2, NSB, 3, 4096])
    ss_out = dout("ss_out", [2, NSB, 16, 128, 128])

    x_scr = dscr("x_scr", [T, D], F32)
    o_scr = dscr("o_scr", [T, 2 * D], BF16)
    qkv_scr = dscr("qkv_scr", [NT + 1, 128, 32 * 128], BF16)
    z_scr = dscr("z_scr", [T + 128, 2 * D], BF16)
    sproj_scr = dscr("sproj_scr", [NS, 4096], F32)
    xpad_scr = dscr("xpad_scr", [NSB, 7, 4096], F32)
    kv_s_scr = dscr("kv_s_scr", [NS, 24 * 128], BF16)
    gb_s_scr = dscr("gb_s_scr", [NS, 32], F32)
    osn_scr = dscr("osn_scr", [NS, 2 * D], F32)
    vs_scr = dscr("vs_scr", [NS, 256], BF16)

    S = Sched(nc)

    with contextlib.ExitStack() as st:
        S.attach(st)
        def sb(name, shape, dt):
            return st.enter_context(nc.sbuf_tensor(name, list(shape), dt))

        def ps(name, shape, dt):
            return st.enter_context(nc.psum_tensor(name, list(shape), dt))

        ident = sb("ident", [128, 128], BF16)
        mown = sb("mown", [128, 4, 128], BF16)
        mprev = sb("mprev", [128, 4, 128], BF16)
        lmask = sb("lmask", [128, 128], F32)
        uinc = sb("uinc", [128, 128], F32)
        mincl = sb("mincl", [128, 4, 128], BF16)
        mstrict = sb("mstrict", [128, 4, 128], BF16)
        ident4 = sb("ident4", [128, 4, 128], BF16)
        ones_f = sb("ones_f", [128, 128], F32)
        ones_b = sb("ones_b", [128, 128], BF16)
        for (t_, d_, k_) in [(ident, c_ident, "ident"), (mown, c_mown, "mown"), (mprev, c_mprev, "mprev"),
                             (lmask, c_lmask, "lmask"), (uinc, c_uinc, "uinc"), (mincl, c_mincl, "mincl"),
                             (mstrict, c_mstrict, "mstrict"), (ident4, c_ident4, "ident4")]:
            S.dma("sp", lambda e, t_=t_, d_=d_: e.dma_start(out=t_[:], in_=d_), [], [k_])
        S.op("pool", lambda e: e.memset(ones_f[:], 1.0), [], ["ones_f"])
        S.op("pool", lambda e: e.memset(ones_b[:], 1.0), [], ["ones_b"])

        pbT = ps("pbT", [128, 1024], BF16)
        pb = [ps("pb%d" % i, [128, 512], F32) for i in range(7)]

        xt = [sb("xt%d" % i, [128, D], F32) for i in range(2)]
        junk = sb("junk", [128, D], F32)
        ss = sb("ss", [128, 1], F32)
        rstd = sb("rstd", [128, 1], F32)
        h32 = sb("h32", [128, D], F32)
        hb = sb("hb", [128, D], BF16)
        hT2 = [sb("hT%d" % i, [128, 8, 128], BF16) for i in range(2)]
        modP = sb("modP", [128, 2 * D], F32)
        gateP = [sb("gateP%d" % i, [128, D], F32) for i in range(2)]
        normg = sb("normg", [128, D], F32)
        wmodb = sb("wmodb", [128, 8, 512], BF16)
        bmodb = sb("bmodb", [128, 512], F32)
        cT17 = sb("cT17", [128, 8, 17], BF16)
        cTp = sb("cTp", [128, 8, 128], BF16)
        cTs = sb("cTs", [128, 8, NS], BF16)
        ogT = sb("ogT", [128, 16, 128], BF16)
        tmpo = sb("tmpo", [128, 512], F32)
        xs = sb("xs_res", [NS, D], F32)
        og_s = sb("og_s", [NS, 2 * D], BF16)
        kv32 = sb("kv32", [128, 512], F32)
        zs = sb("zs", [128, 2 * D], BF16)
        expsink = sb("expsink", [128, 16], F32)
        den2 = [sb("den%d" % i, [128, 4], F32) for i in range(2)]
        negA = sb("negA", [128, 16], F32)
        dtb = sb("dtb", [128, 16], F32)
        ab_t = sb("ab_t", [128, 32], F32)
        gbt = sb("gbt", [128, 32], F32)
        mods_scr = dscr("mods_scr", [4, NS, 3 * D], F32)
        gb_scr = dscr("gb_scr", [T + 128, 32], F32)
        zs_s_scr = dscr("zs_s_scr", [NS, 2 * D], BF16)

        _cst = contextlib.ExitStack()
        c17 = _cst.enter_context(nc.sbuf_tensor("c17", [17, D], F32))
        c17b = _cst.enter_context(nc.sbuf_tensor("c17b", [17, D], BF16))
        S.dma("sp", lambda e: e.dma_start(out=c17[:], in_=ctok), [], ["c17"])
        S.op("act", lambda e: e.activation(out=c17b[:], in_=c17[:], func=AF.Silu), ["c17"], ["c17b"])
        for c in range(8):
            S.op("pe", lambda e, c=c: e.transpose(out=pbT[:, c * 32:c * 32 + 17], in_=c17b[:, c * 128:(c + 1) * 128],
                                                   identity=ident[0:17, 0:17]), ["c17b", "ident"], ["pbT"])
        S.op("dve", lambda e: e.tensor_copy(out=cT17[:], in_=pbT[:, 0:256].rearrange("p (c m) -> p c m", m=32)[:, :, 0:17]),
             ["pbT"], ["cT17"])
        S.op("dve", lambda e: e.tensor_copy(out=cTp[:], in_=cT17[:, :, 16:17].to_broadcast([128, 8, 128])), ["cT17"], ["cTp"])
        S.op("dve", lambda e: e.tensor_copy(out=cTs[:].rearrange("p c (b t) -> p c b t", t=4),
                                            in_=cT17[:, :, 0:16].unsqueeze(3).to_broadcast([128, 8, 16, 4])), ["cT17"], ["cTs"])

        S.barrier()
        _cst.close()

        def modulation(l):
            S.dma("sp", lambda e: e.dma_start(out=normg[:], in_=norm_g[l].partition_broadcast(128)), [], ["normg"])
            for gi in range(6):
                S.dma("pool", lambda e, gi=gi: e.dma_start(
                    out=wmodb[:], in_=w_mod[l, :, gi * 512:(gi + 1) * 512].rearrange("(c p) n -> p c n", p=128)), [], ["wmodb"])
                S.dma("sp", lambda e, gi=gi: e.dma_start(out=bmodb[:], in_=b_mod[l, gi * 512:(gi + 1) * 512].partition_broadcast(128)),
                      [], ["bmodb"])
                if gi < 4:
                    dst, dk_ = modP[:, gi * 512:(gi + 1) * 512], "modP"
                else:
                    dst, dk_ = gateP[l % 2][:, (gi - 4) * 512:(gi - 3) * 512], "gateP%d" % (l % 2)
                for c in range(8):
                    S.op("pe", lambda e, c=c: e.matmul(pb[0][:, :], lhsT=cTp[:, c, :], rhs=wmodb[:, c, :], start=(c == 0), stop=(c == 7)),
                         ["cTp", "wmodb"], ["pb0"])
                S.op("dve", lambda e, dst=dst: e.tensor_tensor(out=dst, in0=pb[0][:, :], in1=bmodb[:, :], op=ALU.add),
                     ["pb0", "bmodb"], [dk_])
                if gi in (2, 3):
                    S.op("dve", lambda e, dst=dst, gi=gi: e.scalar_tensor_tensor(
                        out=dst, in0=dst, scalar=1.0, in1=normg[:, (gi - 2) * 512:(gi - 1) * 512], op0=ALU.add, op1=ALU.mult),
                        [dk_, "normg"], [dk_])
                if do_sample:
                    for c in range(8):
                        S.op("pe", lambda e, c=c: e.matmul(pb[1][0:NS, :], lhsT=cTs[:, c, :], rhs=wmodb[:, c, :], start=(c == 0), stop=(c == 7)),
                             ["cTs", "wmodb"], ["pb1"])
                    S.op("dve", lambda e: e.tensor_tensor(out=tmpo[0:NS, :], in0=pb[1][0:NS, :], in1=bmodb[0:NS, :], op=ALU.add),
                         ["pb1", "bmodb"], ["tmpo"])
                    if gi in (2, 3):
                        S.op("dve", lambda e, gi=gi: e.scalar_tensor_tensor(
                            out=tmpo[0:NS, :], in0=tmpo[0:NS, :], scalar=1.0, in1=normg[0:NS, (gi - 2) * 512:(gi - 1) * 512],
                            op0=ALU.add, op1=ALU.mult), ["tmpo", "normg"], ["tmpo"])
                    S.dma("sp", lambda e, gi=gi: e.dma_start(out=mods_scr[l, :, gi * 512:(gi + 1) * 512], in_=tmpo[0:NS, :]),
                          ["tmpo"], ["mods_scr%d" % l])

        def load_w(dst, dkey, src, kc, width):
            S.groups[dkey] = ["%s.%d" % (dkey, c) for c in range(kc)]
            for c in range(kc):
                S.dma("pool", lambda e, c=c: e.dma_start(out=dst[:, c, 0:width], in_=src[c * 128:(c + 1) * 128, :]), [],
                      ["%s.%d" % (dkey, c)])

        def norm_to_hT(xin, xkey, M, mod, mkey, hT, hTk):
            S.op("act", lambda e: e.activation(out=junk[0:M, 0:D], in_=xin[0:M, :], func=AF.Square, accum_out=ss[0:M, :]),
                 [xkey], ["junk", "ss"])
            S.op("act", lambda e: e.activation(out=rstd[0:M, :], in_=ss[0:M, :], func=AF.Sqrt, scale=1.0 / D, bias=EPS),
                 ["ss"], ["rstd"])
            S.op("dve", lambda e: e.reciprocal(out=rstd[0:M, :], in_=rstd[0:M, :]), ["rstd"], ["rstd"])
            S.op("dve", lambda e: e.scalar_tensor_tensor(out=h32[0:M, :], in0=xin[0:M, :], scalar=rstd[0:M, :],
                                                         in1=mod[0:M, D:2 * D], op0=ALU.mult, op1=ALU.mult),
                 [xkey, "rstd", mkey], ["h32"])
            S.op("dve", lambda e: e.tensor_tensor(out=hb[0:M, :], in0=h32[0:M, :], in1=mod[0:M, 0:D], op=ALU.add),
                 ["h32", mkey], ["hb"])
            for c in range(8):
                S.op("pe", lambda e, c=c: e.transpose(out=pbT[:, c * 128:c * 128 + M], in_=hb[0:M, c * 128:(c + 1) * 128],
                                                       identity=ident[0:M, 0:M]), ["hb", "ident"], ["pbT"], inc=(c == 7))
            S.op("act", lambda e: e.copy(out=hT[:, :, 0:M], in_=pbT[:].rearrange("p (c m) -> p c m", m=128)[:, :, 0:M]),
                 ["pbT"], [hTk])

        def final_norm(xin, xkey, M, dst_ap, dkey, fng):
            S.op("act", lambda e: e.activation(out=junk[0:M, 0:D], in_=xin[0:M, :], func=AF.Square, accum_out=ss[0:M, :]),
                 [xkey], ["junk", "ss"])
            S.op("act", lambda e: e.activation(out=rstd[0:M, :], in_=ss[0:M, :], func=AF.Sqrt, scale=1.0 / D, bias=EPS),
                 ["ss"], ["rstd"])
            S.op("dve", lambda e: e.reciprocal(out=rstd[0:M, :], in_=rstd[0:M, :]), ["rstd"], ["rstd"])
            S.op("dve", lambda e: e.scalar_tensor_tensor(out=h32[0:M, :], in0=xin[0:M, :], scalar=rstd[0:M, :],
                                                         in1=fng[0:M, :], op0=ALU.mult, op1=ALU.mult),
                 [xkey, "rstd", "normg"], ["h32"])
            S.dma("sp", lambda e: e.dma_start(out=dst_ap, in_=h32[0:M, :]), ["h32"], [dkey])

        def out_proj(og, ogkey, M, kc, xin, xkey, gate, mkey):
            for c in range(kc):
                S.op("pe", lambda e, c=c: e.transpose(out=pbT[:, (c % 8) * 128:(c % 8) * 128 + M],
                                                       in_=og[0:M, c * 128:(c + 1) * 128], identity=ident[0:M, 0:M]),
                     [ogkey, "ident"], ["pbT"], inc=(c % 8 == 7))
                if c % 8 == 7:
                    c0 = c - 7
                    S.op("act", lambda e, c0=c0: e.copy(out=ogT[:, c0:c0 + 8, 0:M],
                                                        in_=pbT[:].rearrange("p (c m) -> p c m", m=128)[:, :, 0:M]),
                         ["pbT"], ["ogT"])
            for gi in range(2):
                bank, bk = pb[gi], "pb%d" % gi
                for c in range(kc):
                    S.op("pe", lambda e, c=c, gi=gi, bank=bank: e.matmul(
                        bank[0:M, :], lhsT=ogT[:, c, 0:M], rhs=w_out_sb[:, c, gi * 512:(gi + 1) * 512],
                        start=(c == 0), stop=(c == kc - 1)), ["ogT", "w_out_sb"], [bk], inc=(c == kc - 1))
                S.op("dve", lambda e, gi=gi, bank=bank: e.tensor_tensor(
                    out=tmpo[0:M, :], in0=bank[0:M, :], in1=gate[0:M, gi * 512:(gi + 1) * 512], op=ALU.mult),
                    [bk, mkey], ["tmpo"])
                S.op("dve", lambda e, gi=gi: e.tensor_tensor(
                    out=xin[0:M, gi * 512:(gi + 1) * 512], in0=xin[0:M, gi * 512:(gi + 1) * 512], in1=tmpo[0:M, :], op=ALU.add),
                    ["tmpo", xkey], [xkey])

        def swa_attend(nq, nprev, nown, q_ap, kprev_ap, kown_ap, vprev_ap, vown_ap, rk, j, out_ap, okey):
            N = 4 * nq
            par = j % 2
            bo, bp, bO = (pb[2], pb[3], pb[4]) if par == 0 else (pb[0], pb[1], pb[5])
            bok, bpk, bOk = ("pb2", "pb3", "pb4") if par == 0 else ("pb0", "pb1", "pb5")
            e_own, e_prev = e_own2[par], e_prev2[par]
            eok, epk = "e_own%d" % par, "e_prev%d" % par
            S.op("pe", lambda e: e.matmul(bo[0:nown, 0:N], lhsT=kown_ap, rhs=q_ap, start=True, stop=True), rk, [bok])
            S.op("act", lambda e: e.activation(out=e_own[0:nown, :, 0:nq], in_=bo[0:nown, 0:N].rearrange("p (g q) -> p g q", g=4),
                                               func=AF.Exp, scale=0.125), [bok], [eok])
            S.op("dve", lambda e: e.tensor_tensor(out=e_own[0:nown, :, 0:nq], in0=e_own[0:nown, :, 0:nq],
                                                   in1=mown[0:nown, :, 0:nq], op=ALU.mult), [eok, "mown"], [eok])
            if nprev:
                S.op("pe", lambda e: e.matmul(bp[0:nprev, 0:N], lhsT=kprev_ap, rhs=q_ap, start=True, stop=True), rk, [bpk])
                S.op("act", lambda e: e.activation(out=e_prev[0:nprev, :, 0:nq],
                                                   in_=bp[0:nprev, 0:N].rearrange("p (g q) -> p g q", g=4),
                                                   func=AF.Exp, scale=0.125), [bpk], [epk])
                S.op("dve", lambda e: e.tensor_tensor(out=e_prev[0:nprev, :, 0:nq], in0=e_prev[0:nprev, :, 0:nq],
                                                       in1=mprev[0:nprev, :, 0:nq], op=ALU.mult), [epk, "mprev"], [epk])
            yield
            for g in range(4):
                if nprev:
                    S.op("pe", lambda e, g=g: e.matmul(bO[0:nq, g * 65:(g + 1) * 65], lhsT=e_prev[0:nprev, g, 0:nq], rhs=vprev_ap,
                                                       start=True, stop=False), [epk] + rk, [bOk], inc=False)
                S.op("pe", lambda e, g=g: e.matmul(bO[0:nq, g * 65:(g + 1) * 65], lhsT=e_own[0:nown, g, 0:nq], rhs=vown_ap,
                                                   start=(not nprev), stop=True), [eok] + rk, [bOk], inc=(g == 3))
            yield
            O3 = bO[0:nq, 0:260].rearrange("p (g d) -> p g d", d=65)
            S.op("dve", lambda e: e.tensor_tensor(out=den2[par][0:nq, :], in0=O3[:, :, 64], in1=expsink[0:nq, j * 4:(j + 1) * 4], op=ALU.add),
                 [bOk, "expsink"], ["den%d" % par])
            S.op("dve", lambda e: e.reciprocal(out=den2[par][0:nq, :], in_=den2[par][0:nq, :]), ["den%d" % par], ["den%d" % par])
            S.op("dve", lambda e: e.tensor_tensor(out=out_ap, in0=O3[:, :, 0:64],
                                                  in1=den2[par][0:nq, :].unsqueeze(2).to_broadcast([nq, 4, 64]), op=ALU.mult),
                 [bOk, "den%d" % par], [okey])

        def swa_layer_setup(l2):
            S.dma("sp", lambda e: e.dma_start(out=expsink[:], in_=swa_sinks[l2].partition_broadcast(128)), [], ["expsink"])
            S.op("act", lambda e: e.activation(out=expsink[:], in_=expsink[:], func=AF.Exp), ["expsink"], ["expsink"])

        def swa_prompt_tile(l2, n, hT, hTk, sample=False):
            cur, prv = n % 2, (n + 1) % 2
            if sample:
                cur, prv = 0, 1
            qk, qkk = qkT[cur], "qkT%d" % cur
            for ch in range(10):
                bank = pb[ch // 4 % 2]
                bk = "pb%d" % (ch // 4 % 2)
                sl = slice((ch % 4) * 128, (ch % 4 + 1) * 128)
                for c in range(8):
                    S.op("pe", lambda e, c=c, ch=ch, bank=bank, sl=sl: e.matmul(
                        bank[:, sl], lhsT=w_in_sb[:, c, ch * 128:(ch + 1) * 128], rhs=hT[:, c, :],
                        start=(c == 0), stop=(c == 7)), [hTk, "w_in_sb"], [bk], inc=(c == 7))
                if ch % 4 == 3 or ch == 9:
                    c0 = ch - (ch % 4)
                    nn = ch - c0 + 1
                    S.op("act", lambda e, c0=c0, nn=nn, bank=bank: e.copy(
                        out=qk[:, c0:c0 + nn, :], in_=bank[:, 0:nn * 128].rearrange("p (c m) -> p c m", m=128)),
                        [bk], [qkk])
                    yield
            for c in range(8):
                S.op("pe", lambda e, c=c: e.matmul(pb[5][:, :], lhsT=hT[:, c, :], rhs=w_in_sb[:, c, 1280:1792],
                                                   start=(c == 0), stop=(c == 7)), [hTk, "w_in_sb"], ["pb5"], inc=(c == 7))
            S.op("dve", lambda e: e.tensor_copy(out=vext[cur][:, :, 0:64], in_=pb[5][:, 256:512].rearrange("p (j d) -> p j d", d=64)),
                 ["pb5"], ["vext%d" % cur])
            if sample:
                S.op("dve", lambda e: e.tensor_copy(out=kv32[:], in_=pb[5][:, :]), ["pb5"], ["kv32"])
                S.dma("sp", lambda e: e.dma_start(out=sproj_scr[:, 0:512], in_=kv32[0:NS, :]), ["kv32"], ["sproj_scr"])
                for (dst_, src_, c0_) in [(ks_out, cache_k, 0), (vs_out, cache_v, 256)]:
                    S.dma("sp", lambda e, dst_=dst_, c0_=c0_: e.dma_start(out=dst_[l2][:, 124:128, :], in_=sproj_scr[:, c0_:c0_ + 256].rearrange("(b t) c -> b t c", t=4)), ["sproj_scr"], ["ksvs_out"])
                    S.dma("sp", lambda e, dst_=dst_, src_=src_: e.dma_start(out=dst_[l2][:, 0:124, :], in_=src_[l2][:, 4:128, :]), [], ["ksvs_out2"])
            elif n == NT - 1:
                S.op("dve", lambda e: e.tensor_copy(out=kv32[:], in_=pb[5][:, :]), ["pb5"], ["kv32"])
                S.dma("sp", lambda e: e.dma_start(out=kp_out[l2], in_=kv32[:, 0:256]), ["kv32"], ["kp_out"])
                S.dma("sp", lambda e: e.dma_start(out=vp_out[l2], in_=kv32[:, 256:512]), ["kv32"], ["vp_out"])
            for gi in range(2):
                bank, bk = pb[gi], "pb%d" % gi
                for c in range(8):
                    S.op("pe", lambda e, c=c, gi=gi, bank=bank: e.matmul(
                        bank[:, :], lhsT=hT[:, c, :], rhs=w_in_sb[:, c, 1792 + gi * 512:1792 + (gi + 1) * 512],
                        start=(c == 0), stop=(c == 7)), [hTk, "w_in_sb"], [bk], inc=(c == 7))
                S.op("act", lambda e, gi=gi, bank=bank: e.activation(out=zs[:, gi * 512:(gi + 1) * 512], in_=bank[:, :], func=AF.Silu),
                     [bk], ["zs"])
                yield
            if sample:
                for b in range(NSB):
                    S.dma("pool", lambda e, b=b: e.dma_start(out=hb[:, 0:256], in_=cache_k[l2, b]), [], ["hb"])
                    for jp in range(2):
                        S.op("pe", lambda e, jp=jp: e.transpose(out=pbT[:, jp * 128:(jp + 1) * 128], in_=hb[:, jp * 128:(jp + 1) * 128], identity=ident[:]),
                             ["hb", "ident"], ["pbT"])
                    S.op("act", lambda e: e.copy(out=qkT[1][:, 8:10, :], in_=pbT[:, 0:256].rearrange("p (c m) -> p c m", m=128)), ["pbT"], ["qkT1"])
                    S.dma("pool", lambda e, b=b: e.dma_start(out=vext[1][:, :, 0:64], in_=cache_v[l2, b].rearrange("s (j d) -> s j d", d=64)), [], ["vext1"])
                    for c in range(8):
                        S.op("pe", lambda e, c=c, b=b: e.matmul(pb[5][0:4, 0:256], lhsT=hT[:, c, 4 * b:4 * b + 4], rhs=w_in_sb[:, c, 1536:1792],
                                                              start=(c == 0), stop=(c == 7)), [hTk, "w_in_sb"], ["pb5"], inc=(c == 7))
                    S.op("dve", lambda e: e.tensor_copy(out=vext[0][0:4, :, 0:64], in_=pb[5][0:4, 0:256].rearrange("p (j d) -> p j d", d=64)),
                         ["pb5"], ["vext0"])
                    for j in range(4):
                        jp, half = j // 2, j % 2
                        psl = slice(half * 64, half * 64 + 64)
                        for _ in swa_attend(4, 128, 4, qk[psl, jp * 4:jp * 4 + 4, 4 * b:4 * b + 4], qkT[1][psl, 8 + jp, :], qk[psl, 8 + jp, 4 * b:4 * b + 4],
                                            vext[1][:, j, :], vext[0][0:4, j, :], ["qkT0", "qkT1", "vext0", "vext1"], j,
                                            on32[0:4, j * 256:(j + 1) * 256].rearrange("p (g d) -> p g d", d=64), "on32"):
                            pass
                    S.dma("sp", lambda e, b=b: e.dma_start(out=h32[4 * b:4 * b + 4, :], in_=on32[0:4, :]), ["on32"], ["h32"])
                S.op("dve", lambda e: e.tensor_tensor(out=og_s[0:NS, 0:D], in0=h32[0:NS, :], in1=zs[0:NS, 0:D], op=ALU.mult),
                     ["h32", "zs"], ["og_s"])
                return
            def att(j):
                jp, half = j // 2, j % 2
                psl = slice(half * 64, half * 64 + 64)
                rk = [qkk, "qkT%d" % prv, "vext%d" % cur, "vext%d" % prv]
                return swa_attend(128, 128 if n > 0 else 0, 128,
                                  qk[psl, jp * 4:jp * 4 + 4, :], qkT[prv][psl, 8 + jp, :], qk[psl, 8 + jp, :],
                                  vext[prv][:, j, :], vext[cur][:, j, :], rk, j,
                                  on32[:, j * 256:(j + 1) * 256].rearrange("p (g d) -> p g d", d=64), "on32")
            for j0 in (0, 2):
                gens = [att(j0), att(j0 + 1)]
                while gens:
                    for g_ in list(gens):
                        try:
                            next(g_)
                        except StopIteration:
                            gens.remove(g_)
                yield
            S.op("dve", lambda e: e.tensor_tensor(out=og[:, 0:D], in0=on32[:, 0:D], in1=zs[:, 0:D], op=ALU.mult),
                 ["on32", "zs"], ["og"])
            S.dma("sp", lambda e: e.dma_start(out=o_scr[n * 128:(n + 1) * 128, 0:D], in_=og[:, 0:D]), ["og"], ["o_scr%d" % n])

        def load_x(n, src, skey):
            S.dma("sp", lambda e: e.dma_start(out=xt[n % 2][:], in_=src[n * 128:(n + 1) * 128, :]), [skey], ["xt%d" % (n % 2)])

        def load_o(n, width):
            S.dma("sp", lambda e: e.dma_start(out=ogl[n % 2][:, 0:width], in_=o_scr[n * 128:(n + 1) * 128, 0:width]),
                  ["o_scr%d" % n], ["ogl%d" % (n % 2)])

        def gdn_layer_setup(l2):
            S.dma("sp", lambda e: e.dma_start(out=negA[:], in_=gdn_a_log[l2].partition_broadcast(128)), [], ["negA"])
            S.op("act", lambda e: e.activation(out=negA[:], in_=negA[:], func=AF.Exp), ["negA"], ["negA"])
            S.op("dve", lambda e: e.tensor_scalar(out=negA[:], in0=negA[:], scalar1=-1.0, scalar2=None, op0=ALU.mult), ["negA"], ["negA"])
            S.dma("sp", lambda e: e.dma_start(out=dtb[:], in_=gdn_dt_bias[l2].partition_broadcast(128)), [], ["dtb"])

        def gdn_core_setup(l2):
            S.dma("sp", lambda e: e.dma_start(out=ong[:], in_=gdn_o_norm_g[l2].partition_broadcast(128)), [], ["ong"])
            S.dma("sp", lambda e: e.dma_start(out=cwT[:], in_=gdn_conv_wT[l2]), [], ["cwT"])

        def gates_from_psum(bank_ap, bk, M, dst_ap, dkey):
            S.op("dve", lambda e: e.tensor_tensor(out=ab_t[0:M, 0:16], in0=bank_ap[:, 0:16], in1=dtb[0:M, :], op=ALU.add),
                 [bk, "dtb"], ["ab_t"])
            S.op("act", lambda e: e.activation(out=ab_t[0:M, 0:16], in_=ab_t[0:M, 0:16], func=AF.Exp), ["ab_t"], ["ab_t"])
            S.op("act", lambda e: e.activation(out=ab_t[0:M, 0:16], in_=ab_t[0:M, 0:16], func=AF.Ln, bias=1.0), ["ab_t"], ["ab_t"])
            S.op("dve", lambda e: e.tensor_tensor(out=dst_ap[:, 0:16], in0=ab_t[0:M, 0:16], in1=negA[0:M, :], op=ALU.mult),
                 ["ab_t", "negA"], [dkey])
            S.op("act", lambda e: e.activation(out=dst_ap[:, 16:32], in_=bank_ap[:, 16:32], func=AF.Sigmoid), [bk], [dkey])

        def gdn_inproj_tile(l2, n, hT, hTk):
            xk = "qst"
            for ch in range(32):
                bank = pb[ch // 4 % 2]
                bk = "pb%d" % (ch // 4 % 2)
                sl = slice((ch % 4) * 128, (ch % 4 + 1) * 128)
                for c in range(8):
                    S.op("pe", lambda e, c=c, ch=ch, bank=bank, sl=sl: e.matmul(
                        bank[:, sl], lhsT=w_in_sb[:, c, ch * 128:(ch + 1) * 128], rhs=hT[:, c, :],
                        start=(c == 0), stop=(c == 7)), [hTk, "w_in_sb"], [bk], inc=(c == 7))
                if ch % 4 == 3:
                    c0 = ch - 3
                    eng = "act" if (ch // 4) % 2 == 0 else "dve"
                    if eng == "act":
                        S.op("act", lambda e, c0=c0, bank=bank: e.copy(
                            out=qst[:, c0:c0 + 4, :], in_=bank[:, :].rearrange("p (c m) -> p c m", m=128)), [bk], [xk])
                    else:
                        S.op("dve", lambda e, c0=c0, bank=bank: e.tensor_copy(
                            out=qst[:, c0:c0 + 4, :], in_=bank[:, :].rearrange("p (c m) -> p c m", m=128)), [bk], [xk])
                    if n == NT - 1:
                        S.op("dve", lambda e, bank=bank: e.tensor_copy(
                            out=kv32[:, 0:12].rearrange("p (c m) -> p c m", m=3),
                            in_=bank[:, :].rearrange("p (c m) -> p c m", m=128)[:, :, 125:128]), [bk], ["kv32"])
                        for cc in range(4):
                            S.dma("sp", lambda e, c0=c0, cc=cc: e.dma_start(
                                out=convp_out[l2, :, (c0 + cc) * 128:(c0 + cc + 1) * 128].rearrange("t p -> p t"),
                                in_=kv32[:, cc * 3:cc * 3 + 3], allow_slow_non_contiguous=True),
                                ["kv32"], ["convp_out"])
                    yield
            if n == NT:
                for gi in range(8):
                    bank, bk = pb[gi % 2], "pb%d" % (gi % 2)
                    for c in range(8):
                        S.op("pe", lambda e, c=c, gi=gi, bank=bank: e.matmul(
                            bank[0:NS, :], lhsT=hT[:, c, 0:NS], rhs=w_in_sb[:, c, gi * 512:(gi + 1) * 512],
                            start=(c == 0), stop=(c == 7)), [hTk, "w_in_sb"], [bk], inc=(c == 7))
                    S.op("dve", lambda e, bank=bank: e.tensor_copy(out=tmpo[0:NS, :], in_=bank[0:NS, :]), [bk], ["tmpo"])
                    S.dma("sp", lambda e, gi=gi: e.dma_start(out=sproj_scr[:, gi * 512:(gi + 1) * 512], in_=tmpo[0:NS, :]), ["tmpo"], ["sproj_scr"])
                S.dma("sp", lambda e: e.dma_start(out=convs_out[l2], in_=sproj_scr.rearrange("(b t) c -> b t c", t=4)[:, 1:4, :]),
                      ["sproj_scr"], ["convs_out"])
            S.dma("sp", lambda e: e.dma_start(out=qkv_scr[n].rearrange("p (c m) -> p c m", m=128), in_=qst[:]),
                  [xk], ["qkv_scr%d" % n])
            for gi in range(4):
                bank, bk = pb[gi % 2], "pb%d" % (gi % 2)
                for c in range(8):
                    S.op("pe", lambda e, c=c, gi=gi, bank=bank: e.matmul(
                        bank[:, :], lhsT=hT[:, c, :], rhs=w_in_sb[:, c, 4096 + gi * 512:4096 + (gi + 1) * 512],
                        start=(c == 0), stop=(c == 7)), [hTk, "w_in_sb"], [bk], inc=(c == 7))
                S.op("act", lambda e, gi=gi, bank=bank: e.activation(out=zs[:, gi * 512:(gi + 1) * 512], in_=bank[:, :], func=AF.Silu),
                     [bk], ["zs"])
                yield
            S.dma("sp", lambda e: e.dma_start(out=z_scr[n * 128:(n + 1) * 128, :], in_=zs[:]), ["zs"], ["z_scr%d" % n])
            for c in range(8):
                S.op("pe", lambda e, c=c: e.matmul(pb[5][:, 0:32], lhsT=hT[:, c, :], rhs=w_in_sb[:, c, 6144:6176],
                                                   start=(c == 0), stop=(c == 7)), [hTk, "w_in_sb"], ["pb5"], inc=(c == 7))
            gates_from_psum(pb[5][:, 0:32], "pb5", 128, gbt[:, :], "gbt")
            S.dma("sp", lambda e: e.dma_start(out=gb_scr[n * 128:(n + 1) * 128, :], in_=gbt[:]), ["gbt"], ["gb_scr%d" % n])

        def gdn_chunk(C, nlev, kT, qT, Kt, Vt, g_ap, b_ap, rk, O, okey, extra=None):
            v3 = lambda t_: t_[0:C, :, 0:C]
            b3 = lambda bank: bank[0:C, :].rearrange("p (h m) -> p h m", m=128)[:, :, 0:C]
            bfw = lambda bank: bank[0:C, :].rearrange("p (h m) -> p h m", m=128)
            p22 = lambda ap_: ap_[0:C].rearrange("p (a b) m -> p a b m", b=2)[:, :, :, 0:C]
            bc4 = lambda ap_, W: ap_.unsqueeze(2).to_broadcast([C, 4, W])
            def _gate_part(k_):
                if k_ == 0:
                    S.op("pe", lambda e: e.matmul(pb[5][0:C, 0:16], lhsT=uinc[0:C, 0:C], rhs=g_ap, start=True, stop=True), rk + ["uinc"], ["pb5"])
                    S.op("dve", lambda e: e.tensor_copy(out=gam[0:C, :], in_=pb[5][0:C, 0:16]), ["pb5"], ["gam"])
                    S.op("pe", lambda e: e.matmul(pb[5][:, 16:32], lhsT=ones_f[0:C, :], rhs=g_ap, start=True, stop=True), rk + ["ones_f"], ["pb5"])
                    S.op("dve", lambda e: e.tensor_copy(out=gtl[:], in_=pb[5][:, 16:32]), ["pb5"], ["gtl"])
                    S.op("act", lambda e: e.activation(out=gtot[:], in_=gtl[:], func=AF.Exp), ["gtl"], ["gtot"])
                    S.op("act", lambda e: e.activation(out=eg[0:C, :], in_=gam[0:C, :], func=AF.Exp), ["gam"], ["eg"])
                    S.op("dve", lambda e: e.tensor_scalar(out=negeg[0:C, :], in0=eg[0:C, :], scalar1=-1.0, scalar2=None, op0=ALU.mult), ["eg"], ["negeg"])
                    S.op("dve", lambda e: e.tensor_tensor(out=kdf[0:C, :], in0=gtl[0:C, :], in1=gam[0:C, :], op=ALU.subtract), ["gtl", "gam"], ["kdf"])
                    S.op("act", lambda e: e.activation(out=kdf[0:C, :], in_=kdf[0:C, :], func=AF.Exp), ["kdf"], ["kdf"])

                elif k_ == 1:
                    S.op("pool", lambda e: e.tensor_tensor(
                        out=Kdec[0:C].rearrange("p (q t) d -> p q t d", t=2), in0=Kt.unsqueeze(2).to_broadcast([C, 8, 2, 128]),
                        in1=kdf[0:C, :].rearrange("p (q t) -> p q t", t=2).unsqueeze(3).to_broadcast([C, 8, 2, 128]), op=ALU.mult),
                        rk + ["kdf"], ["Kdec"])

                elif k_ == 2:
                    S.op("pool", lambda e: e.tensor_tensor(out=Sst[:], in0=Sst[:], in1=gtot[:].unsqueeze(2).to_broadcast([128, 16, 128]), op=ALU.mult),
                         ["Sst", "gtot"], ["Sst"])

            def quad_gen(q4, s):
                hs = slice(q4 * 4, q4 * 4 + 4)
                bA, bB, bC = (pb[3], pb[4], pb[6]) if s == 0 else (pb[0], pb[1], pb[5])
                kA, kB, kC = ("pb3", "pb4", "pb6") if s == 0 else ("pb0", "pb1", "pb5")
                Eq, Eb, Nq, NTq, Xq, XTq = Eq2[s], Eb2[s], Nq2[s], NTq2[s], Xq2[s], XTq2[s]
                Pq, PTq = Pq2[s], PTq2[s]
                kn = lambda n_: "%s_s%d" % (n_, s)

                def mm4(bank, bk, L, Lk, Rr, Rk):
                    for hh in range(4):
                        S.op("pe", lambda e, hh=hh: e.matmul(bank[0:C, hh * 128:hh * 128 + C], lhsT=L[0:C, hh, 0:C], rhs=Rr[0:C, hh, 0:C],
                                                             start=True, stop=True), [Lk, Rk], [bk], inc=(hh == 3))
                S.op("dve", lambda e: e.tensor_tensor(out=v3(Gm), in0=lmask[0:C, 0:C].unsqueeze(1).to_broadcast([C, 4, C]),
                                                      in1=bc4(g_ap[:, hs], C), op=ALU.mult), rk + ["lmask"], ["Gm"])
                for hh in range(4):
                    S.op("pe", lambda e, hh=hh: e.matmul(bC[0:C, hh * 128:hh * 128 + C], lhsT=Gm[0:C, hh, 0:C], rhs=uinc[0:C, 0:C],
                                                         start=True, stop=True), ["Gm", "uinc"], [kC], inc=(hh == 3))
                yield
                S.op("act", lambda e: e.activation(out=v3(Eq), in_=b3(bC), func=AF.Exp), [kC], [kn("Eq")])
                for hp in range(2):
                    hq = q4 * 2 + hp
                    S.op("pe", lambda e, hq=hq, hp=hp: e.matmul(bA[0:C, hp * 128:hp * 128 + C], lhsT=kT(hq), rhs=kT(hq),
                                                                start=True, stop=True), rk, [kA], inc=False)
                    S.op("pe", lambda e, hq=hq, hp=hp: e.matmul(bA[0:C, 256 + hp * 128:256 + hp * 128 + C], lhsT=kT(hq), rhs=qT(hq),
                                                                start=True, stop=True), rk, [kA], inc=(hp == 1))
                yield
                S.op("dve", lambda e: e.tensor_tensor(out=v3(Eb), in0=v3(Eq), in1=bc4(b_ap[:, hs], C), op=ALU.mult), [kn("Eq")] + rk, [kn("Eb")])
                S.op("dve", lambda e: e.tensor_tensor(out=v3(Eb), in0=v3(Eb), in1=v3(mstrict), op=ALU.mult), [kn("Eb"), "mstrict"], [kn("Eb")])
                S.op("dve", lambda e: e.tensor_tensor(out=v3(Eq), in0=v3(Eq), in1=v3(mincl), op=ALU.mult), [kn("Eq"), "mincl"], [kn("Eq")])
                kk = bA[0:C, 0:256].rearrange("p (a m) -> p a m", m=128)[:, :, 0:C].unsqueeze(2).to_broadcast([C, 2, 2, C])
                qk_ = bA[0:C, 256:512].rearrange("p (a m) -> p a m", m=128)[:, :, 0:C].unsqueeze(2).to_broadcast([C, 2, 2, C])
                S.op("dve", lambda e: e.tensor_tensor(out=p22(Nq), in0=kk, in1=p22(Eb), op=ALU.mult), [kA, kn("Eb")], [kn("Nq")])
                S.op("dve", lambda e: e.tensor_tensor(out=QKD[0:C, hs, :].rearrange("p (a b) m -> p a b m", b=2)[:, :, :, 0:C],
                                                      in0=qk_, in1=p22(Eq), op=ALU.mult), [kA, kn("Eq")], ["QKD"])
                for hh in range(4):
                    S.op("pe", lambda e, hh=hh: e.transpose(out=pbT[0:C, hh * 128:hh * 128 + C], in_=Nq[0:C, hh, 0:C], identity=ident[0:C, 0:C]),
                         [kn("Nq"), "ident"], ["pbT"], inc=(hh == 3))
                S.op("act", lambda e: e.copy(out=v3(NTq), in_=pbT[0:C, 0:512].rearrange("p (h m) -> p h m", m=128)[:, :, 0:C]),
                     ["pbT"], [kn("NTq")])
                S.op("dve", lambda e: e.tensor_tensor(out=v3(Pq[0]), in0=v3(Nq), in1=v3(bd16), op=ALU.mult), [kn("Nq"), "bd16"], [kn("Pq0")])
                S.op("dve", lambda e: e.tensor_tensor(out=v3(PTq[0]), in0=v3(NTq), in1=v3(bd16), op=ALU.mult), [kn("NTq"), "bd16"], [kn("PTq0")])
                S.op("dve", lambda e: e.tensor_tensor(out=v3(Xq), in0=v3(ident4), in1=v3(Pq[0]), op=ALU.subtract), ["ident4", kn("Pq0")], [kn("Xq")])
                yield
                ci = 0
                for lv in range(nlev):
                    P_, PT_ = Pq[ci], PTq[ci]
                    Pk_, PTk_ = kn("Pq%d" % ci), kn("PTq%d" % ci)
                    ni = 1 - ci
                    lastb = (lv == nlev - 1)
                    if not lastb:
                        mm4(bA, kA, PT_, PTk_, P_, Pk_)
                    mm4(bB, kB, P_, Pk_, PT_, PTk_)
                    yield
                    if not lastb:
                        S.op("act", lambda e, ni=ni: e.copy(out=v3(Pq[ni]), in_=b3(bA)), [kA], [kn("Pq%d" % ni)])
                    S.op("act", lambda e, ni=ni: e.copy(out=v3(PTq[ni]), in_=b3(bB)), [kB], [kn("PTq%d" % ni)])
                    mm4(bC, kC, PTq[ni], kn("PTq%d" % ni), Xq, kn("Xq"))
                    yield
                    S.op("dve", lambda e: e.tensor_tensor(out=v3(Xq), in0=b3(bC), in1=v3(Xq), op=ALU.add), [kC, kn("Xq")], [kn("Xq")])
                    ci = ni
                if C > 16:
                    for si in range(3):
                        S.op("dve", lambda e, si=si: e.tensor_tensor(out=v3(PTq[0]), in0=v3(NTq), in1=v3(offmT[si]), op=ALU.mult),
                             [kn("NTq"), "offmT"], [kn("PTq0")])
                        mm4(bA, kA, PTq[0], kn("PTq0"), Xq, kn("Xq"))
                        for hh in range(4):
                            S.op("pe", lambda e, hh=hh: e.transpose(out=pbT[0:C, hh * 128:hh * 128 + C], in_=Xq[0:C, hh, 0:C],
                                                                    identity=ident[0:C, 0:C]), [kn("Xq"), "ident"], ["pbT"], inc=(hh == 3))
                        S.op("act", lambda e: e.copy(out=v3(XTq), in_=pbT[0:C, 0:512].rearrange("p (h m) -> p h m", m=128)[:, :, 0:C]),
                             ["pbT"], [kn("XTq")])
                        yield
                        S.op("act", lambda e: e.copy(out=v3(Pq[1]), in_=b3(bA)), [kA], [kn("Pq1")])
                        mm4(bC, kC, XTq, kn("XTq"), Pq[1], kn("Pq1"))
                        yield
                        last = (si == 2)
                        Xdst = Xall[0:C, q4 * 4:q4 * 4 + 4, 0:C] if last else v3(Xq)
                        S.op("dve", lambda e, Xdst=Xdst: e.tensor_tensor(out=Xdst, in0=v3(Xq), in1=b3(bC), op=ALU.subtract),
                             [kC, kn("Xq")], ["Xall" if last else kn("Xq")])
                else:
                    S.op("dve", lambda e: e.tensor_copy(out=Xall[0:C, q4 * 4:q4 * 4 + 4, 0:C], in_=v3(Xq)), [kn("Xq")], ["Xall"])

            bg = [extra] if extra is not None else []

            def run_rr(gens, drain=False):
                gens = list(gens)
                while gens or (drain and bg):
                    for g_ in list(gens):
                        try:
                            next(g_)
                        except StopIteration:
                            gens.remove(g_)
                    for g_ in list(bg):
                        try:
                            next(g_)
                        except StopIteration:
                            bg.remove(g_)
            _gate_part(0)
            run_rr([quad_gen(0, 0), quad_gen(1, 1)])
            _gate_part(1)
            run_rr([quad_gen(2, 0), quad_gen(3, 1)], drain=True)
            _gate_part(2)

            def seq_gen(q4, s):
                hs = slice(q4 * 4, q4 * 4 + 4)
                b1, b2 = (pb[0], pb[1]) if s == 0 else (pb[2], pb[3])
                k1, k2 = ("pb0", "pb1") if s == 0 else ("pb2", "pb3")
                tq, Vp, ub = tq2[s], Vp2[s], ub2[s]
                kn = lambda n_: "%s_q%d" % (n_, s)
                for hh in range(4):
                    h = q4 * 4 + hh
                    S.op("pe", lambda e, h=h, hh=hh: e.matmul(b1[0:C, hh * 128:(hh + 1) * 128], lhsT=kT(h // 2), rhs=Sbf[:, h, :],
                                                              start=True, stop=True), rk + ["Sbf"], [k1], inc=(hh == 3))
                for hh in range(4):
                    h = q4 * 4 + hh
                    S.op("pe", lambda e, h=h, hh=hh: e.matmul(b2[0:C, hh * 128:(hh + 1) * 128], lhsT=qT(h // 2), rhs=Sbf[:, h, :],
                                                              start=True, stop=True), rk + ["Sbf"], [k2], inc=(hh == 3))
                yield
                S.op("dve", lambda e: e.tensor_tensor(out=tq[0:C], in0=bfw(b1), in1=bc4(negeg[0:C, hs], 128), op=ALU.mult),
                     [k1, "negeg"], [kn("tq")])
                S.op("dve", lambda e: e.tensor_tensor(out=Vp[0:C], in0=tq[0:C], in1=Vt[:, hs, :], op=ALU.add), [kn("tq")] + rk, [kn("Vp")])
                for hh in range(4):
                    h = q4 * 4 + hh
                    S.op("pe", lambda e, h=h, hh=hh: e.matmul(b1[0:C, hh * 128:(hh + 1) * 128], lhsT=Xall[0:C, h, 0:C], rhs=Vp[0:C, hh, :],
                                                              start=True, stop=True), ["Xall", kn("Vp")], [k1], inc=(hh == 3))
                S.op("dve", lambda e: e.tensor_tensor(out=tq[0:C], in0=bfw(b2), in1=bc4(eg[0:C, hs], 128), op=ALU.mult),
                     [k2, "eg"], [kn("tq")])
                yield
                S.op("dve", lambda e: e.tensor_tensor(out=ub[0:C], in0=bfw(b1), in1=bc4(b_ap[:, hs], 128), op=ALU.mult),
                     [k1] + rk, [kn("ub")])
                for hh in range(4):
                    h = q4 * 4 + hh
                    S.op("pe", lambda e, h=h, hh=hh: e.matmul(b2[0:C, hh * 128:(hh + 1) * 128], lhsT=QKD[0:C, h, 0:C], rhs=ub[0:C, hh, :],
                                                              start=True, stop=True), ["QKD", kn("ub")], [k2], inc=(hh == 3))
                for hh in range(4):
                    h = q4 * 4 + hh
                    S.op("pe", lambda e, h=h, hh=hh: e.matmul(b1[:, hh * 128:(hh + 1) * 128], lhsT=Kdec[0:C, h, :], rhs=ub[0:C, hh, :],
                                                              start=True, stop=True), ["Kdec", kn("ub")], [k1], inc=(hh == 3))
                yield
                S.op("dve", lambda e: e.tensor_tensor(out=O[:, hs, :], in0=bfw(b2), in1=tq[0:C], op=ALU.add), [k2, kn("tq")], [okey])
                S.op("dve", lambda e: e.tensor_tensor(out=Sst[:, hs, :], in0=Sst[:, hs, :],
                                                      in1=b1[:, :].rearrange("p (h m) -> p h m", m=128), op=ALU.add),
                     ["Sst", k1], ["Sst"])
                S.op("act", lambda e: e.copy(out=Sbf[:, hs, :], in_=Sst[:, hs, :]), ["Sst"], ["Sbf"])
            run_rr([seq_gen(0, 0), seq_gen(1, 1)])
            run_rr([seq_gen(2, 0), seq_gen(3, 1)])


        def gdn_onorm_gate(M, o_in, okey, z_in, zkey, dst, dkey):
            S.op("act", lambda e: e.activation(out=junk2[0:M, :], in_=o_in, func=AF.Square), [okey], ["junk2"])
            S.op("dve", lambda e: e.tensor_reduce(out=ssq16[0:M, :], in_=junk2[0:M, :].rearrange("p (h d) -> p h d", d=128), axis=AX.X, op=ALU.add),
                 ["junk2"], ["ssq16"])
            S.op("act", lambda e: e.activation(out=ssq16[0:M, :], in_=ssq16[0:M, :], func=AF.Sqrt, scale=1.0 / 128, bias=EPS), ["ssq16"], ["ssq16"])
            S.op("dve", lambda e: e.reciprocal(out=ssq16[0:M, :], in_=ssq16[0:M, :]), ["ssq16"], ["ssq16"])
            j3 = junk2[0:M, :].rearrange("p (h d) -> p h d", d=128)
            S.op("dve", lambda e: e.tensor_tensor(out=j3, in0=o_in.rearrange("p (h d) -> p h d", d=128),
                                                  in1=ssq16[0:M, :].unsqueeze(2).to_broadcast([M, 16, 128]), op=ALU.mult),
                 [okey, "ssq16"], ["junk2"])
            S.op("dve", lambda e: e.tensor_tensor(out=j3, in0=j3, in1=ong[0:M, :].unsqueeze(1).to_broadcast([M, 16, 128]), op=ALU.mult),
                 ["junk2", "ong"], ["junk2"])
            S.op("dve", lambda e: e.tensor_tensor(out=dst, in0=junk2[0:M, :], in1=z_in, op=ALU.mult), ["junk2", zkey], [dkey])

        def gdn_core_tile(l2, n):
            xi, xk = xin[n % 2], "xin%d" % (n % 2)
            xo, xok = xin[(n + 1) % 2], "xin%d" % ((n + 1) % 2)
            if n == 0:
                S.dma("sp", lambda e: e.dma_start(out=xi[:, :, 3:131], in_=qkv_scr[n].rearrange("p (c m) -> p c m", m=128)),
                      ["qkv_scr%d" % n], [xk])
            S.dma("sp", lambda e: e.dma_start(out=zs[:], in_=z_scr[n * 128:(n + 1) * 128, :]), ["z_scr%d" % n], ["zs"])
            S.dma("sp", lambda e: e.dma_start(out=gbt[:], in_=gb_scr[n * 128:(n + 1) * 128, :]), ["gb_scr%d" % n], ["gbt"])
            if n == 0:
                S.op("pool", lambda e: e.memset(xi[:, :, 0:3], 0.0), [xk], [xk])
            else:
                S.op("pool", lambda e: e.tensor_copy(out=xi[:, :, 0:3], in_=xo[:, :, 128:131]), [xk, xok], [xk])
            if n + 1 < NT:
                S.dma("sp", lambda e: e.dma_start(out=xo[:, :, 3:131], in_=qkv_scr[n + 1].rearrange("p (c m) -> p c m", m=128)),
                      ["qkv_scr%d" % (n + 1)], [xok])
            def conv_group(gi, ylds):
                csl = slice(gi * 8, gi * 8 + 8)
                for j, eng in [(0, "pool"), (1, "pool"), (2, "dve"), (3, "dve")]:
                    S.op(eng, lambda e, j=j, csl=csl: e.tensor_tensor(
                        out=ct[j][:], in0=xi[:, csl, j:j + 128], in1=cwT[:, csl, j:j + 1].to_broadcast([128, 8, 128]), op=ALU.mult),
                        [xk, "cwT"], ["ct%d" % j])
                    if ylds and j % 2 == 1:
                        yield
                S.op("dve", lambda e: e.tensor_tensor(out=ct[2][:], in0=ct[2][:], in1=ct[3][:], op=ALU.add), ["ct2", "ct3"], ["ct2"])
                if ylds:
                    yield
                S.op("dve", lambda e: e.tensor_tensor(out=ct[0][:], in0=ct[0][:], in1=ct[1][:], op=ALU.add), ["ct0", "ct1"], ["ct0"])
                S.op("dve", lambda e: e.tensor_tensor(out=ct[0][:], in0=ct[0][:], in1=ct[2][:], op=ALU.add), ["ct0", "ct2"], ["ct0"])
                if ylds:
                    yield
                S.op("act", lambda e, csl=csl: e.activation(out=cs[:, csl, :], in_=ct[0][:], func=AF.Silu), ["ct0"], ["cs"])

            for gi in range(2):
                for _ in conv_group(gi, False):
                    pass
            gdn_qk_norm()
            for hq in range(8):
                S.op("pe", lambda e, hq=hq: e.transpose(out=pbT[:, hq * 128:(hq + 1) * 128], in_=qkn[:, 8 + hq, :], identity=ident[:]),
                     ["qkn", "ident"], ["pbT"], inc=(hq == 7))
            S.op("act", lambda e: e.copy(out=Ktm[:], in_=pbT[:].rearrange("p (c m) -> p c m", m=128)), ["pbT"], ["Ktm"])

            def v_gen():
                for gi in (2, 3):
                    yield from conv_group(gi, True)
                    yield
                    half = gi - 2
                    for hh in range(8):
                        S.op("pe", lambda e, hh=hh, half=half: e.transpose(out=pbT[:, hh * 128:(hh + 1) * 128], in_=cs[:, 16 + half * 8 + hh, :],
                                                                           identity=ident[:]), ["cs", "ident"], ["pbT"], inc=(hh == 7))
                    S.op("act", lambda e, half=half: e.copy(out=Vtm[:, half * 8:half * 8 + 8, :], in_=pbT[:].rearrange("p (c m) -> p c m", m=128)),
                         ["pbT"], ["Vtm"])
                    yield
            gdn_chunk(128, 3,
                      lambda hq: qkn[:, 8 + hq, :], lambda hq: qkn[:, hq, :], Ktm[:], Vtm[:],
                      gbt[:, 0:16], gbt[:, 16:32], ["qkn", "Ktm", "Vtm", "gbt"],
                      o32[:], "o32", extra=v_gen())
            gdn_onorm_gate(128, o32[:].rearrange("p h d -> p (h d)"), "o32", zs[:], "zs", og[:], "og")
            S.dma("sp", lambda e: e.dma_start(out=o_scr[n * 128:(n + 1) * 128, :], in_=og[:]), ["og"], ["o_scr%d" % n])

        def sample_main(l):
            is_swa_ = (l % 2 == 0)
            l2_ = l // 2
            lp_ = l - 1
            kcp = 8 if (lp_ % 2 == 0) else 16
            if l > 0:
                S.dma("sp", lambda e: e.dma_start(out=h32[0:NS, :], in_=mods_scr[lp_, :, 2 * D:3 * D]), ["mods_scr%d" % lp_], ["h32"])
                out_proj(og_s, "og_s", NS, kcp, xs, "xs", h32, "h32")
            if l < depth:
                S.dma("sp", lambda e: e.dma_start(out=modP[0:NS, :], in_=mods_scr[l, :, 0:2 * D]), ["mods_scr%d" % l], ["modP"])
                norm_to_hT(xs, "xs", NS, modP, "modP", hT2[0], "hT0")
                if is_swa_:
                    for _ in swa_prompt_tile(l2_, NT, hT2[0], "hT0", sample=True):
                        pass
                else:
                    for _ in gdn_inproj_tile(l2_, NT, hT2[0], "hT0"):
                        pass
            else:
                final_norm(xs, "xs", NS, y_s, "y_s", normg)

        def gdn_qk_norm():
            for qd in range(4):
                hs = slice(qd * 4, qd * 4 + 4)
                S.op("act", lambda e, hs=hs: e.activation(out=sq[:], in_=cs[:, hs, :], func=AF.Square), ["cs"], ["sq"])
                S.op("pe", lambda e: e.matmul(pb[0][:, :], lhsT=ones_b[:], rhs=sq[:].rearrange("p h m -> p (h m)"), start=True, stop=True),
                     ["sq", "ones_b"], ["pb0"])
                S.op("dve", lambda e: e.tensor_copy(out=rn[:].rearrange("p h m -> p (h m)"), in_=pb[0][:, :]), ["pb0"], ["rn"])
                S.op("act", lambda e: e.activation(out=rn[:], in_=rn[:], func=AF.Sqrt, bias=EPS), ["rn"], ["rn"])
                S.op("dve", lambda e: e.reciprocal(out=rn[:], in_=rn[:]), ["rn"], ["rn"])
                sc = (128.0 ** -0.5) if qd < 2 else 1.0
                S.op("dve", lambda e, hs=hs, sc=sc: e.scalar_tensor_tensor(out=qkn[:, hs, :], in0=cs[:, hs, :], scalar=sc, in1=rn[:],
                                                                          op0=ALU.mult, op1=ALU.mult), ["cs", "rn"], ["qkn"])

        def sample_gdn_core(l2):
            xi, xk = xin[0], "xin0"
            xv = xi[:, :, 0:112].rearrange("p c (b t) -> p c b t", t=7)
            S.dma("sp", lambda e: e.dma_start(out=xin[1][:, :, 0:128], in_=qkv_scr[NT].rearrange("p (c m) -> p c m", m=128)),
                  ["qkv_scr%d" % NT], ["xin1"])
            for gi in range(4):
                csl = slice(gi * 8, gi * 8 + 8)
                S.op("pool", lambda e, csl=csl: e.tensor_copy(out=xv[:, csl, :, 3:7],
                                                              in_=xin[1][:, csl, 0:64].rearrange("p c (b t) -> p c b t", t=4)),
                     ["xin1", xk], [xk])
                S.dma("pool", lambda e, gi=gi: e.dma_start(out=hb[0:48, :], in_=conv_state[l2][:, :, gi * 1024:(gi + 1) * 1024].rearrange("b t c -> (b t) c")),
                      [], ["hb"])
                for c in range(8):
                    S.op("pe", lambda e, c=c: e.transpose(out=pbT[:, c * 128:c * 128 + 48], in_=hb[0:48, c * 128:(c + 1) * 128], identity=ident[0:48, 0:48]),
                         ["hb", "ident"], ["pbT"])
                S.op("act", lambda e, csl=csl: e.copy(out=xv[:, csl, :, 0:3],
                                                      in_=pbT[:].rearrange("p (c m) -> p c m", m=128)[:, :, 0:48].rearrange("p c (b t) -> p c b t", t=3)),
                     ["pbT", xk], [xk])
                for j, eng in [(0, "pool"), (1, "pool"), (2, "dve"), (3, "dve")]:
                    S.op(eng, lambda e, j=j, csl=csl: e.tensor_tensor(
                        out=ct[j][:, :, 0:64].rearrange("p c (b t) -> p c b t", t=4), in0=xv[:, csl, :, j:j + 4],
                        in1=cwT[:, csl, j:j + 1].unsqueeze(3).to_broadcast([128, 8, 16, 4]), op=ALU.mult), [xk, "cwT"], ["ct%d" % j])
                S.op("dve", lambda e: e.tensor_tensor(out=ct[2][:], in0=ct[2][:], in1=ct[3][:], op=ALU.add), ["ct2", "ct3"], ["ct2"])
                S.op("dve", lambda e: e.tensor_tensor(out=ct[0][:], in0=ct[0][:], in1=ct[1][:], op=ALU.add), ["ct0", "ct1"], ["ct0"])
                S.op("dve", lambda e: e.tensor_tensor(out=ct[0][:], in0=ct[0][:], in1=ct[2][:], op=ALU.add), ["ct0", "ct2"], ["ct0"])
                S.op("act", lambda e, csl=csl: e.activation(out=cs[:, csl, :], in_=ct[0][:], func=AF.Silu), ["ct0"], ["cs"])
            gdn_qk_norm()
            for b in range(NSB):
                S.dma("sp", lambda e, b=b: e.dma_start(out=Sst[:], in_=s_state[l2, b].rearrange("h k v -> k h v")), [], ["Sst"])
                S.op("pool", lambda e: e.tensor_copy(out=Sbf[:], in_=Sst[:]), ["Sst"], ["Sbf"])
                S.dma("sp", lambda e, b=b: e.dma_start(out=gbt[0:4, :], in_=gb_scr[NT * 128 + 4 * b:NT * 128 + 4 * b + 4, :]),
                      ["gb_scr%d" % NT], ["gbt"])
                for hq in range(8):
                    S.op("pe", lambda e, hq=hq, b=b: e.transpose(out=pbT[0:4, hq * 128:(hq + 1) * 128], in_=qkn[:, 8 + hq, 4 * b:4 * b + 4], identity=ident[:]),
                         ["qkn", "ident"], ["pbT"])
                S.op("act", lambda e: e.copy(out=Ktm[0:4, :, :], in_=pbT[0:4, :].rearrange("p (c m) -> p c m", m=128)), ["pbT"], ["Ktm"])
                for half in range(2):
                    for hh in range(8):
                        S.op("pe", lambda e, hh=hh, half=half, b=b: e.transpose(out=pbT[0:4, hh * 128:(hh + 1) * 128],
                                                                               in_=cs[:, 16 + half * 8 + hh, 4 * b:4 * b + 4], identity=ident[:]),
                             ["cs", "ident"], ["pbT"])
                    S.op("act", lambda e, half=half: e.copy(out=Vtm[0:4, half * 8:half * 8 + 8, :], in_=pbT[0:4, :].rearrange("p (c m) -> p c m", m=128)),
                         ["pbT"], ["Vtm"])
                gdn_chunk(4, 1,
                          lambda hq, b=b: qkn[:, 8 + hq, 4 * b:4 * b + 4], lambda hq, b=b: qkn[:, hq, 4 * b:4 * b + 4],
                          Ktm[0:4], Vtm[0:4],
                          gbt[0:4, 0:16], gbt[0:4, 16:32], ["qkn", "Ktm", "Vtm", "gbt"],
                          o32[0:4], "o32")
                S.dma("sp", lambda e, b=b: e.dma_start(out=ss_out[l2, b].rearrange("h k v -> k h v"), in_=Sst[:]), ["Sst"], ["ss_out"])
                S.dma("sp", lambda e, b=b: e.dma_start(out=osall[4 * b:4 * b + 4, :], in_=o32[0:4, :, :].rearrange("p h d -> p (h d)")),
                      ["o32"], ["osall"])
            S.dma("sp", lambda e: e.dma_start(out=zs[0:NS, :], in_=z_scr[NT * 128:NT * 128 + NS, :]), ["z_scr%d" % NT], ["zs"])
            gdn_onorm_gate(NS, osall[0:NS, :], "osall", zs[0:NS, :], "zs", og_s[0:NS, :], "og_s")

        def sbs(stk, name, shape, dt):
            return stk.enter_context(nc.sbuf_tensor(name, list(shape), dt))

        S.dma("sp", lambda e: e.dma_start(out=xs[:], in_=xs_in), [], ["xs"])
        for l in range(depth + 1):
            is_swa = (l % 2 == 0)
            l2 = l // 2
            lp = l - 1
            kc_prev = 8 if (lp % 2 == 0) else 16
            tg = "_p%d" % l
            with contextlib.ExitStack() as stk:
                if l > 0:
                    w_out_sb = sbs(stk, "w_out_sb" + tg, [128, kc_prev, D], BF16)
                    ogl = [sbs(stk, "ogl%d" % i + tg, [128, kc_prev * 128], BF16) for i in range(2)]
                if l < depth:
                    w_in_sb = sbs(stk, "w_in_sb" + tg, [128, 8, SWA_W if is_swa else GDN_W], BF16)
                    if is_swa:
                        qkT = [sbs(stk, "qkT%d" % i + tg, [128, 10, 128], BF16) for i in range(2)]
                        vext = [sbs(stk, "vext%d" % i + tg, [128, 4, 65], BF16) for i in range(2)]
                        e_own2 = [sbs(stk, "e_own%d" % i + tg, [128, 4, 128], BF16) for i in range(2)]
                        e_prev2 = [sbs(stk, "e_prev%d" % i + tg, [128, 4, 128], BF16) for i in range(2)]
                        on32 = sbs(stk, "on32" + tg, [128, D], F32)
                        og = sbs(stk, "og" + tg, [128, D], BF16)
                        for i in range(2):
                            S.op("pool", lambda e, i=i: e.memset(vext[i][:], 1.0), [], ["vext%d" % i])
                    else:
                        qst = sbs(stk, "qst" + tg, [128, 32, 128], BF16)
                if l > 0:
                    if lp % 2 == 0:
                        load_w(w_out_sb, "w_out_sb", swa_w_out[lp // 2], 8, D)
                    else:
                        load_w(w_out_sb, "w_out_sb", gdn_w_out[lp // 2], 16, D)
                if l < depth:
                    if is_swa:
                        load_w(w_in_sb, "w_in_sb", swa_w_in[l2], 8, SWA_W)
                        swa_layer_setup(l2)
                    else:
                        load_w(w_in_sb, "w_in_sb", gdn_w_in[l2], 8, GDN_W)
                        gdn_layer_setup(l2)
                    modulation(l)
                else:
                    S.dma("sp", lambda e: e.dma_start(out=normg[:], in_=final_norm_g.partition_broadcast(128)), [], ["normg"])
                gprev, gpk = gateP[lp % 2], "gateP%d" % (lp % 2)
                src_x, skey_fn = (xp, lambda n: "xp") if l <= 1 else (x_scr, lambda n: "x_scr%d" % n)
                def pre_gen(n):
                    if n + 1 < NT:
                        load_x(n + 1, src_x, skey_fn(n + 1))
                        if l > 0:
                            load_o(n + 1, kc_prev * 128)
                    xc, xck = xt[n % 2], "xt%d" % (n % 2)
                    if l > 0:
                        out_proj(ogl[n % 2], "ogl%d" % (n % 2), 128, kc_prev, xc, xck, gprev, gpk)
                        yield
                        if l < depth:
                            S.dma("sp", lambda e, n=n, xc=xc: e.dma_start(out=x_scr[n * 128:(n + 1) * 128, :], in_=xc[:]),
                                  [xck], ["x_scr%d" % n])
                    if l < depth:
                        norm_to_hT(xc, xck, 128, modP, "modP", hT2[n % 2], "hT%d" % (n % 2))
                    else:
                        final_norm(xc, xck, 128, y_p[n * 128:(n + 1) * 128, :], "y_p", normg)
                    yield

                def run_bg(main, bgs):
                    bgs = list(bgs)
                    for _ in main:
                        for g_ in list(bgs):
                            try:
                                next(g_)
                            except StopIteration:
                                bgs.remove(g_)
                    for g_ in bgs:
                        for _ in g_:
                            pass

                load_x(0, src_x, skey_fn(0))
                if l > 0:
                    load_o(0, kc_prev * 128)
                for _ in pre_gen(0):
                    pass
                for n in range(NT):
                    bgs = [pre_gen(n + 1)] if n + 1 < NT else []
                    if l < depth:
                        if is_swa:
                            main = swa_prompt_tile(l2, n, hT2[n % 2], "hT%d" % (n % 2))
                        else:
                            main = gdn_inproj_tile(l2, n, hT2[n % 2], "hT%d" % (n % 2))
                    else:
                        main = iter(())
                    run_bg(main, bgs)
                if do_sample:
                    sample_main(l)
                S.barrier()
                S.flush()
            if l < depth and not is_swa:
                tg = "_c%d" % l
                with contextlib.ExitStack() as stk:
                    xin = [sbs(stk, "xin%d" % i + tg, [128, 32, 131], BF16) for i in range(2)]
                    cwT = sbs(stk, "cwT" + tg, [128, 32, 4], F32)
                    ctall = sbs(stk, "ctall" + tg, [128, 4, 8, 128], F32)
                    ct = [ctall[:, i] for i in range(4)]
                    osall = ctall[:, 0:2].rearrange("p a c m -> p (a c m)")
                    junk2 = ctall[:, 2:4].rearrange("p a c m -> p (a c m)")
                    rn = ctall[:, 0, 0:4, :]
                    S.groups["osall"] = ["ct0", "ct1"]
                    S.groups["junk2"] = ["ct2", "ct3"]
                    S.groups["rn"] = ["ct0"]
                    cs = sbs(stk, "cs" + tg, [128, 32, 128], BF16)
                    sq = sbs(stk, "sq" + tg, [128, 4, 128], BF16)
                    qkn = sbs(stk, "qkn" + tg, [128, 16, 128], BF16)
                    Ktm = sbs(stk, "Ktm" + tg, [128, 8, 128], BF16)
                    Vtm = sbs(stk, "Vtm" + tg, [128, 16, 128], BF16)
                    ong = sbs(stk, "ong" + tg, [128, 128], F32)
                    gam = sbs(stk, "gam" + tg, [128, 16], F32)
                    gtl = sbs(stk, "gtl" + tg, [128, 16], F32)
                    gtot = sbs(stk, "gtot" + tg, [128, 16], F32)
                    eg = sbs(stk, "eg" + tg, [128, 16], F32)
                    negeg = sbs(stk, "negeg" + tg, [128, 16], F32)
                    kdf = sbs(stk, "kdf" + tg, [128, 16], F32)
                    Gm = sbs(stk, "Gm" + tg, [128, 4, 128], F32)
                    QKD = sbs(stk, "QKD" + tg, [128, 16, 128], BF16)
                    Eq2 = [sbs(stk, "Eq%d" % i + tg, [128, 4, 128], BF16) for i in range(2)]
                    Eb2 = [sbs(stk, "Eb%d" % i + tg, [128, 4, 128], BF16) for i in range(2)]
                    Nq2 = [sbs(stk, "Nq%d" % i + tg, [128, 4, 128], BF16) for i in range(2)]
                    NTq2 = [sbs(stk, "NTq%d" % i + tg, [128, 4, 128], BF16) for i in range(2)]
                    Xq2 = [sbs(stk, "Xq%d" % i + tg, [128, 4, 128], BF16) for i in range(2)]
                    XTq2 = [sbs(stk, "XTq%d" % i + tg, [128, 4, 128], BF16) for i in range(2)]
                    Pq2 = [[sbs(stk, "Pq%d_%d" % (i, j) + tg, [128, 4, 128], BF16) for j in range(2)] for i in range(2)]
                    PTq2 = [[sbs(stk, "PTq%d_%d" % (i, j) + tg, [128, 4, 128], BF16) for j in range(2)] for i in range(2)]
                    tq2 = [sbs(stk, "tq%d" % i + tg, [128, 4, 128], F32) for i in range(2)]
                    Vp2 = [sbs(stk, "Vp%d" % i + tg, [128, 4, 128], BF16) for i in range(2)]
                    ub2 = [sbs(stk, "ub%d" % i + tg, [128, 4, 128], BF16) for i in range(2)]
                    bd16 = sbs(stk, "bd16" + tg, [128, 4, 128], BF16)
                    offmT = [sbs(stk, "offmT%d" % i + tg, [128, 4, 128], BF16) for i in range(3)]
                    S.dma("sp", lambda e: e.dma_start(out=bd16[:], in_=c_bd16), [], ["bd16"])
                    for i in range(3):
                        S.dma("sp", lambda e, i=i: e.dma_start(out=offmT[i][:], in_=c_offT[i]), [], ["offmT"])
                    Xall = sbs(stk, "Xall" + tg, [128, 16, 128], BF16)
                    Kdec = sbs(stk, "Kdec" + tg, [128, 16, 128], BF16)
                    Sst = sbs(stk, "Sst" + tg, [128, 16, 128], F32)
                    Sbf = sbs(stk, "Sbf" + tg, [128, 16, 128], BF16)
                    o32 = sbs(stk, "o32" + tg, [128, 16, 128], F32)
                    ssq16 = sbs(stk, "ssq16" + tg, [128, 16], F32)
                    og = sbs(stk, "og" + tg, [128, 2 * D], BF16)
                    gdn_core_setup(l2)
                    S.op("pool", lambda e: e.memset(Sst[:], 0.0), [], ["Sst"])
                    S.op("pool", lambda e: e.memset(Sbf[:], 0.0), [], ["Sbf"])
                    for n in range(NT):
                        gdn_core_tile(l2, n)
                    S.dma("sp", lambda e: e.dma_start(out=sp_out[l2].rearrange("h k v -> k h v"), in_=Sst[:]), ["Sst"], ["sp_out"])
                    if do_sample:
                        sample_gdn_core(l2)
                    S.barrier()
                    S.flush()

        S.finish()
        S.flush()
    return nc


def _consts():
    bf = ml_dtypes.bfloat16
    i = np.arange(128)
    r = {}
    r["c_ident"] = np.eye(128).astype(bf)
    s, q = i[:, None], i[None, :]
    r["c_mown"] = np.repeat((q >= s)[:, None, :], 4, axis=1).astype(bf)
    r["c_mprev"] = np.repeat((s >= q)[:, None, :], 4, axis=1).astype(bf)
    r["c_lmask"] = (s > q).astype(np.float32)
    r["c_uinc"] = (s <= q).astype(np.float32)
    r["c_mincl"] = np.repeat((q >= s)[:, None, :], 4, axis=1).astype(bf)
    r["c_mstrict"] = np.repeat((q > s)[:, None, :], 4, axis=1).astype(bf)
    r["c_ident4"] = np.repeat((q == s)[:, None, :], 4, axis=1).astype(bf)
    r["c_bd16"] = np.repeat(((q // 16) == (s // 16))[:, None, :], 4, axis=1).astype(bf)
    offs = []
    for sz in (16, 32, 64):
        m = ((s // (2 * sz)) == (q // (2 * sz))) & ((s // sz) % 2 == 0) & ((q // sz) % 2 == 1)
        offs.append(m)
    r["c_off"] = np.stack([np.repeat(m[:, None, :], 4, axis=1) for m in offs]).astype(bf)
    r["c_offT"] = np.stack([np.repeat(m.T[:, None, :], 4, axis=1) for m in offs]).astype(bf)
    return {k: np.ascontiguousarray(v) for k, v in r.items()}


def _swa_cols():
    cols = []
    for jp in range(2):
        for g in range(4):
            a = 4 * (2 * jp) + g
            b = 4 * (2 * jp + 1) + g
            cols += list(range(a * 64, (a + 1) * 64)) + list(range(b * 64, (b + 1) * 64))
    cols += list(range(1024, 1280))
    cols += list(range(1024, 1536))
    cols += list(range(1536, 2560))
    return np.array(cols)


def prep_in_maps(inp, T, cores):
    f = np.float32
    shared = dict(_consts())
    shared["norm_g"] = np.ascontiguousarray(inp["norm_g"], f)
    shared["w_mod"] = np.ascontiguousarray(inp["w_mod"], f)
    shared["b_mod"] = np.ascontiguousarray(inp["b_mod"], f)
    shared["swa_w_in"] = np.ascontiguousarray(np.asarray(inp["swa_w_in"], f)[:, :, _swa_cols()])
    shared["swa_sinks"] = np.ascontiguousarray(inp["swa_sinks"], f)
    shared["swa_w_out"] = np.ascontiguousarray(inp["swa_w_out"], f)
    shared["gdn_w_in"] = np.ascontiguousarray(inp["gdn_w_in"], f)
    cw = np.asarray(inp["gdn_conv_w"], f)
    shared["gdn_conv_w"] = np.ascontiguousarray(cw)
    shared["gdn_conv_wT"] = np.ascontiguousarray(cw.reshape(2, 4, 32, 128).transpose(0, 3, 2, 1))
    shared["gdn_a_log"] = np.ascontiguousarray(inp["gdn_a_log"], f)
    shared["gdn_dt_bias"] = np.ascontiguousarray(inp["gdn_dt_bias"], f)
    shared["gdn_o_norm_g"] = np.ascontiguousarray(inp["gdn_o_norm_g"], f)
    shared["gdn_w_out"] = np.ascontiguousarray(inp["gdn_w_out"], f)
    shared["final_norm_g"] = np.ascontiguousarray(inp["final_norm_g"], f)
    maps = []
    for c in cores:
        b = c // 4
        sl = slice(NSB * c, NSB * (c + 1))
        m = dict(shared)
        m["xp"] = np.ascontiguousarray(np.asarray(inp["x_prompt"])[b, :T], f)
        m["xs"] = np.ascontiguousarray(np.asarray(inp["x_sample"])[sl].reshape(NS, D), f)
        m["ctok"] = np.ascontiguousarray(np.concatenate([np.asarray(inp["c_sample"])[sl], np.asarray(inp["c_prompt"])[b:b + 1]], 0), f)
        m["cache_k"] = np.ascontiguousarray(np.asarray(inp["cache_swa_k"])[:, sl].reshape(2, NSB, 128, 256), f)
        m["cache_v"] = np.ascontiguousarray(np.asarray(inp["cache_swa_v"])[:, sl].reshape(2, NSB, 128, 256), f)
        m["conv_state"] = np.ascontiguousarray(np.asarray(inp["state_gdn_conv"])[:, sl], f)
        m["s_state"] = np.ascontiguousarray(np.asarray(inp["state_gdn_s"])[:, sl], f)
        maps.append(m)
    return maps


_NC_CACHE = {}


def kernel(**inputs):
    T = 8192
    key = (T, 4)
    if key not in _NC_CACHE:
        _NC_CACHE[key] = build(T, 4, True)
    nc = _NC_CACHE[key]
    cores = list(range(8))
    maps = prep_in_maps(inputs, T, cores)
    res = run_bass_kernel_spmd(nc, maps, core_ids=cores)
    R = res.results
    f = np.float32
    y_prompt = np.stack([R[0]["y_p"], R[4]["y_p"]]).astype(f)
    y_sample = np.concatenate([R[c]["y_s"].reshape(NSB, 4, D) for c in range(8)], 0).astype(f)
    kp = np.stack([R[0]["kp_out"], R[4]["kp_out"]], 1).reshape(2, 2, 128, 4, 64).astype(f)
    vp = np.stack([R[0]["vp_out"], R[4]["vp_out"]], 1).reshape(2, 2, 128, 4, 64).astype(f)
    ks = np.concatenate([R[c]["ks_out"] for c in range(8)], 1).reshape(2, 128, 128, 4, 64).astype(f)
    vs = np.concatenate([R[c]["vs_out"] for c in range(8)], 1).reshape(2, 128, 128, 4, 64).astype(f)
    convp = np.stack([R[0]["convp_out"], R[4]["convp_out"]], 1).astype(f)
    sp = np.stack([R[0]["sp_out"], R[4]["sp_out"]], 1).astype(f)
    convs = np.concatenate([R[c]["convs_out"] for c in range(8)], 1).astype(f)
    ssn = np.concatenate([R[c]["ss_out"] for c in range(8)], 1).astype(f)
    return (y_prompt, y_sample, kp, vp, ks, vs, convp, sp, convs, ssn)
```
